# Optimizing a Trainium2 kernel written in Bass

```python
import math
import jax, jax.numpy as jnp
from jax import lax
import numpy as np

D_MODEL = 2048
BATCH = 4
SEQ = 8192
DEPTH = 1

CTX_LEN = 256
GRID_W = 64
EPS = 1e-6

D_FF = 5632

A_WIDTH = D_MODEL
A_GROUPS = 8
A_CHUNK = 128

B_INNER = 2 * D_MODEL
B_HEADDIM = 64
B_HEADS = B_INNER // B_HEADDIM
B_GROUPS = 8
B_STATE = 128
B_CONV = 5
SSD_CHUNK = 128

XB_W = B_INNER + B_GROUPS * B_STATE
XBC_W = XB_W + B_GROUPS * B_STATE
OFF_U = 0
OFF_V = OFF_U + A_WIDTH
OFF_Z = OFF_V + A_WIDTH
OFF_XB = OFF_Z + B_INNER
OFF_C = OFF_XB + XB_W
OFF_DT = OFF_C + B_GROUPS * B_STATE
OFF_GATE = OFF_DT + 2 * B_HEADS
IN_W = OFF_GATE + 2 * D_MODEL

kernel_name = 'hybrid_gmlp_ssd_prefix_block'


def rmsnorm(x, w):
    xf = x.astype(jnp.float32)
    y = xf * lax.rsqrt(jnp.mean(xf * xf, axis=-1, keepdims=True) + EPS)
    return (y * w.astype(jnp.float32)).astype(x.dtype)


def layernorm(x, w, b):
    xf = x.astype(jnp.float32)
    mu = jnp.mean(xf, axis=-1, keepdims=True)
    var = jnp.mean(jnp.square(xf - mu), axis=-1, keepdims=True)
    return ((xf - mu) * lax.rsqrt(var + EPS) * w.astype(jnp.float32) + b.astype(jnp.float32)).astype(x.dtype)


def modulate(h, shift, scale):
    return h * (1 + scale) + shift


def adaln(cvec, w_mod, b_mod):
    m = jax.nn.silu(cvec) @ w_mod + b_mod
    if m.ndim == 2:
        m = m[:, None, :]
    return jnp.split(m, 9, axis=-1)


def swiglu_half_step(x, mod3, g, w_gate, w_up, w_down):
    shift, scale, gate = mod3
    h = modulate(rmsnorm(x, g), shift, scale)
    return x + 0.5 * gate * ((jax.nn.silu(h @ w_gate) * (h @ w_up)) @ w_down)


def dwconv_centred(u, w, b, n_seg):
    bsz, length, ch = u.shape
    seg = length // n_seg
    pad = (w.shape[0] - 1) // 2
    y = lax.conv_general_dilated(u.reshape(bsz * n_seg, seg, ch), w[:, None, :].astype(u.dtype),
                                 window_strides=(1,), padding=[(pad, pad)],
                                 dimension_numbers=('NWC', 'WIO', 'NWC'), feature_group_count=ch)
    return y.reshape(bsz, length, ch) + b


def chunk_gmlp(u, v, ln_w, ln_b, w_s, b_s):
    bsz, length, width = u.shape
    nc = length // A_CHUNK
    vn = layernorm(v, ln_w, ln_b).reshape(bsz, nc, A_CHUNK, A_GROUPS, width // A_GROUPS)
    s = jnp.einsum('gij,bcjgd->bcigd', w_s, vn) + b_s.T[:, :, None]
    return u * s.reshape(bsz, length, width)


def segsum_exp(a_cs):
    q = a_cs.shape[-1]
    causal = jnp.tril(jnp.ones((q, q), dtype=bool))
    diff = a_cs[..., :, None] - a_cs[..., None, :]
    return jnp.exp(jnp.where(causal, diff, -jnp.inf))


def ssd_single(xh, dt, bm, cm, h0, a_h):
    length, nh, hp = xh.shape
    ng, ns = bm.shape[1], bm.shape[2]
    hpg = nh // ng
    q = SSD_CHUNK
    nc = length // q
    a = (dt * a_h).astype(jnp.float32).reshape(nc, q, ng, hpg).transpose(0, 2, 3, 1)
    a_cs = jnp.cumsum(a, axis=-1)
    xc = (xh * dt[..., None]).reshape(nc, q, ng, hpg, hp)
    bc = bm.reshape(nc, q, ng, ns)
    cc = cm.reshape(nc, q, ng, ns)
    cb = jnp.einsum('cign,cjgn->cgij', cc, bc)
    y_diag = jnp.einsum('cgij,cghij,cjghp->cighp', cb, segsum_exp(a_cs), xc)
    decay_to_end = jnp.exp(a_cs[..., -1:] - a_cs)
    states = jnp.einsum('cjgn,cghj,cjghp->cghpn', bc, decay_to_end, xc).astype(jnp.float32)
    chunk_decay = jnp.exp(a_cs[..., -1])

    def step(h, inp):
        dec, st = inp
        return dec[..., None, None] * h + st, h

    _, h_enter = lax.scan(step, h0.astype(jnp.float32).reshape(ng, hpg, hp, ns), (chunk_decay, states))
    y_off = jnp.einsum('cign,cghi,cghpn->cighp', cc, jnp.exp(a_cs), h_enter)
    return (y_diag + y_off).reshape(length, nh, hp).astype(xh.dtype)


def ssd_final_state(xh, dt, a_h, bm):
    bsz, length, nh, hp = xh.shape
    a_cs = jnp.cumsum((dt * a_h).astype(jnp.float32), axis=1)
    decay = jnp.exp(a_cs[:, -1:] - a_cs)
    xw = (xh * (dt * decay)[..., None]).reshape(bsz, length, B_GROUPS, nh // B_GROUPS, hp)
    h = jnp.einsum('blghp,blgn->bghpn', xw, bm)
    return h.reshape(bsz, nh, hp, B_STATE)


def flip_seq(t):
    return jnp.flip(t, axis=1)


def bidir_ssd(xh, bm, cm, dt_f, dt_b, a_neg, h0_f, h0_b):
    def run(xs, dts, a_h, bs, cs, h0):
        return lax.map(lambda t: ssd_single(t[0], t[1], t[2], t[3], t[4], a_h), (xs, dts, bs, cs, h0))
    y_f = run(xh, dt_f, a_neg[0], bm, cm, h0_f)
    y_b = flip_seq(run(flip_seq(xh), flip_seq(dt_b), a_neg[1], flip_seq(bm), flip_seq(cm), h0_b))
    return y_f, y_b


def ssm_xb(p_xb, p_dt, conv_w, conv_b, dt_bias, n_seg):
    xb = jax.nn.silu(dwconv_centred(p_xb, conv_w[:, :XB_W], conv_b[:XB_W], n_seg))
    bsz, length, _ = xb.shape
    xh = xb[..., :B_INNER].reshape(bsz, length, B_HEADS, B_HEADDIM)
    bm = xb[..., B_INNER:].reshape(bsz, length, B_GROUPS, B_STATE)
    dt = jax.nn.softplus(p_dt.reshape(bsz, length, 2, B_HEADS) + dt_bias)
    return xh, bm, dt[:, :, 0], dt[:, :, 1]


def ssm_c(p_c, conv_w, conv_b, n_seg):
    cm = jax.nn.silu(dwconv_centred(p_c, conv_w[:, XB_W:], conv_b[XB_W:], n_seg))
    bsz, length, _ = cm.shape
    return cm.reshape(bsz, length, B_GROUPS, B_STATE)


def context_states(h_c, w_in, conv_w, conv_b, dt_bias, a_neg):
    xh, bm, dt_f, dt_b = ssm_xb(h_c @ w_in[:, OFF_XB:OFF_C], h_c @ w_in[:, OFF_DT:OFF_GATE],
                                conv_w, conv_b, dt_bias, 1)
    h_f = ssd_final_state(xh, dt_f, a_neg[0], bm)
    h_b = ssd_final_state(flip_seq(xh), flip_seq(dt_b), a_neg[1], flip_seq(bm))
    return h_f, h_b


def token_mixer(h, n_seg, h0_f, h0_b, w_in, b_gate, gmlp_ln_w, gmlp_ln_b, gmlp_ws, gmlp_bs, w_a,
                conv_w, conv_b, a_neg, dt_bias, d_skip, ssm_norm, w_b, w_out):
    proj = h @ w_in
    u = jax.nn.gelu(proj[..., OFF_U:OFF_V])
    v = jax.nn.gelu(proj[..., OFF_V:OFF_Z])
    y_a = chunk_gmlp(u, v, gmlp_ln_w, gmlp_ln_b, gmlp_ws, gmlp_bs) @ w_a
    z = proj[..., OFF_Z:OFF_XB]
    xh, bm, dt_f, dt_b = ssm_xb(proj[..., OFF_XB:OFF_C], proj[..., OFF_DT:OFF_GATE], conv_w, conv_b, dt_bias, n_seg)
    cm = ssm_c(proj[..., OFF_C:OFF_DT], conv_w, conv_b, n_seg)
    y_f, y_b = bidir_ssd(xh, bm, cm, dt_f, dt_b, a_neg, h0_f, h0_b)
    bsz, length = z.shape[:2]
    y = (y_f + y_b + d_skip[:, None] * xh).reshape(bsz, length, B_INNER) * jax.nn.silu(z)
    y = rmsnorm(y.reshape(bsz, length, B_GROUPS, B_INNER // B_GROUPS), ssm_norm.reshape(B_GROUPS, -1))
    y_b_branch = y.reshape(bsz, length, B_INNER) @ w_b
    g = jax.nn.sigmoid(proj[..., OFF_GATE:] + b_gate)
    return (g[..., :D_MODEL] * y_a + g[..., D_MODEL:] * y_b_branch) @ w_out


def setup_inputs(seed: int = 0) -> dict:
    key = jax.random.key(seed)
    ks = iter(jax.random.split(key, 40))
    D = D_MODEL
    L = DEPTH

    def nrm(shape, scale):
        return scale * jax.random.normal(next(ks), shape, jnp.float32)

    def gain(shape):
        return 1.0 + nrm(shape, 0.02)

    dt0 = jnp.exp(jax.random.uniform(next(ks), (L, 2, B_HEADS), jnp.float32,
                                     minval=math.log(1e-3), maxval=math.log(1e-1)))
    return {
        'x': nrm((BATCH, SEQ, D), 1.0),
        'c': nrm((BATCH, D), 1.0),
        'ctx': nrm((BATCH, CTX_LEN, D), 1.0),
        'c_ctx': nrm((D,), 1.0),
        'w_mod': nrm((L, D, 9 * D), 0.5 * D ** -0.5),
        'b_mod': nrm((L, 9 * D), 0.02),
        'norm_ffn1': gain((L, D)),
        'ffn1_gate': nrm((L, D, D_FF), D ** -0.5),
        'ffn1_up': nrm((L, D, D_FF), D ** -0.5),
        'ffn1_down': nrm((L, D_FF, D), D_FF ** -0.5),
        'norm_mix': gain((L, D)),
        'w_in': nrm((L, D, IN_W), D ** -0.5),
        'b_gate': nrm((L, 2 * D), 0.02),
        'gmlp_ln_w': gain((L, A_WIDTH)),
        'gmlp_ln_b': nrm((L, A_WIDTH), 0.02),
        'gmlp_ws': nrm((L, A_GROUPS, A_CHUNK, A_CHUNK), A_CHUNK ** -0.5),
        'gmlp_bs': gain((L, A_GROUPS, A_CHUNK)),
        'w_a': nrm((L, A_WIDTH, D), A_WIDTH ** -0.5),
        'conv_w': nrm((L, B_CONV, XBC_W), B_CONV ** -0.5),
        'conv_b': nrm((L, XBC_W), 0.02),
        'a_log': jnp.log(jax.random.uniform(next(ks), (L, 2, B_HEADS), jnp.float32, minval=1.0, maxval=16.0)),
        'dt_bias': dt0 + jnp.log(-jnp.expm1(-dt0)),
        'd_skip': gain((L, B_HEADS)),
        'ssm_norm': gain((L, B_INNER)),
        'w_b': nrm((L, B_INNER, D), B_INNER ** -0.5),
        'w_out': nrm((L, D, D), D ** -0.5),
        'norm_ffn2': gain((L, D)),
        'ffn2_gate': nrm((L, D, D_FF), D ** -0.5),
        'ffn2_up': nrm((L, D, D_FF), D ** -0.5),
        'ffn2_down': nrm((L, D_FF, D), D_FF ** -0.5),
        'norm_final': gain((D,)),
    }


def reference(x, c, ctx, c_ctx, w_mod, b_mod, norm_ffn1, ffn1_gate, ffn1_up, ffn1_down, norm_mix, w_in, b_gate,
              gmlp_ln_w, gmlp_ln_b, gmlp_ws, gmlp_bs, w_a, conv_w, conv_b, a_log, dt_bias, d_skip, ssm_norm,
              w_b, w_out, norm_ffn2, ffn2_gate, ffn2_up, ffn2_down, norm_final):
    rows = x.shape[1] // GRID_W
    for i in range(DEPTH):
        m = adaln(c, w_mod[i], b_mod[i])
        mc = adaln(c_ctx, w_mod[i], b_mod[i])
        a_neg = -jnp.exp(a_log[i].astype(jnp.float32))
        mix_params = (w_in[i], b_gate[i], gmlp_ln_w[i], gmlp_ln_b[i], gmlp_ws[i], gmlp_bs[i], w_a[i],
                      conv_w[i], conv_b[i], a_neg, dt_bias[i], d_skip[i], ssm_norm[i], w_b[i], w_out[i])
        x = swiglu_half_step(x, m[0:3], norm_ffn1[i], ffn1_gate[i], ffn1_up[i], ffn1_down[i])
        ctx = swiglu_half_step(ctx, mc[0:3], norm_ffn1[i], ffn1_gate[i], ffn1_up[i], ffn1_down[i])
        h_c = modulate(rmsnorm(ctx, norm_mix[i]), mc[3], mc[4])
        h0_f, h0_b = context_states(h_c, w_in[i], conv_w[i], conv_b[i], dt_bias[i], a_neg)
        h = modulate(rmsnorm(x, norm_mix[i]), m[3], m[4])
        x = x + m[5] * token_mixer(h, rows, h0_f, h0_b, *mix_params)
        x = swiglu_half_step(x, m[6:9], norm_ffn2[i], ffn2_gate[i], ffn2_up[i], ffn2_down[i])
        if i < DEPTH - 1:
            zeros = jnp.zeros_like(h0_f)
            ctx = ctx + mc[5] * token_mixer(h_c, 1, zeros, zeros, *mix_params)
            ctx = swiglu_half_step(ctx, mc[6:9], norm_ffn2[i], ffn2_gate[i], ffn2_up[i], ffn2_down[i])
    return rmsnorm(x, norm_final)
```

```python
import numpy as np
from contextlib import ExitStack
import concourse.bass as bass
import concourse.mybir as mybir
from concourse.bass_utils import run_bass_kernel_spmd

F32 = mybir.dt.float32
BF16 = mybir.dt.bfloat16
AF = mybir.ActivationFunctionType
ALU = mybir.AluOpType
P = 128
D = 2048
KC = 16
CTX = 256
EPS = 1e-6
OFF_U, OFF_V, OFF_Z, OFF_XB, OFF_C, OFF_DT, OFF_GATE, IN_W = 0, 2048, 4096, 8192, 13312, 14336, 14464, 18560
BLK_U, BLK_V, BLK_Z, BLK_XB, BLK_C, BLK_GA, BLK_GB = 0, 16, 32, 64, 104, 113, 129

EPOCH = 8000
NSLOT = 12
DEBUG = False
NAMES = {}


class Op:
    __slots__ = ("eng", "fn", "deps", "signal", "is_dma", "slot", "dval", "sigcnt", "line")

    def __init__(self, eng, fn, is_dma):
        self.eng = eng
        self.fn = fn
        self.deps = []
        self.signal = False
        self.is_dma = is_dma
        self.slot = None
        self.dval = None
        self.sigcnt = None
        self.line = None


class _Rec:
    def __getattr__(self, name):
        def f(*a, **k):
            self.call = (name, a, k)
            return None
        return f


class Sched:
    ENGS = ("pe", "act", "dve", "pool", "sp")

    def __init__(self, nc):
        self.nc = nc
        self.ops = {e: [] for e in self.ENGS}
        self.last_w = {}
        self.readers = {}
        self.dma_cnt = {e: 0 for e in self.ENGS}
        self.dma_last = {e: [None] * NSLOT for e in self.ENGS}

    def op(self, eng, fn, reads=(), writes=(), dma=False):
        rec = _Rec()
        fn(rec)
        _n, _a, _k = rec.call
        o = Op(eng, (lambda E, _n=_n, _a=_a, _k=_k: getattr(E, _n)(*_a, **_k)), dma)
        if DEBUG:
            import sys as _sys
            f = _sys._getframe(1)
            ln = []
            while f is not None and len(ln) < 4:
                ln.append(f.f_lineno)
                f = f.f_back
            o.line = ln
        deps = []
        for k in reads:
            w = self.last_w.get(k)
            if w is not None:
                deps.append(w)
            if isinstance(k, tuple) and k[0] == "ps":
                rl = self.readers.get(k)
                if rl:
                    deps.extend(v for e_, v in rl.items() if e_ != eng)
        for k in writes:
            w = self.last_w.get(k)
            if w is not None:
                deps.append(w)
            rl = self.readers.get(k)
            if rl:
                deps.extend(rl.values())
        if dma:
            c = self.dma_cnt[eng]
            o.slot = c % NSLOT
            o.dval = 16 * (c // NSLOT + 1)
            prev = self.dma_last[eng][o.slot]
            if prev is not None:
                deps.append(prev)
            self.dma_last[eng][o.slot] = o
            self.dma_cnt[eng] = c + 1
        seen = set()
        for d in deps:
            if d is o or id(d) in seen:
                continue
            seen.add(id(d))
            if (not d.is_dma) and d.eng == eng and eng == "pe":
                continue
            o.deps.append(d)
            if not d.is_dma:
                d.signal = True
        for k in writes:
            self.last_w[k] = o
            self.readers[k] = {}
        for k in reads:
            rl = self.readers.setdefault(k, {})
            rl[("dma", id(o)) if dma else eng] = o
        self.ops[eng].append(o)
        return o

    def barrier(self):
        lasts = []
        for e in self.ENGS:
            for o in reversed(self.ops[e]):
                if not o.is_dma and o.fn is not None:
                    lasts.append(o)
                    break
            for o in self.dma_last[e]:
                if o is not None:
                    lasts.append(o)
        for e in self.ENGS:
            o = Op(e, None, False)
            for d in lasts:
                if d.eng == e and not d.is_dma:
                    continue
                o.deps.append(d)
                if not d.is_dma:
                    d.signal = True
            self.ops[e].append(o)
        self.last_w = {}
        self.readers = {}

    def final_wait(self, eng, ops):
        o = Op(eng, None, False)
        for d in ops:
            o.deps.append(d)
            if not d.is_dma:
                d.signal = True
        self.ops[eng].append(o)

    def emit(self, stack):
        nc = self.nc
        nsig = {}
        for e in self.ENGS:
            c = 0
            for o in self.ops[e]:
                if o.signal and not o.is_dma:
                    c += 1
                    o.sigcnt = c
            nsig[e] = c
        esem = {}
        for e in self.ENGS:
            ne = (nsig[e] + EPOCH - 1) // EPOCH
            esem[e] = [stack.enter_context(nc.semaphore(f"s_{e}_{i}")) for i in range(ne)]
        dsem = {}
        for e in self.ENGS:
            n = min(self.dma_cnt[e], NSLOT)
            dsem[e] = [stack.enter_context(nc.semaphore(f"d_{e}_{i}")) for i in range(n)]
        engobj = {"pe": "tensor", "act": "scalar", "dve": "vector", "pool": "gpsimd", "sp": "sync"}
        stats = {}

        def run(ename, E):
            waited = {}
            maxep = {}
            nw = 0
            for o in self.ops[ename]:
                for d in o.deps:
                    if d.is_dma:
                        sem = dsem[d.eng][d.slot]
                        val = d.dval
                        key = ("d", d.eng, d.slot)
                    else:
                        ep = (d.sigcnt - 1) // EPOCH
                        val = (d.sigcnt - 1) % EPOCH + 1
                        sem = esem[d.eng][ep]
                        key = ("e", d.eng, ep)
                        if maxep.get(d.eng, -1) > ep:
                            continue
                        maxep[d.eng] = ep
                    if waited.get(key, 0) >= val:
                        continue
                    waited[key] = val
                    E.wait_ge(sem, val)
                    nw += 1
                if o.fn is None:
                    continue
                ins = o.fn(E)
                if DEBUG:
                    try:
                        NAMES[ins.ins.name] = o.line
                    except Exception:
                        pass
                if o.is_dma:
                    ins.then_inc(dsem[ename][o.slot], 16)
                elif o.signal:
                    ep = (o.sigcnt - 1) // EPOCH
                    ins.then_inc(esem[ename][ep], 1)
            stats[ename] = (len(self.ops[ename]), nw)

        with nc.Block() as block:
            for ename in self.ENGS:
                getattr(block, engobj[ename])(lambda E, ename=ename: run(ename, E))
        return stats


class Arena:
    def __init__(self, ap, nwords):
        self.ap = ap
        self.n = nwords
        self.off = 0

    def f32(self, n):
        o = self.off
        self.off += n
        assert self.off <= self.n, ("SBUF arena overflow", self.off, self.n)
        return self.ap[:, o:o + n]

    def bf(self, n):
        w = (n + 1) // 2
        o = self.off
        self.off += w
        assert self.off <= self.n, ("SBUF arena overflow", self.off, self.n)
        return self.ap[:, o:o + w].bitcast(BF16)[:, 0:n]


NCONST = 6 * 128


def make_consts():
    i = np.arange(128)
    p, q = i[:, None], i[None, :]
    mats = [p == q, p <= q, p >= q, p > q, p < q, np.ones((128, 128), bool)]
    return np.concatenate([m.astype(np.float32) for m in mats], axis=1)


class _Stop(Exception):
    pass


def build(cfg):
    nc_holder = {}
    try:
        return _build(cfg, nc_holder)
    except _Stop:
        s, st, nc = nc_holder["s"], nc_holder["st"], nc_holder["nc"]
        s.barrier()
        stats = s.emit(st)
        st.close()
        return nc, stats


def _build(cfg, nc_holder):
    FF = cfg["FF"]
    FFC = FF // 128
    TOKH = cfg["TOKH"]
    TA = cfg["TA"]
    TB = cfg["TB"]
    TC = cfg["TC"]
    SEGW = 64
    NCH = TOKH // 128
    NQ = FFC // 11
    assert FFC % 11 == 0 and TOKH % TA == 0 and TOKH % TB == 0 and TOKH % TC == 0
    nc = bass.Bass("TRN2", target_bir_lowering=False)

    def din(name, shape):
        return nc.dram_tensor(name, list(shape), F32, kind="ExternalInput").ap()

    x_in = din("x", [2 * TOKH, D])
    ctx_in = din("ctx", [CTX, D])
    cvec = din("cvec", [32, 128])
    w_mod = din("w_mod", [D, 9 * D])
    b_mod = din("b_mod", [9 * D])
    norms = din("norms", [48, 128])
    norm_final = din("norm_final", [D])
    Wsrc = {
        "f1g": din("ffn1_gate", [D, FF]), "f1u": din("ffn1_up", [D, FF]), "f1d": din("ffn1_down", [FF, D]),
        "f2g": din("ffn2_gate", [D, FF]), "f2u": din("ffn2_up", [D, FF]), "f2d": din("ffn2_down", [FF, D]),
        "win": din("w_in", [D, IN_W]), "wdt": din("w_dt", [D, 128]),
        "wa": din("w_a", [D, D]), "wb": din("w_b", [2 * D, D]), "wo": din("w_out", [D, D]),
    }
    b_gate = din("b_gate", [32, 128])
    ln_wb = din("gmlp_ln_wb", [32, 128])
    wsT_in = din("gmlp_wsT", [128, 8, 128])
    bs_in = din("gmlp_bs", [8 * 128])
    conv_w = din("conv_w", [240, 128])
    conv_b = din("conv_b", [48, 128])
    a_log = din("a_log", [128])
    dt_bias = din("dt_bias", [128])
    d_skip = din("d_skip", [64])
    ssm_norm = din("ssm_norm", [2 * D])
    consts_in = din("consts", [128, NCONST])
    out_d = nc.dram_tensor("out", [TOKH, D], F32, kind="ExternalOutput").ap()

    def dscr(name, shape, dt):
        return nc.dram_tensor(name, list(shape), dt).ap()

    Wbf = {}
    for nm in ("f1g", "f1u", "f2g", "f2u"):
        Wbf[nm] = dscr("bf_" + nm, [FFC, 128, KC, 128], BF16)
    for nm in ("f1d", "f2d"):
        Wbf[nm] = dscr("bf_" + nm, [D // 512, NQ, 128, 11, 512], BF16)
    Wbf["win"] = dscr("bf_win", [IN_W // 128, 128, KC, 128], BF16)
    Wbf["wdt"] = dscr("bf_wdt", [1, 128, KC, 128], BF16)
    Wbf["wa"] = dscr("bf_wa", [16, 128, KC, 128], BF16)
    Wbf["wo"] = dscr("bf_wo", [16, 128, KC, 128], BF16)
    Wbf["wb"] = dscr("bf_wb", [16, 128, 32, 128], BF16)
    x1_d = dscr("x1_d", [TOKH, D], F32)
    hT_d = dscr("hT_d", [128, KC, TOKH], BF16)
    xh_d = dscr("xh_d", [TOKH, 4096], BF16)
    BT_d = dscr("BT_d", [8, 128, TOKH], BF16)
    dtA_d = dscr("dtA_d", [TOKH, 256], F32)
    Sb_d = dscr("Sb_d", [NCH, 128, 4096], F32)
    decb_d = dscr("decb_d", [NCH, 128, 64], F32)
    Hent_d = dscr("Hent_d", [2, NCH, 128, 4096], BF16)
    gates_d = dscr("gates_d", [4, 128, D], F32)
    Hsave_d = dscr("Hsave_d", [128, 4096], F32)

    st = ExitStack()
    NW = 52000
    arena_t = st.enter_context(nc.sbuf_tensor("arena", [128, NW], F32))
    AR = Arena(arena_t, NW)
    pst = [st.enter_context(nc.psum_tensor(f"ps{i}", [128, 512], F32)) for i in range(8)]
    psb = [t.bitcast(BF16) for t in pst]
    s = Sched(nc)
    nc_holder.update(s=s, st=st, nc=nc)
    STOP = cfg.get("stop")

    def stop_if(name):
        if STOP == name:
            raise _Stop()
    bankctr = [0]

    def newbank(lo=0, hi=4):
        b = lo + bankctr[0] % (hi - lo)
        bankctr[0] += 1
        return b

    def PK(b):
        return ("ps", b)

    def mm(out, lhsT, rhs, start, stop, reads, writes):
        s.op("pe", lambda E: E.matmul(out, lhsT=lhsT, rhs=rhs, start=start, stop=stop), reads, writes)

    def dma(eng, out, in_, reads=(), writes=()):
        return s.op(eng, lambda E: E.dma_start(out=out, in_=in_), reads, writes, dma=True)

    CONST = AR.f32(NCONST)
    IDF, LE, GE, GT_, LT_, ONES = [CONST[:, i * 128:(i + 1) * 128] for i in range(6)]
    IDB = AR.bf(128)
    COLS = AR.f32(528)
    MCOL = AR.f32(192)
    AB = AR.f32(160)
    EPSC = AR.f32(1)
    SSQ = AR.f32(4)
    RMS = AR.f32(4)
    RSTD = AR.f32(4)
    DTB = AR.f32(128)
    ANEG = AR.f32(128)
    DSK = AR.f32(64)
    persist_small = AR.off
    HB_ = AR.f32(4096)
    persist_mark = AR.off
    C_CV, C_N, C_BM, C_CW, C_CB, C_BG, C_LN = 0, 32, 80, 176, 416, 464, 496

    dma("sp", CONST, consts_in, writes=["const"])
    s.op("dve", lambda E: E.tensor_copy(out=IDB, in_=IDF), reads=["const"], writes=["idb"])
    s.op("pool", lambda E: E.memset(EPSC, EPS), writes=["eps"])
    dma("sp", DTB, dt_bias.partition_broadcast(128), writes=["dtb"])
    dma("sp", ANEG, a_log.partition_broadcast(128), writes=["aneg"])
    dma("sp", DSK, d_skip.partition_broadcast(128), writes=["dsk"])
    s.op("act", lambda E: E.activation(out=ANEG, in_=ANEG, func=AF.Exp), reads=["aneg"], writes=["aneg"])
    s.op("dve", lambda E: E.tensor_scalar_mul(out=ANEG, in0=ANEG, scalar1=-1.0), reads=["aneg"], writes=["aneg"])

    def conv_k(nm, nblk, kc=KC):
        src = Wsrc[nm].rearrange("(k p) (b c) -> b p k c", p=128, c=128)
        for b0 in range(nblk):
            dma("pool", Wbf[nm][b0], src[b0], writes=[("W", nm, b0)])

    def conv_d(nm):
        src = Wsrc[nm].rearrange("(q f p) (d c) -> d q p f c", p=128, f=11, c=512)
        for d_ in range(D // 512):
            for q_ in range(NQ):
                dma("pool", Wbf[nm][d_, q_], src[d_, q_], writes=[("W", nm, d_, q_)])

    def wkey(nm, b):
        return ("W", nm, b)

    ROWS = AR.f32(5 * 128)
    rowsrc = [(cvec, 32), (norms, 48), None, (conv_w, 240), (conv_b, 48), (b_gate, 32)]
    bm2 = b_mod.rearrange("(j c p) -> j c p", c=16, p=128)
    pieces = [(cvec, 0, 32), (norms, 0, 48)]
    for j in (0, 1, 3, 4, 6, 7):
        pieces.append((bm2[j], 0, 16))
    pieces += [(conv_w, 0, 240), (conv_b, 0, 48), (b_gate, 0, 32), (ln_wb, 0, 32)]
    r = 0
    for (ap, r0, n) in pieces:
        done = 0
        while done < n:
            t = r // 128
            ro = r % 128
            m = min(n - done, 128 - ro)
            dma("sp", ROWS[ro:ro + m, t * 128:(t + 1) * 128], ap[done:done + m, :], writes=[("rows", t)])
            done += m
            r += m
    assert r == 528
    for t in range(5):
        n = min(128, 528 - t * 128)
        b = newbank()
        s.op("pe", lambda E, t=t, n=n, b=b: E.transpose(pst[b][:, 0:n], ROWS[0:n, t * 128:(t + 1) * 128], IDF[0:n, 0:n]),
             reads=[("rows", t), "const"], writes=[PK(b)])
        s.op("dve", lambda E, t=t, n=n, b=b: E.tensor_copy(out=COLS[:, t * 128:t * 128 + n], in_=pst[b][:, 0:n]),
             reads=[PK(b)], writes=["cols"])
    AR.off = persist_mark

    m0 = AR.off
    SC = AR.bf(32)
    SCB = AR.bf(2 * KC * 128)
    s.op("act", lambda E: E.activation(out=SC, in_=COLS[:, C_CV:C_CV + 32], func=AF.Silu), reads=["cols"], writes=["sc"])
    SCBv = SCB.rearrange("p (t k m) -> p t k m", t=2, k=KC)
    for t in range(2):
        s.op("dve", lambda E, t=t: E.tensor_copy(out=SCBv[:, t], in_=SC[:, t * 16:(t + 1) * 16].unsqueeze(2).to_broadcast([128, KC, 128])),
             reads=["sc"], writes=[("scb", t)])
    SCv = SC.rearrange("p (t k) -> p t k", t=2)
    WM = [AR.bf(KC * 128).rearrange("p (k c) -> p k c", k=KC) for _ in range(4)]
    wmsrc = w_mod.rearrange("(k p) (b c) -> b p k c", p=128, c=128)
    wmc = [0]

    def load_wm(blk):
        i = wmc[0] % 4
        wmc[0] += 1
        dma("pool", WM[i], wmsrc[blk], writes=[("wm", i)])
        return WM[i], ("wm", i)

    mb = newbank()
    for ji, j in enumerate((0, 1, 3, 4, 6, 7)):
        for c in range(16):
            wblk, wk = load_wm(j * 16 + c)
            idx = ji * 16 + c
            for k in range(KC):
                mm(pst[mb][:, idx * 2:idx * 2 + 2], wblk[:, k, :], SCv[:, :, k], k == 0, k == KC - 1,
                   reads=[wk, "sc"], writes=[PK(mb)])
    bmcol = COLS[:, C_BM:C_BM + 96]
    s.op("dve", lambda E: E.tensor_tensor(out=MCOL.rearrange("p (i t) -> p i t", t=2), in0=pst[mb][:, 0:192].rearrange("p (i t) -> p i t", t=2),
                                          in1=bmcol.unsqueeze(2).to_broadcast([128, 96, 2]), op=ALU.add),
         reads=[PK(mb), "cols"], writes=["mcol"])
    MC = MCOL.rearrange("p (j c t) -> p j c t", j=6, c=16)
    ABv = AB.rearrange("p (i c) -> p i c", c=16)
    NRM = COLS[:, C_N:C_N + 48].rearrange("p (i c) -> p i c", c=16)

    def mk_ab(ia, nidx, jscale, jshift, t):
        s.op("dve", lambda E: E.tensor_scalar_add(out=ABv[:, ia], in0=MC[:, jscale, :, t], scalar1=1.0), reads=["mcol"], writes=["ab"])
        s.op("dve", lambda E: E.tensor_tensor(out=ABv[:, ia], in0=ABv[:, ia], in1=NRM[:, nidx], op=ALU.mult), reads=["ab", "cols"], writes=["ab"])
        s.op("dve", lambda E: E.tensor_copy(out=ABv[:, ia + 1], in_=MC[:, jshift, :, t]), reads=["mcol"], writes=["ab"])

    mk_ab(0, 0, 1, 0, 0)
    mk_ab(2, 1, 3, 2, 0)
    mk_ab(4, 2, 5, 4, 0)
    mk_ab(6, 0, 1, 0, 1)
    mk_ab(8, 1, 3, 2, 1)
    BMB = [AR.f32(128) for _ in range(2)]
    GST = [AR.f32(128) for _ in range(2)]
    gc = 0
    for (j, t, gi, scl) in ((2, 0, 0, 0.5), (5, 0, 1, 1.0), (8, 0, 2, 0.5), (2, 1, 3, 0.5)):
        for c in range(16):
            wblk, wk = load_wm(j * 16 + c)
            b = newbank()
            for k in range(KC):
                mm(pst[b][:, 0:128], SCBv[:, t, k, :], wblk[:, k, :], k == 0, k == KC - 1, reads=[wk, ("scb", t)], writes=[PK(b)])
            i = gc % 2
            gc += 1
            dma("sp", BMB[i], b_mod[j * D + c * 128:j * D + (c + 1) * 128].partition_broadcast(128), writes=[("bmb", i)])
            s.op("dve", lambda E, i=i, b=b: E.tensor_tensor(out=GST[i], in0=pst[b][:, 0:128], in1=BMB[i], op=ALU.add),
                 reads=[PK(b), ("bmb", i)], writes=[("gst", i)])
            s.op("act", lambda E, i=i, scl=scl: E.activation(out=GST[i], in_=GST[i], func=AF.Copy, scale=scl), reads=[("gst", i)], writes=[("gst", i)])
            dma("sp", gates_d[gi, :, c * 128:(c + 1) * 128], GST[i], reads=[("gst", i)], writes=[("gates", gi)])

    conv_k("f1g", FFC)
    conv_k("f1u", FFC)
    conv_d("f1d")
    conv_k("win", IN_W // 128)
    conv_k("wdt", 1)
    conv_k("wa", 16)
    conv_k("wo", 16)
    srcwb = Wsrc["wb"].rearrange("(k p) (b c) -> b p k c", p=128, c=128)
    for b0 in range(16):
        dma("pool", Wbf["wb"][b0], srcwb[b0], writes=[("W", "wb", b0)])
    conv_k("f2g", FFC)
    conv_k("f2u", FFC)
    conv_d("f2d")
    s.barrier()
    stop_if("setup")
    AR.off = persist_mark

    class G:
        pass

    g = G()

    def alloc_common(T, wdn=11 * 512):
        ns = T // 128
        g.T = T
        g.ns = ns
        g.X = AR.f32(ns * D).rearrange("p (s d) -> p s d", s=ns)
        g.XN = AR.bf(D)
        g.hT = AR.bf(KC * T).rearrange("p (k t) -> p k t", k=KC)
        g.WP = [AR.bf(KC * 128).rearrange("p (k c) -> p k c", k=KC) for _ in range(6)]
        g.WD = [AR.bf(wdn) for _ in range(2)]
        g.wpc = 0
        g.wdc = 0
        g.TMP = [AR.f32(512) for _ in range(2)]
        g.tmpc = 0
        g.GSL = [AR.f32(512) for _ in range(2)]
        g.gslc = 0

    def load_wblk(nm, b):
        i = g.wpc % len(g.WP)
        g.wpc += 1
        dma("sp", g.WP[i], Wbf[nm][b], reads=[wkey(nm, b)], writes=[("wp", i)])
        return g.WP[i], ("wp", i)

    def load_wd(nm, d_, q):
        i = g.wdc % 2
        g.wdc += 1
        ap = g.WD[i].rearrange("p (f c) -> p f c", f=11)
        dma("sp", ap, Wbf[nm][d_, q], reads=[("W", nm, d_, q)], writes=[("wd", i)])
        return ap, ("wd", i)

    def load_wb32(b):
        i = g.wdc % 2
        g.wdc += 1
        ap = g.WD[i][:, 0:32 * 128].rearrange("p (k c) -> p k c", k=32)
        dma("sp", ap, Wbf["wb"][b], reads=[("W", "wb", b)], writes=[("wd", i)])
        return ap, ("wd", i)

    def rstd_of(src_ap, sub, n, xkey):
        s.op("pool", lambda E: E.memset(SSQ[:, sub:sub + 1], 0.0), writes=[("ssq", sub)])
        s.op("act", lambda E: E.activation(out=g.XN[:, 0:n], in_=src_ap, func=AF.Square, accum_out=SSQ[:, sub:sub + 1]),
             reads=[xkey], writes=["xn", ("ssq", sub)])
        s.op("act", lambda E: E.activation(out=RMS[:, sub:sub + 1], in_=SSQ[:, sub:sub + 1], func=AF.Sqrt, scale=1.0 / n, bias=EPSC),
             reads=[("ssq", sub), "eps"], writes=[("rms", sub)])
        s.op("dve", lambda E: E.reciprocal(out=RSTD[:, sub:sub + 1], in_=RMS[:, sub:sub + 1]), reads=[("rms", sub)], writes=[("rstd", sub)])

    def norm_to_hT(ia):
        Acol, Bcol = ABv[:, ia], ABv[:, ia + 1]
        ev = 0
        for sub in range(g.ns):
            rstd_of(g.X[:, sub, :], sub, D, ("X", sub))
            s.op("dve", lambda E, sub=sub: E.tensor_scalar_mul(out=g.XN, in0=g.X[:, sub, :], scalar1=RSTD[:, sub:sub + 1]),
                 reads=[("X", sub), ("rstd", sub)], writes=["xn"])
            for half in range(2):
                b = newbank()
                for c8 in range(8):
                    c = half * 8 + c8
                    s.op("pe", lambda E, b=b, c8=c8, c=c: E.transpose(psb[b][:, c8 * 128:(c8 + 1) * 128], g.XN[:, c * 128:(c + 1) * 128], IDB),
                         reads=["xn", "idb"], writes=[PK(b)])
                for c8 in range(8):
                    c = half * 8 + c8
                    o_ap = g.hT[:, c, sub * 128:(sub + 1) * 128]
                    i_ap = psb[b][:, c8 * 128:(c8 + 1) * 128]
                    if half == 0:
                        s.op("act", lambda E, o_ap=o_ap, i_ap=i_ap, c=c: E.activation(out=o_ap, in_=i_ap, func=AF.Identity, scale=Acol[:, c:c + 1], bias=Bcol[:, c:c + 1]),
                             reads=[PK(b), "ab"], writes=[("hT", c)])
                    else:
                        s.op("dve", lambda E, o_ap=o_ap, i_ap=i_ap, c=c: E.tensor_scalar(out=o_ap, in0=i_ap, scalar1=Acol[:, c:c + 1], scalar2=Bcol[:, c:c + 1], op0=ALU.mult, op1=ALU.add),
                             reads=[PK(b), "ab"], writes=[("hT", c)])
                    ev += 1

    def resid_update(bank, sub, dblk, gi):
        i = g.tmpc % 2
        g.tmpc += 1
        tmp = g.TMP[i]
        s.op("dve", lambda E: E.tensor_tensor(out=tmp, in0=pst[bank][:, 0:512], in1=g.gsl, op=ALU.mult),
             reads=[PK(bank), g.gslk], writes=[("tmp", i)])
        xs = g.X[:, sub, dblk * 512:(dblk + 1) * 512]
        s.op("pool", lambda E: E.tensor_tensor(out=xs, in0=xs, in1=tmp, op=ALU.add), reads=[("tmp", i), ("X", sub)], writes=[("X", sub)])

    def load_gsl(gi, dblk):
        i = g.gslc % 2
        g.gslc += 1
        dma("sp", g.GSL[i], gates_d[gi, :, dblk * 512:(dblk + 1) * 512], reads=[("gates", gi)], writes=[("gsl", i)])
        g.gsl = g.GSL[i]
        g.gslk = ("gsl", i)

    def ffn(ia, gi, wg, wu, wd):
        T, ns = g.T, g.ns
        norm_to_hT(ia)
        stop_if("f_norm")
        for f in range(FFC):
            wgb, kg = load_wblk(wg, f)
            wub, ku = load_wblk(wu, f)
            pg = newbank()
            pu = newbank()
            for k in range(KC):
                mm(pst[pg][:, 0:T], wgb[:, k, :], g.hT[:, k, :], k == 0, k == KC - 1, reads=[kg, ("hT", k)], writes=[PK(pg)])
            for k in range(KC):
                mm(pst[pu][:, 0:T], wub[:, k, :], g.hT[:, k, :], k == 0, k == KC - 1, reads=[ku, ("hT", k)], writes=[PK(pu)])
            sg = g.SG[f % 2]
            s.op("act", lambda E, sg=sg, pg=pg: E.activation(out=sg[:, 0:T], in_=pst[pg][:, 0:T], func=AF.Silu), reads=[PK(pg)], writes=[("sg", f % 2)])
            s.op("dve", lambda E, sg=sg, pu=pu, f=f: E.tensor_tensor(out=g.GTt[:, f, :], in0=pst[pu][:, 0:T], in1=sg[:, 0:T], op=ALU.mult),
                 reads=[PK(pu), ("sg", f % 2)], writes=[("gT", f)])
        stop_if("f_gu")
        for dblk in range(D // 512):
            load_gsl(gi, dblk)
            for q in range(NQ):
                wdb, kd = load_wd(wd, dblk, q)
                for sub in range(ns):
                    bank = 4 + sub
                    for fi in range(11):
                        f = q * 11 + fi
                        mm(pst[bank][:, 0:512], g.GTt[:, f, sub * 128:(sub + 1) * 128], wdb[:, fi, :], q == 0 and fi == 0, q == NQ - 1 and fi == 10,
                           reads=[kd, ("gT", f)], writes=[PK(bank)])
            for sub in range(ns):
                resid_update(4 + sub, sub, dblk, gi)

    def conv_chunk(bank, ch, nseg, segw, T, post, postkey, cengine):
        i = g.prec % 2
        g.prec += 1
        pre = g.PRE[i][:, 0:nseg * (segw + 4)].rearrange("p (s w) -> p s w", s=nseg)
        ca = g.CA[i].rearrange("p (s w) -> p s w", s=nseg)
        s.op("act", lambda E: E.activation(out=pre[:, :, 2:2 + segw], in_=pst[bank][:, 0:T].rearrange("p (s w) -> p s w", s=nseg), func=AF.Copy),
             reads=[PK(bank)], writes=[("pre", i)])
        E_ = cengine
        s.op(E_, lambda E: E.tensor_scalar_mul(out=ca, in0=pre[:, :, 0:segw], scalar1=COLS[:, C_CW + ch:C_CW + ch + 1]),
             reads=[("pre", i), "cols"], writes=[("ca", i)])
        for t in range(1, 5):
            cw = COLS[:, C_CW + t * 48 + ch:C_CW + t * 48 + ch + 1]
            s.op("dve", lambda E, t=t, cw=cw: E.scalar_tensor_tensor(out=ca, in0=pre[:, :, t:t + segw], scalar=cw, in1=ca, op0=ALU.mult, op1=ALU.add),
                 reads=[("pre", i), ("ca", i), "cols"], writes=[("ca", i)])
        s.op("act", lambda E: E.activation(out=post, in_=g.CA[i][:, 0:T], func=AF.Silu, bias=COLS[:, C_CB + ch:C_CB + ch + 1]),
             reads=[("ca", i), "cols"], writes=[postkey])

    def phaseA_alloc(T):
        alloc_common(T)
        ns = g.ns
        g.GTt = AR.bf(FFC * T).rearrange("p (f t) -> p f t", f=FFC)
        g.SG = [AR.bf(T) for _ in range(2)]
        g.XHT = AR.bf(ns * 5120).rearrange("p (s c) -> p s c", s=ns)
        g.PRE = [AR.bf(T + 64) for _ in range(2)]
        g.CA = [AR.f32(T) for _ in range(2)]
        g.POST = [AR.bf(T) for _ in range(3)]
        g.prec = 0
        g.postc = 0
        g.DT = AR.f32(ns * 128).rearrange("p (s c) -> p s c", s=ns)
        g.AA = AR.f32(ns * 128).rearrange("p (s c) -> p s c", s=ns)
        g.WGT = AR.f32(128)
        g.DEC = AR.f32(128)
        g.WX = [AR.bf(512) for _ in range(2)]
        g.wxc = 0
        g.SST = [AR.f32(512) for _ in range(2)]
        g.sstc = 0
        g.HS = [AR.bf(512) for _ in range(2)]
        g.hsc = 0
        g.H2 = AR.f32(4096)

    def zero_pads(nseg, segw):
        for i in range(2):
            s.op("pool", lambda E, i=i: E.memset(g.PRE[i], 0.0), writes=[("pre", i)])

    def phaseA_tile(src_ap, tok0, mode, gi, iaF, iaM, nseg, segw, own_tok0=None, sub_order=None):
        T, ns = g.T, g.ns
        dma("sp", g.X, src_ap[tok0:tok0 + T, :].rearrange("(s p) d -> p s d", p=128), writes=[("X", sub) for sub in range(ns)])
        stop_if("t_load")
        ffn(iaF, gi, "f1g", "f1u", "f1d")
        stop_if("t_ffn")
        if mode == "own":
            dma("sp", x1_d[own_tok0:own_tok0 + T, :].rearrange("(s p) d -> p s d", p=128), g.X, reads=[("X", sub) for sub in range(ns)], writes=["x1d"])
        norm_to_hT(iaM)
        if mode == "own":
            dma("sp", hT_d[:, :, own_tok0:own_tok0 + T], g.hT, reads=[("hT", c) for c in range(KC)], writes=["hTd"])
        stop_if("t_norm")
        for ch in range(40):
            wblk, wk = load_wblk("win", BLK_XB + ch)
            b = newbank()
            for k in range(KC):
                mm(pst[b][:, 0:T], wblk[:, k, :], g.hT[:, k, :], k == 0, k == KC - 1, reads=[wk, ("hT", k)], writes=[PK(b)])
            pi = g.postc % 3
            g.postc += 1
            post = g.POST[pi]
            conv_chunk(b, ch, nseg, segw, T, post, ("post", pi), "pool" if ch % 2 else "dve")
            tb = newbank()
            for sub in range(ns):
                s.op("pe", lambda E, tb=tb, sub=sub, post=post: E.transpose(psb[tb][:, sub * 128:(sub + 1) * 128], post[:, sub * 128:(sub + 1) * 128], IDB),
                     reads=[("post", pi), "idb"], writes=[PK(tb)])
            s.op("act", lambda E, tb=tb, ch=ch: E.activation(out=g.XHT[:, :, ch * 128:(ch + 1) * 128], in_=psb[tb][:, 0:T].rearrange("p (s c) -> p s c", s=ns), func=AF.Copy),
                 reads=[PK(tb)], writes=[("xht", ch)])
            if mode == "own" and ch >= 32:
                dma("sp", BT_d[ch - 32, :, own_tok0:own_tok0 + T], post, reads=[("post", pi)], writes=["BTd"])
        if mode == "own":
            dma("sp", xh_d[own_tok0:own_tok0 + T, :].rearrange("(s p) c -> p s c", p=128), g.XHT[:, :, 0:4096],
                reads=[("xht", ch) for ch in range(32)], writes=["xhd"])
        stop_if("t_xb")
        wblk, wk = load_wblk("wdt", 0)
        for sub in range(ns):
            b = newbank()
            for k in range(KC):
                mm(pst[b][:, 0:128], g.hT[:, k, sub * 128:(sub + 1) * 128], wblk[:, k, :], k == 0, k == KC - 1, reads=[wk, ("hT", k)], writes=[PK(b)])
            s.op("dve", lambda E, b=b, sub=sub: E.tensor_tensor(out=g.DT[:, sub, :], in0=pst[b][:, 0:128], in1=DTB, op=ALU.add), reads=[PK(b), "dtb"], writes=[("dt", sub)])
            s.op("act", lambda E, sub=sub: E.activation(out=g.DT[:, sub, :], in_=g.DT[:, sub, :], func=AF.Exp), reads=[("dt", sub)], writes=[("dt", sub)])
            s.op("act", lambda E, sub=sub: E.activation(out=g.DT[:, sub, :], in_=g.DT[:, sub, :], func=AF.Ln, bias=1.0), reads=[("dt", sub)], writes=[("dt", sub)])
            s.op("dve", lambda E, sub=sub: E.tensor_tensor(out=g.AA[:, sub, :], in0=g.DT[:, sub, :], in1=ANEG, op=ALU.mult), reads=[("dt", sub), "aneg"], writes=[("aa", sub)])
        if mode == "own":
            dv = dtA_d[own_tok0:own_tok0 + T, :].rearrange("(s p) c -> p s c", p=128)
            dma("sp", dv[:, :, 0:128], g.DT, reads=[("dt", sub) for sub in range(ns)], writes=["dtAd"])
            dma("sp", dv[:, :, 128:256], g.AA, reads=[("aa", sub) for sub in range(ns)], writes=["dtAd2"])
        stop_if("t_dt")
        passes = {"ctx": [(0, list(range(ns)), "chainF"), (1, list(range(ns))[::-1], "chainB")],
                  "other": [(1, list(range(ns))[::-1], "chainB")],
                  "own": [(0, list(range(ns)), "chainF"), (1, list(range(ns)), "store")]}[mode]
        for (dr, subs, act) in passes:
            Hbuf = HB_ if not (mode == "ctx" and dr == 0) else g.H2
            hkey = "HB" if Hbuf is HB_ else "H2"
            for sub in subs:
                b = newbank()
                lhs = GT_ if dr == 0 else LT_
                mm(pst[b][:, 0:64], lhs, g.AA[:, sub, dr * 64:(dr + 1) * 64], True, True, reads=["const", ("aa", sub)], writes=[PK(b)])
                mm(pst[b][:, 64:128], ONES, g.AA[:, sub, dr * 64:(dr + 1) * 64], True, True, reads=["const", ("aa", sub)], writes=[PK(b)])
                s.op("act", lambda E, b=b: E.activation(out=g.WGT, in_=pst[b][:, 0:128], func=AF.Exp), reads=[PK(b)], writes=["wgt"])
                s.op("dve", lambda E, sub=sub, dr=dr: E.tensor_tensor(out=g.DEC[:, 0:64], in0=g.WGT[:, 0:64], in1=g.DT[:, sub, dr * 64:(dr + 1) * 64], op=ALU.mult),
                     reads=["wgt", ("dt", sub)], writes=["dec"])
                if act == "store":
                    c = own_tok0 // 128 + sub
                    dma("sp", decb_d[c], g.WGT[:, 64:128], reads=["wgt"], writes=["decbd"])
                if act == "chainF" and mode == "own":
                    c = own_tok0 // 128 + sub
                for gg in range(8):
                    wi = g.wxc % 2
                    g.wxc += 1
                    wx = g.WX[wi]
                    s.op("pool", lambda E, wx=wx, sub=sub, gg=gg: E.tensor_tensor(
                        out=wx.rearrange("p (h d) -> p h d", h=8), in0=g.XHT[:, sub, gg * 512:(gg + 1) * 512].rearrange("p (h d) -> p h d", h=8),
                        in1=g.DEC[:, gg * 8:(gg + 1) * 8].unsqueeze(2).to_broadcast([128, 8, 64]), op=ALU.mult),
                        reads=[("xht", gg * 4 + q) for q in range(4)] + ["dec"], writes=[("wx", wi)])
                    sb_ = newbank()
                    mm(pst[sb_][:, 0:512], g.XHT[:, sub, 4096 + gg * 128:4096 + (gg + 1) * 128], wx, True, True,
                       reads=[("xht", 32 + gg), ("wx", wi)], writes=[PK(sb_)])
                    hs = Hbuf[:, gg * 512:(gg + 1) * 512]
                    if act == "store":
                        si = g.sstc % 2
                        g.sstc += 1
                        s.op("act", lambda E, si=si, sb_=sb_: E.activation(out=g.SST[si], in_=pst[sb_][:, 0:512], func=AF.Copy), reads=[PK(sb_)], writes=[("sst", si)])
                        dma("sp", Sb_d[c, :, gg * 512:(gg + 1) * 512], g.SST[si], reads=[("sst", si)], writes=["Sbd"])
                    else:
                        if act == "chainF" and mode == "own":
                            hi = g.hsc % 2
                            g.hsc += 1
                            s.op("act", lambda E, hi=hi, hs=hs: E.activation(out=g.HS[hi], in_=hs, func=AF.Copy), reads=[(hkey, gg)], writes=[("hs", hi)])
                            dma("sp", Hent_d[0, c, :, gg * 512:(gg + 1) * 512], g.HS[hi], reads=[("hs", hi)], writes=["Hentd"])
                        s.op("pool", lambda E, hs=hs, gg=gg: E.tensor_tensor(
                            out=hs.rearrange("p (h d) -> p h d", h=8), in0=hs.rearrange("p (h d) -> p h d", h=8),
                            in1=g.WGT[:, 64 + gg * 8:64 + (gg + 1) * 8].unsqueeze(2).to_broadcast([128, 8, 64]), op=ALU.mult),
                            reads=[(hkey, gg), "wgt"], writes=[(hkey, gg)])
                        s.op("dve", lambda E, hs=hs, sb_=sb_: E.tensor_tensor(out=hs, in0=hs, in1=pst[sb_][:, 0:512], op=ALU.add),
                             reads=[(hkey, gg), PK(sb_)], writes=[(hkey, gg)])

    HKEYS = [("HB", gg) for gg in range(8)]
    AR.off = persist_mark
    phaseA_alloc(CTX)
    zero_pads(1, CTX)
    s.op("pool", lambda E: E.memset(HB_, 0.0), writes=HKEYS)
    s.op("pool", lambda E: E.memset(g.H2, 0.0), writes=[("H2", gg) for gg in range(8)])
    phaseA_tile(ctx_in, 0, "ctx", 3, 6, 8, 1, CTX)
    dma("sp", Hsave_d, g.H2, reads=[("H2", gg) for gg in range(8)], writes=["hsave"])
    s.barrier()
    stop_if("A0")
    AR.off = persist_mark
    phaseA_alloc(TA)
    zero_pads(TA // SEGW, SEGW)
    for t in reversed(range(TOKH // TA)):
        phaseA_tile(x_in, TOKH + t * TA, "other", 0, 0, 2, TA // SEGW, SEGW)
    s.barrier()
    dma("sp", g.H2, Hsave_d, reads=["hsave"], writes=["h2tmp"])
    dma("sp", Hsave_d, HB_, reads=HKEYS, writes=["hsave"])
    s.op("dve", lambda E: E.tensor_copy(out=HB_, in_=g.H2), reads=["h2tmp"], writes=HKEYS)
    stop_if("AO")
    for t in range(TOKH // TA):
        phaseA_tile(x_in, t * TA, "own", 0, 0, 2, TA // SEGW, SEGW, own_tok0=t * TA)
    s.barrier()
    stop_if("AW")
    AR.off = persist_mark
    dma("sp", HB_, Hsave_d, writes=HKEYS)
    SBUFS = [AR.f32(4096) for _ in range(2)]
    DCB = [AR.f32(64) for _ in range(2)]
    HSB = [AR.bf(4096) for _ in range(2)]
    for n_, c in enumerate(reversed(range(NCH))):
        i = n_ % 2
        dma("sp", SBUFS[i], Sb_d[c], reads=["Sbd"], writes=[("sbuf", i)])
        dma("sp", DCB[i], decb_d[c], reads=["decbd"], writes=[("dcb", i)])
        s.op("act", lambda E, i=i: E.activation(out=HSB[i], in_=HB_, func=AF.Copy), reads=HKEYS, writes=[("hsb", i)])
        dma("sp", Hent_d[1, c], HSB[i], reads=[("hsb", i)], writes=["Hentd"])
        for hf in range(2):
            eng = "dve" if hf == 0 else "pool"
            hs = HB_[:, hf * 2048:(hf + 1) * 2048]
            s.op(eng, lambda E, hs=hs, i=i, hf=hf: E.tensor_tensor(out=hs.rearrange("p (h d) -> p h d", h=32), in0=hs.rearrange("p (h d) -> p h d", h=32),
                                                                in1=DCB[i][:, hf * 32:(hf + 1) * 32].unsqueeze(2).to_broadcast([128, 32, 64]), op=ALU.mult),
                 reads=[("dcb", i)] + HKEYS[hf * 4:(hf + 1) * 4], writes=HKEYS[hf * 4:(hf + 1) * 4])
            s.op(eng, lambda E, hs=hs, i=i, hf=hf: E.tensor_tensor(out=hs, in0=hs, in1=SBUFS[i][:, hf * 2048:(hf + 1) * 2048], op=ALU.add),
                 reads=[("sbuf", i)] + HKEYS[hf * 4:(hf + 1) * 4], writes=HKEYS[hf * 4:(hf + 1) * 4])
    s.barrier()

    stop_if("SC")
    AR.off = persist_small
    alloc_common(TB, wdn=32 * 128)
    T, ns = g.T, g.ns
    g.PRE = [AR.bf(T + 64) for _ in range(2)]
    g.CA = [AR.f32(T) for _ in range(2)]
    g.prec = 0
    zero_pads(T // SEGW, SEGW)
    DTA = AR.f32(ns * 256).rearrange("p (s c) -> p s c", s=ns)
    EACS = AR.f32(ns * 128).rearrange("p (s c) -> p s c", s=ns)
    MRG = AR.bf(KC * T).rearrange("p (k t) -> p k t", k=KC)
    GBUF = [AR.bf(T) for _ in range(2)]
    WST = AR.bf(8 * 128).rearrange("p (g i) -> p g i", g=8)
    BSB = AR.f32(8 * 128).rearrange("p (g i) -> p g i", g=8)
    BS2 = AR.f32(16 * 128).rearrange("p (f i) -> p f i", f=16)
    ONB = AR.bf(128)
    BNS = AR.f32(4 * 6)
    BNA = AR.f32(2)
    LNS = AR.f32(4)
    TSB = [AR.f32(T) for _ in range(2)]
    regR = AR.off
    XHG = [AR.bf(ns * 512).rearrange("p (s c) -> p s c", s=ns) for _ in range(2)]
    BTG = [AR.bf(T) for _ in range(2)]
    HEG = [[[AR.bf(512) for _ in range(ns)] for _ in range(2)] for _ in range(2)]
    SSN = [AR.f32(512) for _ in range(2)]
    CTG = [AR.bf(T) for _ in range(2)]
    ZS = [AR.bf(ns * 512).rearrange("p (s c) -> p s c", s=ns) for _ in range(2)]
    CBM = [AR.bf(128) for _ in range(2)]
    LH8 = [AR.f32(1024) for _ in range(2)]
    E8 = [AR.bf(1024) for _ in range(2)]
    M8 = [AR.bf(1024) for _ in range(2)]
    XDT = [AR.bf(512) for _ in range(2)]
    TY = [AR.f32(512) for _ in range(3)]
    YT = AR.f32(512)
    YN = AR.bf(512)
    YNT = AR.bf(32 * T).rearrange("p (k t) -> p k t", k=32)
    endR1 = AR.off
    AR.off = regR
    UT = AR.bf(KC * T).rearrange("p (k t) -> p k t", k=KC)
    VF = AR.f32(ns * D).rearrange("p (s d) -> p s d", s=ns)
    VNB = AR.bf(ns * D).rearrange("p (s d) -> p s d", s=ns)
    AR.off = max(AR.off, endR1)
    dma("pool", WST, wsT_in, writes=["wst"])
    dma("sp", BSB.rearrange("p g i -> p (g i)"), bs_in.partition_broadcast(128), writes=["bsb"])
    s.op("dve", lambda E: E.tensor_copy(out=ONB, in_=ONES), reads=["const"], writes=["onb"])
    LNWC = COLS[:, C_LN:C_LN + 16]
    LNBC = COLS[:, C_LN + 16:C_LN + 32]
    for hh in range(2):
        b = newbank()
        mm(pst[b][:, 0:512], ONB, WST[:, hh * 4:(hh + 1) * 4, :].rearrange("p g i -> p (g i)"), True, True, reads=["onb", "wst"], writes=[PK(b)])
        for f4 in range(8):
            fc = hh * 8 + f4
            gq = f4 // 2
            s.op("dve", lambda E, b=b, fc=fc, gq=gq: E.scalar_tensor_tensor(out=BS2[:, fc, :], in0=pst[b][:, gq * 128:(gq + 1) * 128], scalar=LNBC[:, fc:fc + 1],
                                                                         in1=BSB[:, fc // 2, :], op0=ALU.mult, op1=ALU.add),
                 reads=[PK(b), "cols", "bsb"], writes=["bs2"])
    BGC = COLS[:, C_BG:C_BG + 32]

    for tix in range(TOKH // TB):
        tok0 = tix * TB
        dma("sp", g.hT, hT_d[:, :, tok0:tok0 + T], reads=["hTd"], writes=[("hT", c) for c in range(KC)])
        dma("sp", DTA, dtA_d[tok0:tok0 + T, :].rearrange("(s p) c -> p s c", p=128), reads=["dtAd", "dtAd2"], writes=["dta"])
        for sub in range(ns):
            b = newbank()
            mm(pst[b][:, 0:64], LE, DTA[:, sub, 128:192], True, True, reads=["const", "dta"], writes=[PK(b)])
            mm(pst[b][:, 64:128], GE, DTA[:, sub, 192:256], True, True, reads=["const", "dta"], writes=[PK(b)])
            s.op("act", lambda E, b=b, sub=sub: E.activation(out=EACS[:, sub, :], in_=pst[b][:, 0:128], func=AF.Exp), reads=[PK(b)], writes=[("eacs", sub)])
        for gg in range(8):
            gi2 = gg % 2
            dma("sp", XHG[gi2], xh_d[tok0:tok0 + T, gg * 512:(gg + 1) * 512].rearrange("(s p) c -> p s c", p=128), reads=["xhd"], writes=[("xhg", gi2)])
            dma("sp", BTG[gi2], BT_d[gg, :, tok0:tok0 + T], reads=["BTd"], writes=[("btg", gi2)])
            for dr in range(2):
                for sub in range(ns):
                    dma("sp", HEG[gi2][dr][sub], Hent_d[dr, tok0 // 128 + sub, :, gg * 512:(gg + 1) * 512], reads=["Hentd"], writes=[("heg", gi2, dr, sub)])
            dma("sp", SSN[gi2], ssm_norm[gg * 512:(gg + 1) * 512].partition_broadcast(128), writes=[("ssn", gi2)])
            wblk, wk = load_wblk("win", BLK_C + gg)
            b = newbank()
            for k in range(KC):
                mm(pst[b][:, 0:T], wblk[:, k, :], g.hT[:, k, :], k == 0, k == KC - 1, reads=[wk, ("hT", k)], writes=[PK(b)])
            conv_chunk(b, 40 + gg, T // SEGW, SEGW, T, CTG[gi2], ("ctg", gi2), "pool")
            zb = [4 + sub for sub in range(ns)]
            for j in range(4):
                wblk, wk = load_wblk("win", BLK_Z + gg * 4 + j)
                for sub in range(ns):
                    for k in range(KC):
                        mm(pst[zb[sub]][:, j * 128:(j + 1) * 128], g.hT[:, k, sub * 128:(sub + 1) * 128], wblk[:, k, :], k == 0, k == KC - 1,
                           reads=[wk, ("hT", k)], writes=[PK(zb[sub])])
            for sub in range(ns):
                s.op("act", lambda E, sub=sub, gi2=gi2: E.activation(out=ZS[gi2][:, sub, :], in_=pst[zb[sub]][:, 0:512], func=AF.Silu),
                     reads=[PK(zb[sub])], writes=[("zs", gi2, sub)])
            for sub in range(ns):
                b = newbank()
                mm(pst[b][:, 0:128], BTG[gi2][:, sub * 128:(sub + 1) * 128], CTG[gi2][:, sub * 128:(sub + 1) * 128], True, True,
                   reads=[("btg", gi2), ("ctg", gi2)], writes=[PK(b)])
                s.op("dve", lambda E, b=b: E.tensor_tensor(out=CBM[0], in0=pst[b][:, 0:128], in1=LE, op=ALU.mult), reads=[PK(b), "const"], writes=[("cbm", 0)])
                s.op("dve", lambda E, b=b: E.tensor_tensor(out=CBM[1], in0=pst[b][:, 0:128], in1=GE, op=ALU.mult), reads=[PK(b), "const"], writes=[("cbm", 1)])
                yb = 6 + (sub % 2)
                tys = []
                for dr in range(2):
                    UTm, TRI = (GT_, LE) if dr == 0 else (LT_, GE)
                    acol = DTA[:, sub, 128 + dr * 64 + gg * 8:128 + dr * 64 + gg * 8 + 8]
                    dcol = DTA[:, sub, dr * 64 + gg * 8:dr * 64 + gg * 8 + 8]
                    ecol = EACS[:, sub, dr * 64 + gg * 8:dr * 64 + gg * 8 + 8]
                    s.op("pool", lambda E, dr=dr, acol=acol, UTm=UTm: E.tensor_tensor(
                        out=LH8[dr].rearrange("p (h j) -> p h j", h=8), in0=acol.unsqueeze(2).to_broadcast([128, 8, 128]),
                        in1=UTm.unsqueeze(1).to_broadcast([128, 8, 128]), op=ALU.mult), reads=["dta", "const"], writes=[("lh8", dr)])
                    db = [newbank(), newbank()]
                    for h in range(8):
                        mm(pst[db[h // 4]][:, (h % 4) * 128:(h % 4 + 1) * 128], LH8[dr][:, h * 128:(h + 1) * 128], TRI, True, True,
                           reads=[("lh8", dr), "const"], writes=[PK(db[h // 4])])
                    for hh in range(2):
                        s.op("act", lambda E, dr=dr, hh=hh, db=db: E.activation(out=E8[dr][:, hh * 512:(hh + 1) * 512], in_=pst[db[hh]][:, 0:512], func=AF.Exp),
                             reads=[PK(db[hh])], writes=[("e8", dr, hh)])
                    s.op("dve", lambda E, dr=dr: E.tensor_tensor(out=M8[dr].rearrange("p (h i) -> p h i", h=8), in0=E8[dr].rearrange("p (h i) -> p h i", h=8),
                                                               in1=CBM[dr].unsqueeze(1).to_broadcast([128, 8, 128]), op=ALU.mult),
                         reads=[("e8", dr, 0), ("e8", dr, 1), ("cbm", dr)], writes=[("m8", dr)])
                    s.op("pool", lambda E, dr=dr, dcol=dcol, sub=sub, gi2=gi2: E.tensor_tensor(
                        out=XDT[dr].rearrange("p (h d) -> p h d", h=8), in0=XHG[gi2][:, sub, :].rearrange("p (h d) -> p h d", h=8),
                        in1=dcol.unsqueeze(2).to_broadcast([128, 8, 64]), op=ALU.mult), reads=[("xhg", gi2), "dta"], writes=[("xdt", dr)])
                    for h in range(8):
                        mm(pst[yb][:, h * 64:(h + 1) * 64], M8[dr][:, h * 128:(h + 1) * 128], XDT[dr][:, h * 64:(h + 1) * 64], dr == 0 and h == 0, dr == 1 and h == 7,
                           reads=[("m8", dr), ("xdt", dr)], writes=[PK(yb)])
                    ob = newbank()
                    mm(pst[ob][:, 0:512], CTG[gi2][:, sub * 128:(sub + 1) * 128], HEG[gi2][dr][sub], True, True,
                       reads=[("ctg", gi2), ("heg", gi2, dr, sub)], writes=[PK(ob)])
                    s.op("dve", lambda E, dr=dr, ob=ob, ecol=ecol: E.tensor_tensor(
                        out=TY[dr].rearrange("p (h d) -> p h d", h=8), in0=pst[ob][:, 0:512].rearrange("p (h d) -> p h d", h=8),
                        in1=ecol.unsqueeze(2).to_broadcast([128, 8, 64]), op=ALU.mult), reads=[PK(ob), ("eacs", sub)], writes=[("ty", dr)])
                s.op("pool", lambda E, sub=sub, gi2=gi2, gg=gg: E.tensor_tensor(
                    out=TY[2].rearrange("p (h d) -> p h d", h=8), in0=XHG[gi2][:, sub, :].rearrange("p (h d) -> p h d", h=8),
                    in1=DSK[:, gg * 8:(gg + 1) * 8].unsqueeze(2).to_broadcast([128, 8, 64]), op=ALU.mult), reads=[("xhg", gi2), "dsk"], writes=[("ty", 2)])
                s.op("dve", lambda E, yb=yb: E.tensor_tensor(out=YT, in0=pst[yb][:, 0:512], in1=TY[0], op=ALU.add), reads=[PK(yb), ("ty", 0)], writes=["yt"])
                s.op("pool", lambda E: E.tensor_tensor(out=YT, in0=YT, in1=TY[1], op=ALU.add), reads=["yt", ("ty", 1)], writes=["yt"])
                s.op("pool", lambda E: E.tensor_tensor(out=YT, in0=YT, in1=TY[2], op=ALU.add), reads=["yt", ("ty", 2)], writes=["yt"])
                s.op("pool", lambda E, sub=sub, gi2=gi2: E.tensor_tensor(out=YT, in0=YT, in1=ZS[gi2][:, sub, :], op=ALU.mult), reads=["yt", ("zs", gi2, sub)], writes=["yt"])
                rstd_of(YT, 0, 512, "yt")
                s.op("dve", lambda E, gi2=gi2: E.scalar_tensor_tensor(out=YN, in0=YT, scalar=RSTD[:, 0:1], in1=SSN[gi2], op0=ALU.mult, op1=ALU.mult),
                     reads=["yt", ("rstd", 0), ("ssn", gi2)], writes=["yn"])
                tb = newbank()
                for q in range(4):
                    s.op("pe", lambda E, tb=tb, q=q: E.transpose(psb[tb][:, q * 128:(q + 1) * 128], YN[:, q * 128:(q + 1) * 128], IDB), reads=["yn", "idb"], writes=[PK(tb)])
                s.op("act", lambda E, tb=tb, gg=gg, sub=sub: E.activation(out=YNT[:, gg * 4:(gg + 1) * 4, sub * 128:(sub + 1) * 128],
                                                                         in_=psb[tb][:, 0:512].rearrange("p (q c) -> p q c", q=4), func=AF.Copy),
                     reads=[PK(tb)], writes=[("ynt", gg)])
        for dc in range(16):
            wbb, kb = load_wb32(dc)
            b = newbank()
            for k in range(32):
                mm(pst[b][:, 0:T], wbb[:, k, :], YNT[:, k, :], k == 0, k == 31, reads=[kb, ("ynt", k // 4)], writes=[PK(b)])
            wblk, wk = load_wblk("win", BLK_GB + dc)
            b2 = newbank()
            for k in range(KC):
                mm(pst[b2][:, 0:T], wblk[:, k, :], g.hT[:, k, :], k == 0, k == KC - 1, reads=[wk, ("hT", k)], writes=[PK(b2)])
            gi_ = dc % 2
            s.op("act", lambda E, b2=b2, gi_=gi_, dc=dc: E.activation(out=GBUF[gi_], in_=pst[b2][:, 0:T], func=AF.Sigmoid, bias=BGC[:, 16 + dc:17 + dc]),
                 reads=[PK(b2), "cols"], writes=[("gbuf", gi_)])
            s.op("dve", lambda E, b=b, gi_=gi_, dc=dc: E.tensor_tensor(out=MRG[:, dc, :], in0=pst[b][:, 0:T], in1=GBUF[gi_], op=ALU.mult),
                 reads=[PK(b), ("gbuf", gi_)], writes=[("mrg", dc)])
        s.barrier()
        for fc in range(16):
            wblk, wk = load_wblk("win", BLK_U + fc)
            b = newbank()
            for k in range(KC):
                mm(pst[b][:, 0:T], wblk[:, k, :], g.hT[:, k, :], k == 0, k == KC - 1, reads=[wk, ("hT", k)], writes=[PK(b)])
            s.op("act", lambda E, b=b, fc=fc: E.activation(out=UT[:, fc, :], in_=pst[b][:, 0:T], func=AF.Gelu), reads=[PK(b)], writes=[("ut", fc)])
        for jg in range(4):
            vb = [4 + sub for sub in range(ns)]
            for j in range(4):
                wblk, wk = load_wblk("win", BLK_V + jg * 4 + j)
                for sub in range(ns):
                    for k in range(KC):
                        mm(pst[vb[sub]][:, j * 128:(j + 1) * 128], g.hT[:, k, sub * 128:(sub + 1) * 128], wblk[:, k, :], k == 0, k == KC - 1,
                           reads=[wk, ("hT", k)], writes=[PK(vb[sub])])
            for sub in range(ns):
                s.op("act", lambda E, sub=sub, jg=jg: E.activation(out=VF[:, sub, jg * 512:(jg + 1) * 512], in_=pst[vb[sub]][:, 0:512], func=AF.Gelu),
                     reads=[PK(vb[sub])], writes=[("vf", sub)])
        for sub in range(ns):
            for q in range(4):
                s.op("dve", lambda E, sub=sub, q=q: E.bn_stats(out=BNS[:, q * 6:(q + 1) * 6], in_=VF[:, sub, q * 512:(q + 1) * 512]), reads=[("vf", sub)], writes=["bns"])
            s.op("dve", lambda E: E.bn_aggr(out=BNA, in_=BNS.rearrange("p (q s) -> p q s", q=4)), reads=["bns"], writes=["bna"])
            s.op("act", lambda E: E.activation(out=LNS[:, 0:1], in_=BNA[:, 1:2], func=AF.Sqrt, bias=EPSC), reads=["bna", "eps"], writes=["lns"])
            s.op("dve", lambda E: E.reciprocal(out=LNS[:, 1:2], in_=LNS[:, 0:1]), reads=["lns"], writes=["lns"])
            s.op("dve", lambda E: E.scalar_tensor_tensor(out=LNS[:, 2:3], in0=BNA[:, 0:1], scalar=-1.0, in1=LNS[:, 1:2], op0=ALU.mult, op1=ALU.mult),
                 reads=["lns", "bna"], writes=["lns"])
            s.op("act", lambda E, sub=sub: E.activation(out=VNB[:, sub, :], in_=VF[:, sub, :], func=AF.Identity, scale=LNS[:, 1:2], bias=LNS[:, 2:3]),
                 reads=["lns", ("vf", sub)], writes=[("vnb", sub)])
        for fc in range(16):
            b = newbank()
            for sub in range(ns):
                mm(pst[b][:, sub * 128:(sub + 1) * 128], VNB[:, sub, fc * 128:(fc + 1) * 128], WST[:, fc // 2, :], True, True,
                   reads=[("vnb", sub), "wst"], writes=[PK(b)])
            ti = fc % 2
            s.op("dve", lambda E, b=b, ti=ti, fc=fc: E.scalar_tensor_tensor(out=TSB[ti].rearrange("p (s i) -> p s i", s=ns), in0=pst[b][:, 0:T].rearrange("p (s i) -> p s i", s=ns),
                                                                          scalar=LNWC[:, fc:fc + 1], in1=BS2[:, fc, :].unsqueeze(1).to_broadcast([128, ns, 128]), op0=ALU.mult, op1=ALU.add),
                 reads=[PK(b), "bs2", "cols"], writes=[("tsb", ti)])
            s.op("pool", lambda E, ti=ti, fc=fc: E.tensor_tensor(out=UT[:, fc, :], in0=UT[:, fc, :], in1=TSB[ti], op=ALU.mult),
                 reads=[("tsb", ti), ("ut", fc)], writes=[("ut", fc)])
        for dc in range(16):
            wblk, wk = load_wblk("wa", dc)
            b = newbank()
            for k in range(KC):
                mm(pst[b][:, 0:T], wblk[:, k, :], UT[:, k, :], k == 0, k == KC - 1, reads=[wk, ("ut", k)], writes=[PK(b)])
            wblk2, wk2 = load_wblk("win", BLK_GA + dc)
            b2 = newbank()
            for k in range(KC):
                mm(pst[b2][:, 0:T], wblk2[:, k, :], g.hT[:, k, :], k == 0, k == KC - 1, reads=[wk2, ("hT", k)], writes=[PK(b2)])
            gi_ = dc % 2
            s.op("act", lambda E, b2=b2, gi_=gi_, dc=dc: E.activation(out=GBUF[gi_], in_=pst[b2][:, 0:T], func=AF.Sigmoid, bias=BGC[:, dc:dc + 1]),
                 reads=[PK(b2), "cols"], writes=[("gbuf", gi_)])
            ti = dc % 2
            s.op("dve", lambda E, b=b, gi_=gi_, ti=ti: E.tensor_tensor(out=TSB[ti], in0=pst[b][:, 0:T], in1=GBUF[gi_], op=ALU.mult),
                 reads=[PK(b), ("gbuf", gi_)], writes=[("tsb", ti)])
            s.op("pool", lambda E, ti=ti, dc=dc: E.tensor_tensor(out=MRG[:, dc, :], in0=MRG[:, dc, :], in1=TSB[ti], op=ALU.add),
                 reads=[("tsb", ti), ("mrg", dc)], writes=[("mrg", dc)])
        dma("sp", g.X, x1_d[tok0:tok0 + T, :].rearrange("(s p) d -> p s d", p=128), reads=["x1d"], writes=[("X", sub) for sub in range(ns)])
        for dblk in range(4):
            load_gsl(1, dblk)
            ob = [4 + sub for sub in range(ns)]
            for j in range(4):
                wblk, wk = load_wblk("wo", dblk * 4 + j)
                for sub in range(ns):
                    for k in range(KC):
                        mm(pst[ob[sub]][:, j * 128:(j + 1) * 128], MRG[:, k, sub * 128:(sub + 1) * 128], wblk[:, k, :], k == 0, k == KC - 1,
                           reads=[wk, ("mrg", k)], writes=[PK(ob[sub])])
            for sub in range(ns):
                resid_update(ob[sub], sub, dblk, 1)
        dma("sp", x1_d[tok0:tok0 + T, :].rearrange("(s p) d -> p s d", p=128), g.X, reads=[("X", sub) for sub in range(ns)], writes=["x1d"])
        s.barrier()

    stop_if("B")
    AR.off = persist_small
    alloc_common(TC)
    T, ns = g.T, g.ns
    g.GTt = AR.bf(FFC * T).rearrange("p (f t) -> p f t", f=FFC)
    g.SG = [AR.bf(T) for _ in range(2)]
    NFB = AR.f32(D)
    OUTB = AR.f32(D)
    dma("sp", NFB, norm_final.partition_broadcast(128), writes=["nfb"])
    outs = []
    for tix in range(TOKH // TC):
        tok0 = tix * TC
        dma("sp", g.X, x1_d[tok0:tok0 + T, :].rearrange("(s p) d -> p s d", p=128), reads=["x1d"], writes=[("X", sub) for sub in range(ns)])
        ffn(4, 2, "f2g", "f2u", "f2d")
        for sub in range(ns):
            rstd_of(g.X[:, sub, :], sub, D, ("X", sub))
            s.op("dve", lambda E, sub=sub: E.scalar_tensor_tensor(out=OUTB, in0=g.X[:, sub, :], scalar=RSTD[:, sub:sub + 1], in1=NFB, op0=ALU.mult, op1=ALU.mult),
                 reads=[("X", sub), ("rstd", sub), "nfb"], writes=["outb"])
            outs.append(dma("sp", out_d[tok0 + sub * 128:tok0 + (sub + 1) * 128, :], OUTB, reads=["outb"], writes=["outd"]))
    s.final_wait("sp", outs)
    stats = s.emit(st)
    st.close()
    return nc, stats


FULL_CFG = dict(FF=5632, TOKH=4096, TA=256, TB=256, TC=512)


def make_in_maps(inp, cfg):
    TOKH = cfg["TOKH"]
    x = np.asarray(inp["x"], np.float32)
    ctx = np.asarray(inp["ctx"], np.float32)
    B = x.shape[0]
    f32 = lambda a: np.ascontiguousarray(np.asarray(a, np.float32))
    consts = make_consts()
    shared = {
        "w_mod": f32(inp["w_mod"][0]), "b_mod": f32(inp["b_mod"][0]),
        "norms": f32(np.concatenate([inp["norm_ffn1"][0], inp["norm_mix"][0], inp["norm_ffn2"][0]]).reshape(48, 128)),
        "norm_final": f32(inp["norm_final"]),
        "ffn1_gate": f32(inp["ffn1_gate"][0]), "ffn1_up": f32(inp["ffn1_up"][0]), "ffn1_down": f32(inp["ffn1_down"][0]),
        "ffn2_gate": f32(inp["ffn2_gate"][0]), "ffn2_up": f32(inp["ffn2_up"][0]), "ffn2_down": f32(inp["ffn2_down"][0]),
        "w_in": f32(inp["w_in"][0]), "w_a": f32(inp["w_a"][0]), "w_b": f32(inp["w_b"][0]), "w_out": f32(inp["w_out"][0]),
        "b_gate": f32(inp["b_gate"][0]).reshape(32, 128), "gmlp_ln_wb": f32(np.concatenate([inp["gmlp_ln_w"][0], inp["gmlp_ln_b"][0]]).reshape(32, 128)),
        "conv_b": f32(inp["conv_b"][0]).reshape(48, 128), "d_skip": f32(inp["d_skip"][0]), "ssm_norm": f32(inp["ssm_norm"][0]),
        "consts": consts,
    }
    win = shared["w_in"]
    ws = np.asarray(inp["gmlp_ws"][0], np.float32)
    bs = np.asarray(inp["gmlp_bs"][0], np.float32)
    cw = np.asarray(inp["conv_w"][0], np.float32)
    al = np.asarray(inp["a_log"][0], np.float32)
    db = np.asarray(inp["dt_bias"][0], np.float32)
    wdt = win[:, OFF_DT:OFF_DT + 128]
    per_s = []
    for s_ in range(2):
        if s_ == 0:
            d = {"w_dt": f32(wdt), "gmlp_wsT": f32(ws.transpose(2, 0, 1)), "gmlp_bs": f32(bs.reshape(-1)),
                 "conv_w": f32(cw.reshape(240, 128)), "a_log": f32(al.reshape(-1)), "dt_bias": f32(db.reshape(-1))}
        else:
            wsf = ws[:, ::-1, ::-1]
            d = {"w_dt": f32(np.concatenate([wdt[:, 64:128], wdt[:, 0:64]], axis=1)),
                 "gmlp_wsT": f32(wsf.transpose(2, 0, 1)), "gmlp_bs": f32(bs[:, ::-1].reshape(-1)),
                 "conv_w": f32(cw[::-1].reshape(240, 128)), "a_log": f32(al[::-1].reshape(-1)), "dt_bias": f32(db[::-1].reshape(-1))}
        per_s.append(d)
    in_maps = []
    cc = np.asarray(inp["c_ctx"], np.float32)
    for core in range(2 * B):
        b, s_ = core // 2, core % 2
        xb = x[b] if s_ == 0 else x[b, ::-1]
        cb = ctx[b] if s_ == 0 else ctx[b, ::-1]
        m = dict(shared)
        m.update(per_s[s_])
        m["x"] = f32(xb)
        m["ctx"] = f32(cb)
        m["cvec"] = f32(np.concatenate([np.asarray(inp["c"], np.float32)[b], cc]).reshape(32, 128))
        in_maps.append(m)
    return in_maps


_CACHE = {}


def run(inp, cfg):
    key = tuple(sorted(cfg.items()))
    if key not in _CACHE:
        _CACHE[key] = build(cfg)[0]
    nc = _CACHE[key]
    in_maps = make_in_maps(inp, cfg)
    res = run_bass_kernel_spmd(nc, in_maps, core_ids=list(range(len(in_maps))))
    TOKH = cfg["TOKH"]
    B = len(in_maps) // 2
    out = np.empty((B, 2 * TOKH, D), np.float32)
    for core in range(2 * B):
        b, s_ = core // 2, core % 2
        o = np.asarray(res.results[core]["out"], np.float32)
        if s_ == 0:
            out[b, 0:TOKH] = o
        else:
            out[b, TOKH:] = o[::-1]
    return out


def kernel(**inputs):
    return run(inputs, FULL_CFG)
```

```python
import numpy as np
from contextlib import ExitStack
import concourse.bass as bass
import concourse.mybir as mybir
from concourse.bass_utils import run_bass_kernel_spmd

F32 = mybir.dt.float32
BF16 = mybir.dt.bfloat16
AF = mybir.ActivationFunctionType
ALU = mybir.AluOpType
P = 128
D = 2048
KC = 16
CTX = 256
EPS = 1e-6
OFF_U, OFF_V, OFF_Z, OFF_XB, OFF_C, OFF_DT, OFF_GATE, IN_W = 0, 2048, 4096, 8192, 13312, 14336, 14464, 18560
BLK_U, BLK_V, BLK_Z, BLK_XB, BLK_C, BLK_GA, BLK_GB = 0, 16, 32, 64, 104, 113, 129

EPOCH = 8000
NSLOT = 12
DEBUG = False
NAMES = {}


class Op:
    __slots__ = ("eng", "fn", "deps", "signal", "is_dma", "slot", "dval", "sigcnt", "line")

    def __init__(self, eng, fn, is_dma):
        self.eng = eng
        self.fn = fn
        self.deps = []
        self.signal = False
        self.is_dma = is_dma
        self.slot = None
        self.dval = None
        self.sigcnt = None
        self.line = None


class _Rec:
    def __getattr__(self, name):
        def f(*a, **k):
            self.call = (name, a, k)
            return None
        return f


class Sched:
    ENGS = ("pe", "act", "dve", "pool", "sp")

    def __init__(self, nc):
        self.nc = nc
        self.ops = {e: [] for e in self.ENGS}
        self.last_w = {}
        self.readers = {}
        self.dma_cnt = {e: 0 for e in self.ENGS}
        self.dma_last = {e: [None] * NSLOT for e in self.ENGS}

    def op(self, eng, fn, reads=(), writes=(), dma=False):
        rec = _Rec()
        fn(rec)
        _n, _a, _k = rec.call
        o = Op(eng, (lambda E, _n=_n, _a=_a, _k=_k: getattr(E, _n)(*_a, **_k)), dma)
        if DEBUG:
            import sys as _sys
            f = _sys._getframe(1)
            ln = []
            while f is not None and len(ln) < 4:
                ln.append(f.f_lineno)
                f = f.f_back
            o.line = ln
        deps = []
        for k in reads:
            w = self.last_w.get(k)
            if w is not None:
                deps.append(w)
            if isinstance(k, tuple) and k[0] == "ps":
                rl = self.readers.get(k)
                if rl:
                    deps.extend(v for e_, v in rl.items() if e_ != eng)
        for k in writes:
            w = self.last_w.get(k)
            if w is not None:
                deps.append(w)
            rl = self.readers.get(k)
            if rl:
                deps.extend(rl.values())
        if dma:
            c = self.dma_cnt[eng]
            o.slot = c % NSLOT
            o.dval = 16 * (c // NSLOT + 1)
            prev = self.dma_last[eng][o.slot]
            if prev is not None:
                deps.append(prev)
            self.dma_last[eng][o.slot] = o
            self.dma_cnt[eng] = c + 1
        seen = set()
        for d in deps:
            if d is o or id(d) in seen:
                continue
            seen.add(id(d))
            if (not d.is_dma) and d.eng == eng and eng == "pe":
                continue
            o.deps.append(d)
            if not d.is_dma:
                d.signal = True
        for k in writes:
            self.last_w[k] = o
            self.readers[k] = {}
        for k in reads:
            rl = self.readers.setdefault(k, {})
            rl[("dma", id(o)) if dma else eng] = o
        self.ops[eng].append(o)
        return o

    def barrier(self):
        lasts = []
        for e in self.ENGS:
            for o in reversed(self.ops[e]):
                if not o.is_dma and o.fn is not None:
                    lasts.append(o)
                    break
            if e == "pool":
                continue
            for o in self.dma_last[e]:
                if o is not None:
                    lasts.append(o)
        for e in self.ENGS:
            o = Op(e, None, False)
            for d in lasts:
                if d.eng == e and not d.is_dma:
                    continue
                o.deps.append(d)
                if not d.is_dma:
                    d.signal = True
            self.ops[e].append(o)
        self.last_w = {k: v for k, v in self.last_w.items() if isinstance(k, tuple) and k[0] == "W"}
        self.readers = {}

    def final_wait(self, eng, ops):
        o = Op(eng, None, False)
        for d in ops:
            o.deps.append(d)
            if not d.is_dma:
                d.signal = True
        self.ops[eng].append(o)

    def emit(self, stack):
        nc = self.nc
        nsig = {}
        for e in self.ENGS:
            c = 0
            for o in self.ops[e]:
                if o.signal and not o.is_dma:
                    c += 1
                    o.sigcnt = c
            nsig[e] = c
        esem = {}
        for e in self.ENGS:
            ne = (nsig[e] + EPOCH - 1) // EPOCH
            esem[e] = [stack.enter_context(nc.semaphore(f"s_{e}_{i}")) for i in range(ne)]
        dsem = {}
        for e in self.ENGS:
            n = min(self.dma_cnt[e], NSLOT)
            dsem[e] = [stack.enter_context(nc.semaphore(f"d_{e}_{i}")) for i in range(n)]
        engobj = {"pe": "tensor", "act": "scalar", "dve": "vector", "pool": "gpsimd", "sp": "sync"}
        stats = {}

        def run(ename, E):
            waited = {}
            maxep = {}
            nw = 0
            for o in self.ops[ename]:
                for d in o.deps:
                    if d.is_dma:
                        sem = dsem[d.eng][d.slot]
                        val = d.dval
                        key = ("d", d.eng, d.slot)
                    else:
                        ep = (d.sigcnt - 1) // EPOCH
                        val = (d.sigcnt - 1) % EPOCH + 1
                        sem = esem[d.eng][ep]
                        key = ("e", d.eng, ep)
                        if maxep.get(d.eng, -1) > ep:
                            continue
                        maxep[d.eng] = ep
                    if waited.get(key, 0) >= val:
                        continue
                    waited[key] = val
                    E.wait_ge(sem, val)
                    nw += 1
                if o.fn is None:
                    continue
                ins = o.fn(E)
                if DEBUG:
                    try:
                        NAMES[ins.ins.name] = o.line
                    except Exception:
                        pass
                if o.is_dma:
                    ins.then_inc(dsem[ename][o.slot], 16)
                elif o.signal:
                    ep = (o.sigcnt - 1) // EPOCH
                    ins.then_inc(esem[ename][ep], 1)
            stats[ename] = (len(self.ops[ename]), nw)

        with nc.Block() as block:
            for ename in self.ENGS:
                getattr(block, engobj[ename])(lambda E, ename=ename: run(ename, E))
        return stats


class Arena:
    def __init__(self, ap, nwords):
        self.ap = ap
        self.n = nwords
        self.off = 0

    def f32(self, n):
        o = self.off
        self.off += n
        assert self.off <= self.n, ("SBUF arena overflow", self.off, self.n)
        return self.ap[:, o:o + n]

    def bf(self, n):
        w = (n + 1) // 2
        o = self.off
        self.off += w
        assert self.off <= self.n, ("SBUF arena overflow", self.off, self.n)
        return self.ap[:, o:o + w].bitcast(BF16)[:, 0:n]


NCONST = 6 * 128


def make_consts():
    i = np.arange(128)
    p, q = i[:, None], i[None, :]
    mats = [p == q, p <= q, p >= q, p > q, p < q, np.ones((128, 128), bool)]
    return np.concatenate([m.astype(np.float32) for m in mats], axis=1)


class _Stop(Exception):
    pass


def build(cfg):
    nc_holder = {}
    try:
        return _build(cfg, nc_holder)
    except _Stop:
        s, st, nc = nc_holder["s"], nc_holder["st"], nc_holder["nc"]
        s.barrier()
        stats = s.emit(st)
        st.close()
        return nc, stats


def _build(cfg, nc_holder):
    FF = cfg["FF"]
    FFC = FF // 128
    TOKH = cfg["TOKH"]
    TA = cfg["TA"]
    TB = cfg["TB"]
    TC = cfg["TC"]
    SEGW = 64
    NCH = TOKH // 128
    NQ = FFC // 11
    assert FFC % 11 == 0 and TOKH % TA == 0 and TOKH % TB == 0 and TOKH % TC == 0
    nc = bass.Bass("TRN2", target_bir_lowering=False)

    def din(name, shape):
        return nc.dram_tensor(name, list(shape), F32, kind="ExternalInput").ap()

    x_in = din("x", [2 * TOKH, D])
    ctx_in = din("ctx", [CTX, D])
    cvec = din("cvec", [32, 128])
    w_mod = din("w_mod", [D, 9 * D])
    b_mod = din("b_mod", [9 * D])
    norms = din("norms", [48, 128])
    norm_final = din("norm_final", [D])
    Wsrc = {
        "f1g": din("ffn1_gate", [D, FF]), "f1u": din("ffn1_up", [D, FF]), "f1d": din("ffn1_down", [FF, D]),
        "f2g": din("ffn2_gate", [D, FF]), "f2u": din("ffn2_up", [D, FF]), "f2d": din("ffn2_down", [FF, D]),
        "win": din("w_in", [D, IN_W]), "wdt": din("w_dt", [D, 128]),
        "wa": din("w_a", [D, D]), "wb": din("w_b", [2 * D, D]), "wo": din("w_out", [D, D]),
    }
    b_gate = din("b_gate", [32, 128])
    ln_wb = din("gmlp_ln_wb", [32, 128])
    wsT_in = din("gmlp_wsT", [128, 8, 128])
    bs_in = din("gmlp_bs", [8 * 128])
    conv_w = din("conv_w", [240, 128])
    conv_b = din("conv_b", [48, 128])
    a_log = din("a_log", [128])
    dt_bias = din("dt_bias", [128])
    d_skip = din("d_skip", [64])
    ssm_norm = din("ssm_norm", [2 * D])
    consts_in = din("consts", [128, NCONST])
    out_d = nc.dram_tensor("out", [TOKH, D], F32, kind="ExternalOutput").ap()

    def dscr(name, shape, dt):
        return nc.dram_tensor(name, list(shape), dt).ap()

    Wbf = {}
    for nm in ("f1g", "f1u", "f2g", "f2u"):
        Wbf[nm] = dscr("bf_" + nm, [FFC, 128, KC, 128], BF16)
    for nm in ("f1d", "f2d"):
        Wbf[nm] = dscr("bf_" + nm, [D // 512, NQ, 128, 11, 512], BF16)
    Wbf["win"] = dscr("bf_win", [IN_W // 128, 128, KC, 128], BF16)
    Wbf["wdt"] = dscr("bf_wdt", [1, 128, KC, 128], BF16)
    Wbf["wa"] = dscr("bf_wa", [16, 128, KC, 128], BF16)
    Wbf["wo"] = dscr("bf_wo", [16, 128, KC, 128], BF16)
    Wbf["wb"] = dscr("bf_wb", [16, 128, 32, 128], BF16)
    x1_d = dscr("x1_d", [TOKH, D], F32)
    hT_d = dscr("hT_d", [128, KC, TOKH], BF16)
    xh_d = dscr("xh_d", [TOKH, 4096], BF16)
    BT_d = dscr("BT_d", [8, 128, TOKH], BF16)
    dtA_d = dscr("dtA_d", [TOKH, 256], F32)
    Sb_d = dscr("Sb_d", [NCH, 128, 4096], F32)
    decb_d = dscr("decb_d", [NCH, 128, 64], F32)
    Hent_d = dscr("Hent_d", [2, NCH, 128, 4096], BF16)
    gates_d = dscr("gates_d", [4, 128, D], F32)
    Hsave_d = dscr("Hsave_d", [128, 4096], F32)

    st = ExitStack()
    NW = 52000
    arena_t = st.enter_context(nc.sbuf_tensor("arena", [128, NW], F32))
    AR = Arena(arena_t, NW)
    pst = [st.enter_context(nc.psum_tensor(f"ps{i}", [128, 512], F32)) for i in range(8)]
    psb = [t.bitcast(BF16) for t in pst]
    s = Sched(nc)
    nc_holder.update(s=s, st=st, nc=nc)
    STOP = cfg.get("stop")

    def stop_if(name):
        if STOP == name:
            raise _Stop()
    bankctr = [0]

    def newbank(lo=0, hi=4):
        b = lo + bankctr[0] % (hi - lo)
        bankctr[0] += 1
        return b

    def PK(b):
        return ("ps", b)

    def mm(out, lhsT, rhs, start, stop, reads, writes):
        s.op("pe", lambda E: E.matmul(out, lhsT=lhsT, rhs=rhs, start=start, stop=stop), reads, writes)

    def dma(eng, out, in_, reads=(), writes=()):
        return s.op(eng, lambda E: E.dma_start(out=out, in_=in_), reads, writes, dma=True)

    CONST = AR.f32(NCONST)
    IDF, LE, GE, GT_, LT_, ONES = [CONST[:, i * 128:(i + 1) * 128] for i in range(6)]
    IDB = AR.bf(128)
    COLS = AR.f32(528)
    MCOL = AR.f32(192)
    AB = AR.f32(160)
    EPSC = AR.f32(1)
    SSQ = AR.f32(4)
    RMS = AR.f32(4)
    RSTD = AR.f32(4)
    DTB = AR.f32(128)
    ANEG = AR.f32(128)
    DSK = AR.f32(64)
    persist_small = AR.off
    HB_ = AR.f32(4096)
    persist_mark = AR.off
    C_CV, C_N, C_BM, C_CW, C_CB, C_BG, C_LN = 0, 32, 80, 176, 416, 464, 496

    dma("sp", CONST, consts_in, writes=["const"])
    s.op("dve", lambda E: E.tensor_copy(out=IDB, in_=IDF), reads=["const"], writes=["idb"])
    s.op("pool", lambda E: E.memset(EPSC, EPS), writes=["eps"])
    dma("sp", DTB, dt_bias.partition_broadcast(128), writes=["dtb"])
    dma("sp", ANEG, a_log.partition_broadcast(128), writes=["aneg"])
    dma("sp", DSK, d_skip.partition_broadcast(128), writes=["dsk"])
    s.op("act", lambda E: E.activation(out=ANEG, in_=ANEG, func=AF.Exp), reads=["aneg"], writes=["aneg"])
    s.op("dve", lambda E: E.tensor_scalar_mul(out=ANEG, in0=ANEG, scalar1=-1.0), reads=["aneg"], writes=["aneg"])

    def conv_k(nm, nblk, kc=KC):
        src = Wsrc[nm].rearrange("(k p) (b c) -> b p k c", p=128, c=128)
        for b0 in range(nblk):
            dma("pool", Wbf[nm][b0], src[b0], writes=[("W", nm, b0)])

    def conv_d(nm):
        src = Wsrc[nm].rearrange("(q f p) (d c) -> d q p f c", p=128, f=11, c=512)
        for d_ in range(D // 512):
            for q_ in range(NQ):
                dma("pool", Wbf[nm][d_, q_], src[d_, q_], writes=[("W", nm, d_, q_)])

    def wkey(nm, b):
        return ("W", nm, b)

    ROWS = AR.f32(5 * 128)
    rowsrc = [(cvec, 32), (norms, 48), None, (conv_w, 240), (conv_b, 48), (b_gate, 32)]
    bm2 = b_mod.rearrange("(j c p) -> j c p", c=16, p=128)
    pieces = [(cvec, 0, 32), (norms, 0, 48)]
    for j in (0, 1, 3, 4, 6, 7):
        pieces.append((bm2[j], 0, 16))
    pieces += [(conv_w, 0, 240), (conv_b, 0, 48), (b_gate, 0, 32), (ln_wb, 0, 32)]
    r = 0
    for (ap, r0, n) in pieces:
        done = 0
        while done < n:
            t = r // 128
            ro = r % 128
            m = min(n - done, 128 - ro)
            dma("sp", ROWS[ro:ro + m, t * 128:(t + 1) * 128], ap[done:done + m, :], writes=[("rows", t)])
            done += m
            r += m
    assert r == 528
    for t in range(5):
        n = min(128, 528 - t * 128)
        b = newbank()
        s.op("pe", lambda E, t=t, n=n, b=b: E.transpose(pst[b][:, 0:n], ROWS[0:n, t * 128:(t + 1) * 128], IDF[0:n, 0:n]),
             reads=[("rows", t), "const"], writes=[PK(b)])
        s.op("dve", lambda E, t=t, n=n, b=b: E.tensor_copy(out=COLS[:, t * 128:t * 128 + n], in_=pst[b][:, 0:n]),
             reads=[PK(b)], writes=["cols"])
    AR.off = persist_mark

    m0 = AR.off
    SC = AR.bf(32)
    SCB = AR.bf(2 * KC * 128)
    s.op("act", lambda E: E.activation(out=SC, in_=COLS[:, C_CV:C_CV + 32], func=AF.Silu), reads=["cols"], writes=["sc"])
    SCBv = SCB.rearrange("p (t k m) -> p t k m", t=2, k=KC)
    for t in range(2):
        s.op("dve", lambda E, t=t: E.tensor_copy(out=SCBv[:, t], in_=SC[:, t * 16:(t + 1) * 16].unsqueeze(2).to_broadcast([128, KC, 128])),
             reads=["sc"], writes=[("scb", t)])
    SCv = SC.rearrange("p (t k) -> p t k", t=2)
    WM = [AR.bf(KC * 128).rearrange("p (k c) -> p k c", k=KC) for _ in range(4)]
    wmsrc = w_mod.rearrange("(k p) (b c) -> b p k c", p=128, c=128)
    wmc = [0]

    def load_wm(blk):
        i = wmc[0] % 4
        wmc[0] += 1
        dma("pool", WM[i], wmsrc[blk], writes=[("wm", i)])
        return WM[i], ("wm", i)

    mb = newbank()
    for ji, j in enumerate((0, 1, 3, 4, 6, 7)):
        for c in range(16):
            wblk, wk = load_wm(j * 16 + c)
            idx = ji * 16 + c
            for k in range(KC):
                mm(pst[mb][:, idx * 2:idx * 2 + 2], wblk[:, k, :], SCv[:, :, k], k == 0, k == KC - 1,
                   reads=[wk, "sc"], writes=[PK(mb)])
    bmcol = COLS[:, C_BM:C_BM + 96]
    s.op("dve", lambda E: E.tensor_tensor(out=MCOL.rearrange("p (i t) -> p i t", t=2), in0=pst[mb][:, 0:192].rearrange("p (i t) -> p i t", t=2),
                                          in1=bmcol.unsqueeze(2).to_broadcast([128, 96, 2]), op=ALU.add),
         reads=[PK(mb), "cols"], writes=["mcol"])
    MC = MCOL.rearrange("p (j c t) -> p j c t", j=6, c=16)
    ABv = AB.rearrange("p (i c) -> p i c", c=16)
    NRM = COLS[:, C_N:C_N + 48].rearrange("p (i c) -> p i c", c=16)

    def mk_ab(ia, nidx, jscale, jshift, t):
        s.op("dve", lambda E: E.tensor_scalar_add(out=ABv[:, ia], in0=MC[:, jscale, :, t], scalar1=1.0), reads=["mcol"], writes=["ab"])
        s.op("dve", lambda E: E.tensor_tensor(out=ABv[:, ia], in0=ABv[:, ia], in1=NRM[:, nidx], op=ALU.mult), reads=["ab", "cols"], writes=["ab"])
        s.op("dve", lambda E: E.tensor_copy(out=ABv[:, ia + 1], in_=MC[:, jshift, :, t]), reads=["mcol"], writes=["ab"])

    mk_ab(0, 0, 1, 0, 0)
    mk_ab(2, 1, 3, 2, 0)
    mk_ab(4, 2, 5, 4, 0)
    mk_ab(6, 0, 1, 0, 1)
    mk_ab(8, 1, 3, 2, 1)
    BMB = [AR.f32(128) for _ in range(2)]
    GST = [AR.f32(128) for _ in range(2)]
    gc = 0
    for (j, t, gi, scl) in ((2, 0, 0, 0.5), (5, 0, 1, 1.0), (8, 0, 2, 0.5), (2, 1, 3, 0.5)):
        for c in range(16):
            wblk, wk = load_wm(j * 16 + c)
            b = newbank()
            for k in range(KC):
                mm(pst[b][:, 0:128], SCBv[:, t, k, :], wblk[:, k, :], k == 0, k == KC - 1, reads=[wk, ("scb", t)], writes=[PK(b)])
            i = gc % 2
            gc += 1
            dma("sp", BMB[i], b_mod[j * D + c * 128:j * D + (c + 1) * 128].partition_broadcast(128), writes=[("bmb", i)])
            s.op("dve", lambda E, i=i, b=b: E.tensor_tensor(out=GST[i], in0=pst[b][:, 0:128], in1=BMB[i], op=ALU.add),
                 reads=[PK(b), ("bmb", i)], writes=[("gst", i)])
            s.op("act", lambda E, i=i, scl=scl: E.activation(out=GST[i], in_=GST[i], func=AF.Copy, scale=scl), reads=[("gst", i)], writes=[("gst", i)])
            dma("sp", gates_d[gi, :, c * 128:(c + 1) * 128], GST[i], reads=[("gst", i)], writes=[("gates", gi)])

    conv_k("f1g", FFC)
    conv_k("f1u", FFC)
    conv_d("f1d")
    conv_k("win", IN_W // 128)
    conv_k("wdt", 1)
    conv_k("wa", 16)
    conv_k("wo", 16)
    srcwb = Wsrc["wb"].rearrange("(k p) (b c) -> b p k c", p=128, c=128)
    for b0 in range(16):
        dma("pool", Wbf["wb"][b0], srcwb[b0], writes=[("W", "wb", b0)])
    conv_k("f2g", FFC)
    conv_k("f2u", FFC)
    conv_d("f2d")
    s.barrier()
    stop_if("setup")
    AR.off = persist_mark

    class G:
        pass

    g = G()

    def alloc_common(T, wdn=11 * 512):
        ns = T // 128
        g.T = T
        g.ns = ns
        g.X = AR.f32(ns * D).rearrange("p (s d) -> p s d", s=ns)
        g.XN = AR.bf(D)
        g.hT = AR.bf(KC * T).rearrange("p (k t) -> p k t", k=KC)
        g.WP = [AR.bf(KC * 128).rearrange("p (k c) -> p k c", k=KC) for _ in range(6)]
        g.WD = [AR.bf(wdn) for _ in range(2)]
        g.wpc = 0
        g.wdc = 0
        g.TMP = [AR.f32(512) for _ in range(2)]
        g.tmpc = 0
        g.GSL = [AR.f32(512) for _ in range(2)]
        g.gslc = 0

    def load_wblk(nm, b):
        i = g.wpc % len(g.WP)
        g.wpc += 1
        dma("sp", g.WP[i], Wbf[nm][b], reads=[wkey(nm, b)], writes=[("wp", i)])
        return g.WP[i], ("wp", i)

    def load_wd(nm, d_, q):
        i = g.wdc % 2
        g.wdc += 1
        ap = g.WD[i].rearrange("p (f c) -> p f c", f=11)
        dma("sp", ap, Wbf[nm][d_, q], reads=[("W", nm, d_, q)], writes=[("wd", i)])
        return ap, ("wd", i)

    def load_wb32(b):
        i = g.wdc % 2
        g.wdc += 1
        ap = g.WD[i][:, 0:32 * 128].rearrange("p (k c) -> p k c", k=32)
        dma("sp", ap, Wbf["wb"][b], reads=[("W", "wb", b)], writes=[("wd", i)])
        return ap, ("wd", i)

    def rstd_of(src_ap, sub, n, xkey):
        s.op("pool", lambda E: E.memset(SSQ[:, sub:sub + 1], 0.0), writes=[("ssq", sub)])
        s.op("act", lambda E: E.activation(out=g.XN[:, 0:n], in_=src_ap, func=AF.Square, accum_out=SSQ[:, sub:sub + 1]),
             reads=[xkey], writes=["xn", ("ssq", sub)])
        s.op("act", lambda E: E.activation(out=RMS[:, sub:sub + 1], in_=SSQ[:, sub:sub + 1], func=AF.Sqrt, scale=1.0 / n, bias=EPSC),
             reads=[("ssq", sub), "eps"], writes=[("rms", sub)])
        s.op("dve", lambda E: E.reciprocal(out=RSTD[:, sub:sub + 1], in_=RMS[:, sub:sub + 1]), reads=[("rms", sub)], writes=[("rstd", sub)])

    def norm_to_hT(ia):
        Acol, Bcol = ABv[:, ia], ABv[:, ia + 1]
        ev = 0
        for sub in range(g.ns):
            rstd_of(g.X[:, sub, :], sub, D, ("X", sub))
            s.op("dve", lambda E, sub=sub: E.tensor_scalar_mul(out=g.XN, in0=g.X[:, sub, :], scalar1=RSTD[:, sub:sub + 1]),
                 reads=[("X", sub), ("rstd", sub)], writes=["xn"])
            for half in range(2):
                b = newbank()
                for c8 in range(8):
                    c = half * 8 + c8
                    s.op("pe", lambda E, b=b, c8=c8, c=c: E.transpose(psb[b][:, c8 * 128:(c8 + 1) * 128], g.XN[:, c * 128:(c + 1) * 128], IDB),
                         reads=["xn", "idb"], writes=[PK(b)])
                for c8 in range(8):
                    c = half * 8 + c8
                    o_ap = g.hT[:, c, sub * 128:(sub + 1) * 128]
                    i_ap = psb[b][:, c8 * 128:(c8 + 1) * 128]
                    if half == 0:
                        s.op("act", lambda E, o_ap=o_ap, i_ap=i_ap, c=c: E.activation(out=o_ap, in_=i_ap, func=AF.Identity, scale=Acol[:, c:c + 1], bias=Bcol[:, c:c + 1]),
                             reads=[PK(b), "ab"], writes=[("hT", c)])
                    else:
                        s.op("dve", lambda E, o_ap=o_ap, i_ap=i_ap, c=c: E.tensor_scalar(out=o_ap, in0=i_ap, scalar1=Acol[:, c:c + 1], scalar2=Bcol[:, c:c + 1], op0=ALU.mult, op1=ALU.add),
                             reads=[PK(b), "ab"], writes=[("hT", c)])
                    ev += 1

    def resid_update(bank, sub, dblk, gi):
        i = g.tmpc % 2
        g.tmpc += 1
        tmp = g.TMP[i]
        s.op("dve", lambda E: E.tensor_tensor(out=tmp, in0=pst[bank][:, 0:512], in1=g.gsl, op=ALU.mult),
             reads=[PK(bank), g.gslk], writes=[("tmp", i)])
        xs = g.X[:, sub, dblk * 512:(dblk + 1) * 512]
        s.op("pool", lambda E: E.tensor_tensor(out=xs, in0=xs, in1=tmp, op=ALU.add), reads=[("tmp", i), ("X", sub)], writes=[("X", sub)])

    def load_gsl(gi, dblk):
        i = g.gslc % 2
        g.gslc += 1
        dma("sp", g.GSL[i], gates_d[gi, :, dblk * 512:(dblk + 1) * 512], reads=[("gates", gi)], writes=[("gsl", i)])
        g.gsl = g.GSL[i]
        g.gslk = ("gsl", i)

    def ffn(ia, gi, wg, wu, wd):
        T, ns = g.T, g.ns
        norm_to_hT(ia)
        stop_if("f_norm")
        for f in range(FFC):
            wgb, kg = load_wblk(wg, f)
            wub, ku = load_wblk(wu, f)
            pg = newbank()
            pu = newbank()
            for k in range(KC):
                mm(pst[pg][:, 0:T], wgb[:, k, :], g.hT[:, k, :], k == 0, k == KC - 1, reads=[kg, ("hT", k)], writes=[PK(pg)])
            for k in range(KC):
                mm(pst[pu][:, 0:T], wub[:, k, :], g.hT[:, k, :], k == 0, k == KC - 1, reads=[ku, ("hT", k)], writes=[PK(pu)])
            sg = g.SG[f % 2]
            s.op("act", lambda E, sg=sg, pg=pg: E.activation(out=sg[:, 0:T], in_=pst[pg][:, 0:T], func=AF.Silu), reads=[PK(pg)], writes=[("sg", f % 2)])
            s.op("dve", lambda E, sg=sg, pu=pu, f=f: E.tensor_tensor(out=g.GTt[:, f, :], in0=pst[pu][:, 0:T], in1=sg[:, 0:T], op=ALU.mult),
                 reads=[PK(pu), ("sg", f % 2)], writes=[("gT", f)] + (["alias_gt"] if f == 0 else []))
        stop_if("f_gu")
        for dblk in range(D // 512):
            load_gsl(gi, dblk)
            for q in range(NQ):
                wdb, kd = load_wd(wd, dblk, q)
                for sub in range(ns):
                    bank = 4 + sub
                    for fi in range(11):
                        f = q * 11 + fi
                        mm(pst[bank][:, 0:512], g.GTt[:, f, sub * 128:(sub + 1) * 128], wdb[:, fi, :], q == 0 and fi == 0, q == NQ - 1 and fi == 10,
                           reads=[kd, ("gT", f)], writes=[PK(bank)])
            for sub in range(ns):
                resid_update(4 + sub, sub, dblk, gi)

    def conv_chunk(bank, ch, nseg, segw, T, post, postkey, cengine):
        i = g.prec % 2
        g.prec += 1
        pre = g.PRE[i][:, 0:nseg * (segw + 4)].rearrange("p (s w) -> p s w", s=nseg)
        ca = g.CA[i].rearrange("p (s w) -> p s w", s=nseg)
        s.op("act", lambda E: E.activation(out=pre[:, :, 2:2 + segw], in_=pst[bank][:, 0:T].rearrange("p (s w) -> p s w", s=nseg), func=AF.Copy),
             reads=[PK(bank)], writes=[("pre", i)])
        E_ = cengine
        s.op(E_, lambda E: E.tensor_scalar_mul(out=ca, in0=pre[:, :, 0:segw], scalar1=COLS[:, C_CW + ch:C_CW + ch + 1]),
             reads=[("pre", i), "cols"], writes=[("ca", i)])
        for t in range(1, 5):
            cw = COLS[:, C_CW + t * 48 + ch:C_CW + t * 48 + ch + 1]
            s.op("dve", lambda E, t=t, cw=cw: E.scalar_tensor_tensor(out=ca, in0=pre[:, :, t:t + segw], scalar=cw, in1=ca, op0=ALU.mult, op1=ALU.add),
                 reads=[("pre", i), ("ca", i), "cols"], writes=[("ca", i)])
        s.op("act", lambda E: E.activation(out=post, in_=g.CA[i][:, 0:T], func=AF.Silu, bias=COLS[:, C_CB + ch:C_CB + ch + 1]),
             reads=[("ca", i), "cols"], writes=[postkey])

    def phaseA_alloc(T, need_h2):
        alloc_common(T)
        ns = g.ns
        o_ = AR.off
        flat = AR.bf(max(FFC * T, ns * 5120, 8192))
        g.GTraw = arena_t[:, o_:o_ + 4096]
        g.GTt = flat[:, 0:FFC * T].rearrange("p (f t) -> p f t", f=FFC)
        g.XHT = flat[:, 0:ns * 5120].rearrange("p (s c) -> p s c", s=ns)
        g.SG = [AR.bf(T) for _ in range(2)]
        g.PRE = [AR.bf(T + 64) for _ in range(2)]
        g.CA = [AR.f32(T) for _ in range(2)]
        g.POST = [AR.bf(T) for _ in range(3)]
        g.prec = 0
        g.postc = 0
        g.DT = AR.f32(ns * 128).rearrange("p (s c) -> p s c", s=ns)
        g.AA = AR.f32(ns * 128).rearrange("p (s c) -> p s c", s=ns)
        g.WGT = AR.f32(128)
        g.DEC = AR.f32(128)
        g.WX = [AR.bf(512) for _ in range(2)]
        g.wxc = 0
        g.SST = [AR.f32(512) for _ in range(2)]
        g.sstc = 0
        g.HS = [AR.bf(512) for _ in range(2)]
        g.hsc = 0
        g.H2 = AR.f32(4096) if need_h2 else None

    def zero_pads(nseg, segw):
        for i in range(2):
            s.op("pool", lambda E, i=i: E.memset(g.PRE[i], 0.0), writes=[("pre", i)])

    def phaseA_tile(src_ap, tok0, mode, gi, iaF, iaM, nseg, segw, own_tok0=None, sub_order=None):
        T, ns = g.T, g.ns
        dma("sp", g.X, src_ap[tok0:tok0 + T, :].rearrange("(s p) d -> p s d", p=128), writes=[("X", sub) for sub in range(ns)])
        stop_if("t_load")
        ffn(iaF, gi, "f1g", "f1u", "f1d")
        stop_if("t_ffn")
        if mode == "own":
            dma("sp", x1_d[own_tok0:own_tok0 + T, :].rearrange("(s p) d -> p s d", p=128), g.X, reads=[("X", sub) for sub in range(ns)], writes=["x1d"])
        norm_to_hT(iaM)
        if mode == "own":
            dma("sp", hT_d[:, :, own_tok0:own_tok0 + T], g.hT, reads=[("hT", c) for c in range(KC)], writes=["hTd"])
        stop_if("t_norm")
        for ch in range(40):
            wblk, wk = load_wblk("win", BLK_XB + ch)
            b = newbank()
            for k in range(KC):
                mm(pst[b][:, 0:T], wblk[:, k, :], g.hT[:, k, :], k == 0, k == KC - 1, reads=[wk, ("hT", k)], writes=[PK(b)])
            pi = g.postc % 3
            g.postc += 1
            post = g.POST[pi]
            conv_chunk(b, ch, nseg, segw, T, post, ("post", pi), "pool" if ch % 2 else "dve")
            tb = newbank()
            for sub in range(ns):
                s.op("pe", lambda E, tb=tb, sub=sub, post=post: E.transpose(psb[tb][:, sub * 128:(sub + 1) * 128], post[:, sub * 128:(sub + 1) * 128], IDB),
                     reads=[("post", pi), "idb"], writes=[PK(tb)])
            s.op("act", lambda E, tb=tb, ch=ch: E.activation(out=g.XHT[:, :, ch * 128:(ch + 1) * 128], in_=psb[tb][:, 0:T].rearrange("p (s c) -> p s c", s=ns), func=AF.Copy),
                 reads=[PK(tb)], writes=[("xht", ch)])
            if mode == "own" and ch >= 32:
                dma("sp", BT_d[ch - 32, :, own_tok0:own_tok0 + T], post, reads=[("post", pi)], writes=["BTd"])
        if mode == "own":
            dma("sp", xh_d[own_tok0:own_tok0 + T, :].rearrange("(s p) c -> p s c", p=128), g.XHT[:, :, 0:4096],
                reads=[("xht", ch) for ch in range(32)] + ["alias_gt"], writes=["xhd"])
        stop_if("t_xb")
        wblk, wk = load_wblk("wdt", 0)
        for sub in range(ns):
            b = newbank()
            for k in range(KC):
                mm(pst[b][:, 0:128], g.hT[:, k, sub * 128:(sub + 1) * 128], wblk[:, k, :], k == 0, k == KC - 1, reads=[wk, ("hT", k)], writes=[PK(b)])
            s.op("dve", lambda E, b=b, sub=sub: E.tensor_tensor(out=g.DT[:, sub, :], in0=pst[b][:, 0:128], in1=DTB, op=ALU.add), reads=[PK(b), "dtb"], writes=[("dt", sub)])
            s.op("act", lambda E, sub=sub: E.activation(out=g.DT[:, sub, :], in_=g.DT[:, sub, :], func=AF.Exp), reads=[("dt", sub)], writes=[("dt", sub)])
            s.op("act", lambda E, sub=sub: E.activation(out=g.DT[:, sub, :], in_=g.DT[:, sub, :], func=AF.Ln, bias=1.0), reads=[("dt", sub)], writes=[("dt", sub)])
            s.op("dve", lambda E, sub=sub: E.tensor_tensor(out=g.AA[:, sub, :], in0=g.DT[:, sub, :], in1=ANEG, op=ALU.mult), reads=[("dt", sub), "aneg"], writes=[("aa", sub)])
        if mode == "own":
            dv = dtA_d[own_tok0:own_tok0 + T, :].rearrange("(s p) c -> p s c", p=128)
            dma("sp", dv[:, :, 0:128], g.DT, reads=[("dt", sub) for sub in range(ns)], writes=["dtAd"])
            dma("sp", dv[:, :, 128:256], g.AA, reads=[("aa", sub) for sub in range(ns)], writes=["dtAd2"])
        stop_if("t_dt")
        passes = {"ctx": [(0, list(range(ns)), "chainF"), (1, list(range(ns))[::-1], "chainB")],
                  "other": [(1, list(range(ns))[::-1], "chainB")],
                  "own": [(0, list(range(ns)), "chainF"), (1, list(range(ns)), "store")]}[mode]
        for (dr, subs, act) in passes:
            Hbuf = HB_ if not (mode == "ctx" and dr == 0) else g.H2
            hkey = "HB" if Hbuf is HB_ else "H2"
            for sub in subs:
                b = newbank()
                lhs = GT_ if dr == 0 else LT_
                mm(pst[b][:, 0:64], lhs, g.AA[:, sub, dr * 64:(dr + 1) * 64], True, True, reads=["const", ("aa", sub)], writes=[PK(b)])
                mm(pst[b][:, 64:128], ONES, g.AA[:, sub, dr * 64:(dr + 1) * 64], True, True, reads=["const", ("aa", sub)], writes=[PK(b)])
                s.op("act", lambda E, b=b: E.activation(out=g.WGT, in_=pst[b][:, 0:128], func=AF.Exp), reads=[PK(b)], writes=["wgt"])
                s.op("dve", lambda E, sub=sub, dr=dr: E.tensor_tensor(out=g.DEC[:, 0:64], in0=g.WGT[:, 0:64], in1=g.DT[:, sub, dr * 64:(dr + 1) * 64], op=ALU.mult),
                     reads=["wgt", ("dt", sub)], writes=["dec"])
                if act == "store":
                    c = own_tok0 // 128 + sub
                    dma("sp", decb_d[c], g.WGT[:, 64:128], reads=["wgt"], writes=["decbd"])
                if act == "chainF" and mode == "own":
                    c = own_tok0 // 128 + sub
                for gg in range(8):
                    wi = g.wxc % 2
                    g.wxc += 1
                    wx = g.WX[wi]
                    s.op("pool", lambda E, wx=wx, sub=sub, gg=gg: E.tensor_tensor(
                        out=wx.rearrange("p (h d) -> p h d", h=8), in0=g.XHT[:, sub, gg * 512:(gg + 1) * 512].rearrange("p (h d) -> p h d", h=8),
                        in1=g.DEC[:, gg * 8:(gg + 1) * 8].unsqueeze(2).to_broadcast([128, 8, 64]), op=ALU.mult),
                        reads=[("xht", gg * 4 + q) for q in range(4)] + ["dec", "alias_gt"], writes=[("wx", wi)])
                    sb_ = newbank()
                    mm(pst[sb_][:, 0:512], g.XHT[:, sub, 4096 + gg * 128:4096 + (gg + 1) * 128], wx, True, True,
                       reads=[("xht", 32 + gg), ("wx", wi), "alias_gt"], writes=[PK(sb_)])
                    hs = Hbuf[:, gg * 512:(gg + 1) * 512]
                    if act == "store":
                        si = g.sstc % 2
                        g.sstc += 1
                        s.op("act", lambda E, si=si, sb_=sb_: E.activation(out=g.SST[si], in_=pst[sb_][:, 0:512], func=AF.Copy), reads=[PK(sb_)], writes=[("sst", si)])
                        dma("sp", Sb_d[c, :, gg * 512:(gg + 1) * 512], g.SST[si], reads=[("sst", si)], writes=["Sbd"])
                    else:
                        if act == "chainF" and mode == "own":
                            hi = g.hsc % 2
                            g.hsc += 1
                            s.op("act", lambda E, hi=hi, hs=hs: E.activation(out=g.HS[hi], in_=hs, func=AF.Copy), reads=[(hkey, gg)], writes=[("hs", hi)])
                            dma("sp", Hent_d[0, c, :, gg * 512:(gg + 1) * 512], g.HS[hi], reads=[("hs", hi)], writes=["Hentd"])
                        s.op("pool", lambda E, hs=hs, gg=gg: E.tensor_tensor(
                            out=hs.rearrange("p (h d) -> p h d", h=8), in0=hs.rearrange("p (h d) -> p h d", h=8),
                            in1=g.WGT[:, 64 + gg * 8:64 + (gg + 1) * 8].unsqueeze(2).to_broadcast([128, 8, 64]), op=ALU.mult),
                            reads=[(hkey, gg), "wgt"], writes=[(hkey, gg)])
                        s.op("dve", lambda E, hs=hs, sb_=sb_: E.tensor_tensor(out=hs, in0=hs, in1=pst[sb_][:, 0:512], op=ALU.add),
                             reads=[(hkey, gg), PK(sb_)], writes=[(hkey, gg)])

    HKEYS = [("HB", gg) for gg in range(8)]
    AR.off = persist_mark
    phaseA_alloc(CTX, True)
    zero_pads(1, CTX)
    s.op("pool", lambda E: E.memset(HB_, 0.0), writes=HKEYS)
    s.op("pool", lambda E: E.memset(g.H2, 0.0), writes=[("H2", gg) for gg in range(8)])
    phaseA_tile(ctx_in, 0, "ctx", 3, 6, 8, 1, CTX)
    dma("sp", Hsave_d, g.H2, reads=[("H2", gg) for gg in range(8)], writes=["hsave"])
    s.barrier()
    stop_if("A0")
    AR.off = persist_mark
    phaseA_alloc(TA, False)
    zero_pads(TA // SEGW, SEGW)
    for t in reversed(range(TOKH // TA)):
        phaseA_tile(x_in, TOKH + t * TA, "other", 0, 0, 2, TA // SEGW, SEGW)
    s.barrier()
    dma("sp", g.GTraw, Hsave_d, reads=["hsave"], writes=["h2tmp"])
    dma("sp", Hsave_d, HB_, reads=HKEYS, writes=["hsave"])
    s.op("dve", lambda E: E.tensor_copy(out=HB_, in_=g.GTraw), reads=["h2tmp"], writes=HKEYS)
    s.barrier()
    stop_if("AO")
    for t in range(TOKH // TA):
        phaseA_tile(x_in, t * TA, "own", 0, 0, 2, TA // SEGW, SEGW, own_tok0=t * TA)
    s.barrier()
    stop_if("AW")
    AR.off = persist_mark
    dma("sp", HB_, Hsave_d, writes=HKEYS)
    SBUFS = [AR.f32(4096) for _ in range(2)]
    DCB = [AR.f32(64) for _ in range(2)]
    HSB = [AR.bf(4096) for _ in range(2)]
    for n_, c in enumerate(reversed(range(NCH))):
        i = n_ % 2
        dma("sp", SBUFS[i], Sb_d[c], reads=["Sbd"], writes=[("sbuf", i)])
        dma("sp", DCB[i], decb_d[c], reads=["decbd"], writes=[("dcb", i)])
        s.op("act", lambda E, i=i: E.activation(out=HSB[i], in_=HB_, func=AF.Copy), reads=HKEYS, writes=[("hsb", i)])
        dma("sp", Hent_d[1, c], HSB[i], reads=[("hsb", i)], writes=["Hentd"])
        for hf in range(2):
            eng = "dve" if hf == 0 else "pool"
            hs = HB_[:, hf * 2048:(hf + 1) * 2048]
            s.op(eng, lambda E, hs=hs, i=i, hf=hf: E.tensor_tensor(out=hs.rearrange("p (h d) -> p h d", h=32), in0=hs.rearrange("p (h d) -> p h d", h=32),
                                                                in1=DCB[i][:, hf * 32:(hf + 1) * 32].unsqueeze(2).to_broadcast([128, 32, 64]), op=ALU.mult),
                 reads=[("dcb", i)] + HKEYS[hf * 4:(hf + 1) * 4], writes=HKEYS[hf * 4:(hf + 1) * 4])
            s.op(eng, lambda E, hs=hs, i=i, hf=hf: E.tensor_tensor(out=hs, in0=hs, in1=SBUFS[i][:, hf * 2048:(hf + 1) * 2048], op=ALU.add),
                 reads=[("sbuf", i)] + HKEYS[hf * 4:(hf + 1) * 4], writes=HKEYS[hf * 4:(hf + 1) * 4])
    s.barrier()

    stop_if("SC")
    AR.off = persist_small
    alloc_common(TB, wdn=32 * 128)
    T, ns = g.T, g.ns
    g.PRE = [AR.bf(T + 64) for _ in range(2)]
    g.CA = [AR.f32(T) for _ in range(2)]
    g.prec = 0
    zero_pads(T // SEGW, SEGW)
    DTA = AR.f32(ns * 256).rearrange("p (s c) -> p s c", s=ns)
    EACS = AR.f32(ns * 128).rearrange("p (s c) -> p s c", s=ns)
    MRG = AR.bf(KC * T).rearrange("p (k t) -> p k t", k=KC)
    GBUF = [AR.bf(T) for _ in range(2)]
    WST = AR.bf(8 * 128).rearrange("p (g i) -> p g i", g=8)
    BSB = AR.f32(8 * 128).rearrange("p (g i) -> p g i", g=8)
    BS2 = AR.f32(16 * 128).rearrange("p (f i) -> p f i", f=16)
    ONB = AR.bf(128)
    BNS = AR.f32(4 * 6)
    BNA = AR.f32(2)
    LNS = AR.f32(4)
    TSB = [AR.f32(T) for _ in range(2)]
    regR = AR.off
    XHG = [AR.bf(ns * 512).rearrange("p (s c) -> p s c", s=ns) for _ in range(2)]
    BTG = [AR.bf(T) for _ in range(2)]
    HEG = [[[AR.bf(512) for _ in range(ns)] for _ in range(2)] for _ in range(2)]
    SSN = [AR.f32(512) for _ in range(2)]
    CTG = [AR.bf(T) for _ in range(2)]
    ZS = [AR.bf(ns * 512).rearrange("p (s c) -> p s c", s=ns) for _ in range(2)]
    CBM = [AR.bf(128) for _ in range(2)]
    LH8 = [AR.f32(1024) for _ in range(2)]
    E8 = [AR.bf(1024) for _ in range(2)]
    M8 = [AR.bf(1024) for _ in range(2)]
    XDT = [AR.bf(512) for _ in range(2)]
    TY = [AR.f32(512) for _ in range(3)]
    YT = AR.f32(512)
    YN = AR.bf(512)
    YNT = AR.bf(32 * T).rearrange("p (k t) -> p k t", k=32)
    endR1 = AR.off
    AR.off = regR
    UT = AR.bf(KC * T).rearrange("p (k t) -> p k t", k=KC)
    VF = AR.f32(ns * D).rearrange("p (s d) -> p s d", s=ns)
    VNB = AR.bf(ns * D).rearrange("p (s d) -> p s d", s=ns)
    AR.off = max(AR.off, endR1)
    dma("pool", WST, wsT_in, writes=["wst"])
    dma("sp", BSB.rearrange("p g i -> p (g i)"), bs_in.partition_broadcast(128), writes=["bsb"])
    s.op("dve", lambda E: E.tensor_copy(out=ONB, in_=ONES), reads=["const"], writes=["onb"])
    LNWC = COLS[:, C_LN:C_LN + 16]
    LNBC = COLS[:, C_LN + 16:C_LN + 32]
    for hh in range(2):
        b = newbank()
        mm(pst[b][:, 0:512], ONB, WST[:, hh * 4:(hh + 1) * 4, :].rearrange("p g i -> p (g i)"), True, True, reads=["onb", "wst"], writes=[PK(b)])
        for f4 in range(8):
            fc = hh * 8 + f4
            gq = f4 // 2
            s.op("dve", lambda E, b=b, fc=fc, gq=gq: E.scalar_tensor_tensor(out=BS2[:, fc, :], in0=pst[b][:, gq * 128:(gq + 1) * 128], scalar=LNBC[:, fc:fc + 1],
                                                                         in1=BSB[:, fc // 2, :], op0=ALU.mult, op1=ALU.add),
                 reads=[PK(b), "cols", "bsb"], writes=["bs2"])
    BGC = COLS[:, C_BG:C_BG + 32]

    for tix in range(TOKH // TB):
        tok0 = tix * TB
        dma("sp", g.hT, hT_d[:, :, tok0:tok0 + T], reads=["hTd"], writes=[("hT", c) for c in range(KC)])
        dma("sp", DTA, dtA_d[tok0:tok0 + T, :].rearrange("(s p) c -> p s c", p=128), reads=["dtAd", "dtAd2"], writes=["dta"])
        for sub in range(ns):
            b = newbank()
            mm(pst[b][:, 0:64], LE, DTA[:, sub, 128:192], True, True, reads=["const", "dta"], writes=[PK(b)])
            mm(pst[b][:, 64:128], GE, DTA[:, sub, 192:256], True, True, reads=["const", "dta"], writes=[PK(b)])
            s.op("act", lambda E, b=b, sub=sub: E.activation(out=EACS[:, sub, :], in_=pst[b][:, 0:128], func=AF.Exp), reads=[PK(b)], writes=[("eacs", sub)])
        for gg in range(8):
            gi2 = gg % 2
            dma("sp", XHG[gi2], xh_d[tok0:tok0 + T, gg * 512:(gg + 1) * 512].rearrange("(s p) c -> p s c", p=128), reads=["xhd"], writes=[("xhg", gi2)])
            dma("sp", BTG[gi2], BT_d[gg, :, tok0:tok0 + T], reads=["BTd"], writes=[("btg", gi2)])
            for dr in range(2):
                for sub in range(ns):
                    dma("sp", HEG[gi2][dr][sub], Hent_d[dr, tok0 // 128 + sub, :, gg * 512:(gg + 1) * 512], reads=["Hentd"], writes=[("heg", gi2, dr, sub)])
            dma("sp", SSN[gi2], ssm_norm[gg * 512:(gg + 1) * 512].partition_broadcast(128), writes=[("ssn", gi2)])
            wblk, wk = load_wblk("win", BLK_C + gg)
            b = newbank()
            for k in range(KC):
                mm(pst[b][:, 0:T], wblk[:, k, :], g.hT[:, k, :], k == 0, k == KC - 1, reads=[wk, ("hT", k)], writes=[PK(b)])
            conv_chunk(b, 40 + gg, T // SEGW, SEGW, T, CTG[gi2], ("ctg", gi2), "pool")
            zb = [4 + sub for sub in range(ns)]
            for j in range(4):
                wblk, wk = load_wblk("win", BLK_Z + gg * 4 + j)
                for sub in range(ns):
                    for k in range(KC):
                        mm(pst[zb[sub]][:, j * 128:(j + 1) * 128], g.hT[:, k, sub * 128:(sub + 1) * 128], wblk[:, k, :], k == 0, k == KC - 1,
                           reads=[wk, ("hT", k)], writes=[PK(zb[sub])])
            for sub in range(ns):
                s.op("act", lambda E, sub=sub, gi2=gi2: E.activation(out=ZS[gi2][:, sub, :], in_=pst[zb[sub]][:, 0:512], func=AF.Silu),
                     reads=[PK(zb[sub])], writes=[("zs", gi2, sub)])
            for sub in range(ns):
                b = newbank()
                mm(pst[b][:, 0:128], BTG[gi2][:, sub * 128:(sub + 1) * 128], CTG[gi2][:, sub * 128:(sub + 1) * 128], True, True,
                   reads=[("btg", gi2), ("ctg", gi2)], writes=[PK(b)])
                s.op("dve", lambda E, b=b: E.tensor_tensor(out=CBM[0], in0=pst[b][:, 0:128], in1=LE, op=ALU.mult), reads=[PK(b), "const"], writes=[("cbm", 0)])
                s.op("dve", lambda E, b=b: E.tensor_tensor(out=CBM[1], in0=pst[b][:, 0:128], in1=GE, op=ALU.mult), reads=[PK(b), "const"], writes=[("cbm", 1)])
                yb = 6 + (sub % 2)
                tys = []
                for dr in range(2):
                    UTm, TRI = (GT_, LE) if dr == 0 else (LT_, GE)
                    acol = DTA[:, sub, 128 + dr * 64 + gg * 8:128 + dr * 64 + gg * 8 + 8]
                    dcol = DTA[:, sub, dr * 64 + gg * 8:dr * 64 + gg * 8 + 8]
                    ecol = EACS[:, sub, dr * 64 + gg * 8:dr * 64 + gg * 8 + 8]
                    s.op("pool", lambda E, dr=dr, acol=acol, UTm=UTm: E.tensor_tensor(
                        out=LH8[dr].rearrange("p (h j) -> p h j", h=8), in0=acol.unsqueeze(2).to_broadcast([128, 8, 128]),
                        in1=UTm.unsqueeze(1).to_broadcast([128, 8, 128]), op=ALU.mult), reads=["dta", "const"], writes=[("lh8", dr)])
                    db = [newbank(), newbank()]
                    for h in range(8):
                        mm(pst[db[h // 4]][:, (h % 4) * 128:(h % 4 + 1) * 128], LH8[dr][:, h * 128:(h + 1) * 128], TRI, True, True,
                           reads=[("lh8", dr), "const"], writes=[PK(db[h // 4])])
                    for hh in range(2):
                        s.op("act", lambda E, dr=dr, hh=hh, db=db: E.activation(out=E8[dr][:, hh * 512:(hh + 1) * 512], in_=pst[db[hh]][:, 0:512], func=AF.Exp),
                             reads=[PK(db[hh])], writes=[("e8", dr, hh)])
                    s.op("dve", lambda E, dr=dr: E.tensor_tensor(out=M8[dr].rearrange("p (h i) -> p h i", h=8), in0=E8[dr].rearrange("p (h i) -> p h i", h=8),
                                                               in1=CBM[dr].unsqueeze(1).to_broadcast([128, 8, 128]), op=ALU.mult),
                         reads=[("e8", dr, 0), ("e8", dr, 1), ("cbm", dr)], writes=[("m8", dr)])
                    s.op("pool", lambda E, dr=dr, dcol=dcol, sub=sub, gi2=gi2: E.tensor_tensor(
                        out=XDT[dr].rearrange("p (h d) -> p h d", h=8), in0=XHG[gi2][:, sub, :].rearrange("p (h d) -> p h d", h=8),
                        in1=dcol.unsqueeze(2).to_broadcast([128, 8, 64]), op=ALU.mult), reads=[("xhg", gi2), "dta"], writes=[("xdt", dr)])
                    for h in range(8):
                        mm(pst[yb][:, h * 64:(h + 1) * 64], M8[dr][:, h * 128:(h + 1) * 128], XDT[dr][:, h * 64:(h + 1) * 64], dr == 0 and h == 0, dr == 1 and h == 7,
                           reads=[("m8", dr), ("xdt", dr)], writes=[PK(yb)])
                    ob = newbank()
                    mm(pst[ob][:, 0:512], CTG[gi2][:, sub * 128:(sub + 1) * 128], HEG[gi2][dr][sub], True, True,
                       reads=[("ctg", gi2), ("heg", gi2, dr, sub)], writes=[PK(ob)])
                    s.op("dve", lambda E, dr=dr, ob=ob, ecol=ecol: E.tensor_tensor(
                        out=TY[dr].rearrange("p (h d) -> p h d", h=8), in0=pst[ob][:, 0:512].rearrange("p (h d) -> p h d", h=8),
                        in1=ecol.unsqueeze(2).to_broadcast([128, 8, 64]), op=ALU.mult), reads=[PK(ob), ("eacs", sub)], writes=[("ty", dr)])
                s.op("pool", lambda E, sub=sub, gi2=gi2, gg=gg: E.tensor_tensor(
                    out=TY[2].rearrange("p (h d) -> p h d", h=8), in0=XHG[gi2][:, sub, :].rearrange("p (h d) -> p h d", h=8),
                    in1=DSK[:, gg * 8:(gg + 1) * 8].unsqueeze(2).to_broadcast([128, 8, 64]), op=ALU.mult), reads=[("xhg", gi2), "dsk"], writes=[("ty", 2)])
                s.op("dve", lambda E, yb=yb: E.tensor_tensor(out=YT, in0=pst[yb][:, 0:512], in1=TY[0], op=ALU.add), reads=[PK(yb), ("ty", 0)], writes=["yt"])
                s.op("pool", lambda E: E.tensor_tensor(out=YT, in0=YT, in1=TY[1], op=ALU.add), reads=["yt", ("ty", 1)], writes=["yt"])
                s.op("pool", lambda E: E.tensor_tensor(out=YT, in0=YT, in1=TY[2], op=ALU.add), reads=["yt", ("ty", 2)], writes=["yt"])
                s.op("pool", lambda E, sub=sub, gi2=gi2: E.tensor_tensor(out=YT, in0=YT, in1=ZS[gi2][:, sub, :], op=ALU.mult), reads=["yt", ("zs", gi2, sub)], writes=["yt"])
                rstd_of(YT, 0, 512, "yt")
                s.op("dve", lambda E, gi2=gi2: E.scalar_tensor_tensor(out=YN, in0=YT, scalar=RSTD[:, 0:1], in1=SSN[gi2], op0=ALU.mult, op1=ALU.mult),
                     reads=["yt", ("rstd", 0), ("ssn", gi2)], writes=["yn"])
                tb = newbank()
                for q in range(4):
                    s.op("pe", lambda E, tb=tb, q=q: E.transpose(psb[tb][:, q * 128:(q + 1) * 128], YN[:, q * 128:(q + 1) * 128], IDB), reads=["yn", "idb"], writes=[PK(tb)])
                s.op("act", lambda E, tb=tb, gg=gg, sub=sub: E.activation(out=YNT[:, gg * 4:(gg + 1) * 4, sub * 128:(sub + 1) * 128],
                                                                         in_=psb[tb][:, 0:512].rearrange("p (q c) -> p q c", q=4), func=AF.Copy),
                     reads=[PK(tb)], writes=[("ynt", gg)])
        for dc in range(16):
            wbb, kb = load_wb32(dc)
            b = newbank()
            for k in range(32):
                mm(pst[b][:, 0:T], wbb[:, k, :], YNT[:, k, :], k == 0, k == 31, reads=[kb, ("ynt", k // 4)], writes=[PK(b)])
            wblk, wk = load_wblk("win", BLK_GB + dc)
            b2 = newbank()
            for k in range(KC):
                mm(pst[b2][:, 0:T], wblk[:, k, :], g.hT[:, k, :], k == 0, k == KC - 1, reads=[wk, ("hT", k)], writes=[PK(b2)])
            gi_ = dc % 2
            s.op("act", lambda E, b2=b2, gi_=gi_, dc=dc: E.activation(out=GBUF[gi_], in_=pst[b2][:, 0:T], func=AF.Sigmoid, bias=BGC[:, 16 + dc:17 + dc]),
                 reads=[PK(b2), "cols"], writes=[("gbuf", gi_)])
            s.op("dve", lambda E, b=b, gi_=gi_, dc=dc: E.tensor_tensor(out=MRG[:, dc, :], in0=pst[b][:, 0:T], in1=GBUF[gi_], op=ALU.mult),
                 reads=[PK(b), ("gbuf", gi_)], writes=[("mrg", dc)])
        s.barrier()
        for fc in range(16):
            wblk, wk = load_wblk("win", BLK_U + fc)
            b = newbank()
            for k in range(KC):
                mm(pst[b][:, 0:T], wblk[:, k, :], g.hT[:, k, :], k == 0, k == KC - 1, reads=[wk, ("hT", k)], writes=[PK(b)])
            s.op("act", lambda E, b=b, fc=fc: E.activation(out=UT[:, fc, :], in_=pst[b][:, 0:T], func=AF.Gelu), reads=[PK(b)], writes=[("ut", fc)])
        for jg in range(4):
            vb = [4 + sub for sub in range(ns)]
            for j in range(4):
                wblk, wk = load_wblk("win", BLK_V + jg * 4 + j)
                for sub in range(ns):
                    for k in range(KC):
                        mm(pst[vb[sub]][:, j * 128:(j + 1) * 128], g.hT[:, k, sub * 128:(sub + 1) * 128], wblk[:, k, :], k == 0, k == KC - 1,
                           reads=[wk, ("hT", k)], writes=[PK(vb[sub])])
            for sub in range(ns):
                s.op("act", lambda E, sub=sub, jg=jg: E.activation(out=VF[:, sub, jg * 512:(jg + 1) * 512], in_=pst[vb[sub]][:, 0:512], func=AF.Gelu),
                     reads=[PK(vb[sub])], writes=[("vf", sub)])
        for sub in range(ns):
            for q in range(4):
                s.op("dve", lambda E, sub=sub, q=q: E.bn_stats(out=BNS[:, q * 6:(q + 1) * 6], in_=VF[:, sub, q * 512:(q + 1) * 512]), reads=[("vf", sub)], writes=["bns"])
            s.op("dve", lambda E: E.bn_aggr(out=BNA, in_=BNS.rearrange("p (q s) -> p q s", q=4)), reads=["bns"], writes=["bna"])
            s.op("act", lambda E: E.activation(out=LNS[:, 0:1], in_=BNA[:, 1:2], func=AF.Sqrt, bias=EPSC), reads=["bna", "eps"], writes=["lns"])
            s.op("dve", lambda E: E.reciprocal(out=LNS[:, 1:2], in_=LNS[:, 0:1]), reads=["lns"], writes=["lns"])
            s.op("dve", lambda E: E.scalar_tensor_tensor(out=LNS[:, 2:3], in0=BNA[:, 0:1], scalar=-1.0, in1=LNS[:, 1:2], op0=ALU.mult, op1=ALU.mult),
                 reads=["lns", "bna"], writes=["lns"])
            s.op("act", lambda E, sub=sub: E.activation(out=VNB[:, sub, :], in_=VF[:, sub, :], func=AF.Identity, scale=LNS[:, 1:2], bias=LNS[:, 2:3]),
                 reads=["lns", ("vf", sub)], writes=[("vnb", sub)])
        for fc in range(16):
            b = newbank()
            for sub in range(ns):
                mm(pst[b][:, sub * 128:(sub + 1) * 128], VNB[:, sub, fc * 128:(fc + 1) * 128], WST[:, fc // 2, :], True, True,
                   reads=[("vnb", sub), "wst"], writes=[PK(b)])
            ti = fc % 2
            s.op("dve", lambda E, b=b, ti=ti, fc=fc: E.scalar_tensor_tensor(out=TSB[ti].rearrange("p (s i) -> p s i", s=ns), in0=pst[b][:, 0:T].rearrange("p (s i) -> p s i", s=ns),
                                                                          scalar=LNWC[:, fc:fc + 1], in1=BS2[:, fc, :].unsqueeze(1).to_broadcast([128, ns, 128]), op0=ALU.mult, op1=ALU.add),
                 reads=[PK(b), "bs2", "cols"], writes=[("tsb", ti)])
            s.op("pool", lambda E, ti=ti, fc=fc: E.tensor_tensor(out=UT[:, fc, :], in0=UT[:, fc, :], in1=TSB[ti], op=ALU.mult),
                 reads=[("tsb", ti), ("ut", fc)], writes=[("ut", fc)])
        for dc in range(16):
            wblk, wk = load_wblk("wa", dc)
            b = newbank()
            for k in range(KC):
                mm(pst[b][:, 0:T], wblk[:, k, :], UT[:, k, :], k == 0, k == KC - 1, reads=[wk, ("ut", k)], writes=[PK(b)])
            wblk2, wk2 = load_wblk("win", BLK_GA + dc)
            b2 = newbank()
            for k in range(KC):
                mm(pst[b2][:, 0:T], wblk2[:, k, :], g.hT[:, k, :], k == 0, k == KC - 1, reads=[wk2, ("hT", k)], writes=[PK(b2)])
            gi_ = dc % 2
            s.op("act", lambda E, b2=b2, gi_=gi_, dc=dc: E.activation(out=GBUF[gi_], in_=pst[b2][:, 0:T], func=AF.Sigmoid, bias=BGC[:, dc:dc + 1]),
                 reads=[PK(b2), "cols"], writes=[("gbuf", gi_)])
            ti = dc % 2
            s.op("dve", lambda E, b=b, gi_=gi_, ti=ti: E.tensor_tensor(out=TSB[ti], in0=pst[b][:, 0:T], in1=GBUF[gi_], op=ALU.mult),
                 reads=[PK(b), ("gbuf", gi_)], writes=[("tsb", ti)])
            s.op("pool", lambda E, ti=ti, dc=dc: E.tensor_tensor(out=MRG[:, dc, :], in0=MRG[:, dc, :], in1=TSB[ti], op=ALU.add),
                 reads=[("tsb", ti), ("mrg", dc)], writes=[("mrg", dc)])
        dma("sp", g.X, x1_d[tok0:tok0 + T, :].rearrange("(s p) d -> p s d", p=128), reads=["x1d"], writes=[("X", sub) for sub in range(ns)])
        for dblk in range(4):
            load_gsl(1, dblk)
            ob = [4 + sub for sub in range(ns)]
            for j in range(4):
                wblk, wk = load_wblk("wo", dblk * 4 + j)
                for sub in range(ns):
                    for k in range(KC):
                        mm(pst[ob[sub]][:, j * 128:(j + 1) * 128], MRG[:, k, sub * 128:(sub + 1) * 128], wblk[:, k, :], k == 0, k == KC - 1,
                           reads=[wk, ("mrg", k)], writes=[PK(ob[sub])])
            for sub in range(ns):
                resid_update(ob[sub], sub, dblk, 1)
        dma("sp", x1_d[tok0:tok0 + T, :].rearrange("(s p) d -> p s d", p=128), g.X, reads=[("X", sub) for sub in range(ns)], writes=["x1d"])
        s.barrier()

    stop_if("B")
    AR.off = persist_small
    alloc_common(TC)
    T, ns = g.T, g.ns
    g.GTt = AR.bf(FFC * T).rearrange("p (f t) -> p f t", f=FFC)
    g.SG = [AR.bf(T) for _ in range(2)]
    NFB = AR.f32(D)
    OUTB = AR.f32(D)
    dma("sp", NFB, norm_final.partition_broadcast(128), writes=["nfb"])
    outs = []
    for tix in range(TOKH // TC):
        tok0 = tix * TC
        dma("sp", g.X, x1_d[tok0:tok0 + T, :].rearrange("(s p) d -> p s d", p=128), reads=["x1d"], writes=[("X", sub) for sub in range(ns)])
        ffn(4, 2, "f2g", "f2u", "f2d")
        for sub in range(ns):
            rstd_of(g.X[:, sub, :], sub, D, ("X", sub))
            s.op("dve", lambda E, sub=sub: E.scalar_tensor_tensor(out=OUTB, in0=g.X[:, sub, :], scalar=RSTD[:, sub:sub + 1], in1=NFB, op0=ALU.mult, op1=ALU.mult),
                 reads=[("X", sub), ("rstd", sub), "nfb"], writes=["outb"])
            outs.append(dma("sp", out_d[tok0 + sub * 128:tok0 + (sub + 1) * 128, :], OUTB, reads=["outb"], writes=["outd"]))
    s.final_wait("sp", outs)
    stats = s.emit(st)
    st.close()
    return nc, stats


FULL_CFG = dict(FF=5632, TOKH=4096, TA=512, TB=256, TC=512)


def make_in_maps(inp, cfg):
    TOKH = cfg["TOKH"]
    x = np.asarray(inp["x"], np.float32)
    ctx = np.asarray(inp["ctx"], np.float32)
    B = x.shape[0]
    f32 = lambda a: np.ascontiguousarray(np.asarray(a, np.float32))
    consts = make_consts()
    shared = {
        "w_mod": f32(inp["w_mod"][0]), "b_mod": f32(inp["b_mod"][0]),
        "norms": f32(np.concatenate([inp["norm_ffn1"][0], inp["norm_mix"][0], inp["norm_ffn2"][0]]).reshape(48, 128)),
        "norm_final": f32(inp["norm_final"]),
        "ffn1_gate": f32(inp["ffn1_gate"][0]), "ffn1_up": f32(inp["ffn1_up"][0]), "ffn1_down": f32(inp["ffn1_down"][0]),
        "ffn2_gate": f32(inp["ffn2_gate"][0]), "ffn2_up": f32(inp["ffn2_up"][0]), "ffn2_down": f32(inp["ffn2_down"][0]),
        "w_in": f32(inp["w_in"][0]), "w_a": f32(inp["w_a"][0]), "w_b": f32(inp["w_b"][0]), "w_out": f32(inp["w_out"][0]),
        "b_gate": f32(inp["b_gate"][0]).reshape(32, 128), "gmlp_ln_wb": f32(np.concatenate([inp["gmlp_ln_w"][0], inp["gmlp_ln_b"][0]]).reshape(32, 128)),
        "conv_b": f32(inp["conv_b"][0]).reshape(48, 128), "d_skip": f32(inp["d_skip"][0]), "ssm_norm": f32(inp["ssm_norm"][0]),
        "consts": consts,
    }
    win = shared["w_in"]
    ws = np.asarray(inp["gmlp_ws"][0], np.float32)
    bs = np.asarray(inp["gmlp_bs"][0], np.float32)
    cw = np.asarray(inp["conv_w"][0], np.float32)
    al = np.asarray(inp["a_log"][0], np.float32)
    db = np.asarray(inp["dt_bias"][0], np.float32)
    wdt = win[:, OFF_DT:OFF_DT + 128]
    per_s = []
    for s_ in range(2):
        if s_ == 0:
            d = {"w_dt": f32(wdt), "gmlp_wsT": f32(ws.transpose(2, 0, 1)), "gmlp_bs": f32(bs.reshape(-1)),
                 "conv_w": f32(cw.reshape(240, 128)), "a_log": f32(al.reshape(-1)), "dt_bias": f32(db.reshape(-1))}
        else:
            wsf = ws[:, ::-1, ::-1]
            d = {"w_dt": f32(np.concatenate([wdt[:, 64:128], wdt[:, 0:64]], axis=1)),
                 "gmlp_wsT": f32(wsf.transpose(2, 0, 1)), "gmlp_bs": f32(bs[:, ::-1].reshape(-1)),
                 "conv_w": f32(cw[::-1].reshape(240, 128)), "a_log": f32(al[::-1].reshape(-1)), "dt_bias": f32(db[::-1].reshape(-1))}
        per_s.append(d)
    in_maps = []
    cc = np.asarray(inp["c_ctx"], np.float32)
    for core in range(2 * B):
        b, s_ = core // 2, core % 2
        xb = x[b] if s_ == 0 else x[b, ::-1]
        cb = ctx[b] if s_ == 0 else ctx[b, ::-1]
        m = dict(shared)
        m.update(per_s[s_])
        m["x"] = f32(xb)
        m["ctx"] = f32(cb)
        m["cvec"] = f32(np.concatenate([np.asarray(inp["c"], np.float32)[b], cc]).reshape(32, 128))
        in_maps.append(m)
    return in_maps


_CACHE = {}


def run(inp, cfg):
    key = tuple(sorted(cfg.items()))
    if key not in _CACHE:
        _CACHE[key] = build(cfg)[0]
    nc = _CACHE[key]
    in_maps = make_in_maps(inp, cfg)
    res = run_bass_kernel_spmd(nc, in_maps, core_ids=list(range(len(in_maps))))
    TOKH = cfg["TOKH"]
    B = len(in_maps) // 2
    out = np.empty((B, 2 * TOKH, D), np.float32)
    for core in range(2 * B):
        b, s_ = core // 2, core % 2
        o = np.asarray(res.results[core]["out"], np.float32)
        if s_ == 0:
            out[b, 0:TOKH] = o
        else:
            out[b, TOKH:] = o[::-1]
    return out


def kernel(**inputs):
    return run(inputs, FULL_CFG)
```

```python
import numpy as np
from contextlib import ExitStack
import concourse.bass as bass
import concourse.mybir as mybir
from concourse.bass_utils import run_bass_kernel_spmd

F32 = mybir.dt.float32
BF16 = mybir.dt.bfloat16
AF = mybir.ActivationFunctionType
ALU = mybir.AluOpType
P = 128
D = 2048
KC = 16
CTX = 256
EPS = 1e-6
OFF_U, OFF_V, OFF_Z, OFF_XB, OFF_C, OFF_DT, OFF_GATE, IN_W = 0, 2048, 4096, 8192, 13312, 14336, 14464, 18560
BLK_U, BLK_V, BLK_Z, BLK_XB, BLK_C, BLK_GA, BLK_GB = 0, 16, 32, 64, 104, 113, 129

EPOCH = 8000
NSLOT = 12
DEBUG = False
NAMES = {}


class Op:
    __slots__ = ("eng", "fn", "deps", "signal", "is_dma", "slot", "dval", "sigcnt", "line")

    def __init__(self, eng, fn, is_dma):
        self.eng = eng
        self.fn = fn
        self.deps = []
        self.signal = False
        self.is_dma = is_dma
        self.slot = None
        self.dval = None
        self.sigcnt = None
        self.line = None


class _Rec:
    def __getattr__(self, name):
        def f(*a, **k):
            self.call = (name, a, k)
            return None
        return f


class Sched:
    ENGS = ("pe", "act", "dve", "pool", "sp")

    def __init__(self, nc):
        self.nc = nc
        self.ops = {e: [] for e in self.ENGS}
        self.last_w = {}
        self.readers = {}
        self.dma_cnt = {e: 0 for e in self.ENGS}
        self.dma_last = {e: [None] * NSLOT for e in self.ENGS}

    def op(self, eng, fn, reads=(), writes=(), dma=False):
        rec = _Rec()
        fn(rec)
        _n, _a, _k = rec.call
        o = Op(eng, (lambda E, _n=_n, _a=_a, _k=_k: getattr(E, _n)(*_a, **_k)), dma)
        if DEBUG:
            import sys as _sys
            f = _sys._getframe(1)
            ln = []
            while f is not None and len(ln) < 4:
                ln.append(f.f_lineno)
                f = f.f_back
            o.line = ln
        deps = []
        for k in reads:
            w = self.last_w.get(k)
            if w is not None:
                deps.append(w)
            if isinstance(k, tuple) and k[0] == "ps":
                rl = self.readers.get(k)
                if rl:
                    deps.extend(v for e_, v in rl.items() if e_ != eng)
        for k in writes:
            w = self.last_w.get(k)
            if w is not None:
                deps.append(w)
            rl = self.readers.get(k)
            if rl:
                deps.extend(rl.values())
        if dma:
            c = self.dma_cnt[eng]
            o.slot = c % NSLOT
            o.dval = 16 * (c // NSLOT + 1)
            prev = self.dma_last[eng][o.slot]
            if prev is not None:
                deps.append(prev)
            self.dma_last[eng][o.slot] = o
            self.dma_cnt[eng] = c + 1
        seen = set()
        for d in deps:
            if d is o or id(d) in seen:
                continue
            seen.add(id(d))
            if (not d.is_dma) and d.eng == eng and eng == "pe":
                continue
            o.deps.append(d)
            if not d.is_dma:
                d.signal = True
        for k in writes:
            self.last_w[k] = o
            self.readers[k] = {}
        for k in reads:
            rl = self.readers.setdefault(k, {})
            rl[("dma", id(o)) if dma else eng] = o
        self.ops[eng].append(o)
        return o

    def barrier(self):
        lasts = []
        for e in self.ENGS:
            for o in reversed(self.ops[e]):
                if not o.is_dma and o.fn is not None:
                    lasts.append(o)
                    break
            if e == "pool":
                continue
            for o in self.dma_last[e]:
                if o is not None:
                    lasts.append(o)
        for e in self.ENGS:
            o = Op(e, None, False)
            for d in lasts:
                if d.eng == e and not d.is_dma:
                    continue
                o.deps.append(d)
                if not d.is_dma:
                    d.signal = True
            self.ops[e].append(o)
        self.last_w = {k: v for k, v in self.last_w.items() if isinstance(k, tuple) and k[0] == "W"}
        self.readers = {}

    def final_wait(self, eng, ops):
        o = Op(eng, None, False)
        for d in ops:
            o.deps.append(d)
            if not d.is_dma:
                d.signal = True
        self.ops[eng].append(o)

    def emit(self, stack):
        nc = self.nc
        nsig = {}
        for e in self.ENGS:
            c = 0
            for o in self.ops[e]:
                if o.signal and not o.is_dma:
                    c += 1
                    o.sigcnt = c
            nsig[e] = c
        esem = {}
        for e in self.ENGS:
            ne = (nsig[e] + EPOCH - 1) // EPOCH
            esem[e] = [stack.enter_context(nc.semaphore(f"s_{e}_{i}")) for i in range(ne)]
        dsem = {}
        for e in self.ENGS:
            n = min(self.dma_cnt[e], NSLOT)
            dsem[e] = [stack.enter_context(nc.semaphore(f"d_{e}_{i}")) for i in range(n)]
        engobj = {"pe": "tensor", "act": "scalar", "dve": "vector", "pool": "gpsimd", "sp": "sync"}
        stats = {}

        def run(ename, E):
            waited = {}
            maxep = {}
            nw = 0
            for o in self.ops[ename]:
                for d in o.deps:
                    if d.is_dma:
                        sem = dsem[d.eng][d.slot]
                        val = d.dval
                        key = ("d", d.eng, d.slot)
                    else:
                        ep = (d.sigcnt - 1) // EPOCH
                        val = (d.sigcnt - 1) % EPOCH + 1
                        sem = esem[d.eng][ep]
                        key = ("e", d.eng, ep)
                        if maxep.get(d.eng, -1) > ep:
                            continue
                        maxep[d.eng] = ep
                    if waited.get(key, 0) >= val:
                        continue
                    waited[key] = val
                    E.wait_ge(sem, val)
                    nw += 1
                if o.fn is None:
                    continue
                ins = o.fn(E)
                if DEBUG:
                    try:
                        NAMES[ins.ins.name] = o.line
                    except Exception:
                        pass
                if o.is_dma:
                    ins.then_inc(dsem[ename][o.slot], 16)
                elif o.signal:
                    ep = (o.sigcnt - 1) // EPOCH
                    ins.then_inc(esem[ename][ep], 1)
            stats[ename] = (len(self.ops[ename]), nw)

        with nc.Block() as block:
            for ename in self.ENGS:
                getattr(block, engobj[ename])(lambda E, ename=ename: run(ename, E))
        return stats


class Defer:
    def __init__(self):
        self.q = []

    def push(self, fn, delay):
        self.q.append([delay, fn])

    def tick(self):
        for it in self.q:
            it[0] -= 1
        ready = [it for it in self.q if it[0] <= 0]
        self.q = [it for it in self.q if it[0] > 0]
        for it in ready:
            it[1]()

    def flush(self):
        q, self.q = self.q, []
        for it in q:
            it[1]()


class Arena:
    def __init__(self, ap, nwords):
        self.ap = ap
        self.n = nwords
        self.off = 0

    def f32(self, n):
        o = self.off
        self.off += n
        assert self.off <= self.n, ("SBUF arena overflow", self.off, self.n)
        return self.ap[:, o:o + n]

    def bf(self, n):
        w = (n + 1) // 2
        o = self.off
        self.off += w
        assert self.off <= self.n, ("SBUF arena overflow", self.off, self.n)
        return self.ap[:, o:o + w].bitcast(BF16)[:, 0:n]


NCONST = 6 * 128


def make_consts():
    i = np.arange(128)
    p, q = i[:, None], i[None, :]
    mats = [p == q, p <= q, p >= q, p > q, p < q, np.ones((128, 128), bool)]
    return np.concatenate([m.astype(np.float32) for m in mats], axis=1)


class _Stop(Exception):
    pass


def build(cfg):
    nc_holder = {}
    try:
        return _build(cfg, nc_holder)
    except _Stop:
        s, st, nc = nc_holder["s"], nc_holder["st"], nc_holder["nc"]
        s.barrier()
        stats = s.emit(st)
        st.close()
        return nc, stats


def _build(cfg, nc_holder):
    FF = cfg["FF"]
    FFC = FF // 128
    TOKH = cfg["TOKH"]
    TA = cfg["TA"]
    TB = cfg["TB"]
    TC = cfg["TC"]
    SEGW = 64
    NCH = TOKH // 128
    NQ = FFC // 11
    assert FFC % 11 == 0 and TOKH % TA == 0 and TOKH % TB == 0 and TOKH % TC == 0
    nc = bass.Bass("TRN2", target_bir_lowering=False)

    def din(name, shape):
        return nc.dram_tensor(name, list(shape), F32, kind="ExternalInput").ap()

    x_in = din("x", [2 * TOKH, D])
    ctx_in = din("ctx", [CTX, D])
    cvec = din("cvec", [32, 128])
    w_mod = din("w_mod", [D, 9 * D])
    b_mod = din("b_mod", [9 * D])
    norms = din("norms", [48, 128])
    norm_final = din("norm_final", [D])
    Wsrc = {
        "f1g": din("ffn1_gate", [D, FF]), "f1u": din("ffn1_up", [D, FF]), "f1d": din("ffn1_down", [FF, D]),
        "f2g": din("ffn2_gate", [D, FF]), "f2u": din("ffn2_up", [D, FF]), "f2d": din("ffn2_down", [FF, D]),
        "win": din("w_in", [D, IN_W]), "wdt": din("w_dt", [D, 128]),
        "wa": din("w_a", [D, D]), "wb": din("w_b", [2 * D, D]), "wo": din("w_out", [D, D]),
    }
    b_gate = din("b_gate", [32, 128])
    ln_wb = din("gmlp_ln_wb", [32, 128])
    wsT_in = din("gmlp_wsT", [128, 8, 128])
    bs_in = din("gmlp_bs", [8 * 128])
    conv_w = din("conv_w", [240, 128])
    conv_b = din("conv_b", [48, 128])
    a_log = din("a_log", [128])
    dt_bias = din("dt_bias", [128])
    d_skip = din("d_skip", [64])
    ssm_norm = din("ssm_norm", [2 * D])
    consts_in = din("consts", [128, NCONST])
    out_d = nc.dram_tensor("out", [TOKH, D], F32, kind="ExternalOutput").ap()

    def dscr(name, shape, dt):
        return nc.dram_tensor(name, list(shape), dt).ap()

    Wbf = {}
    for nm in ("f1g", "f1u", "f2g", "f2u"):
        Wbf[nm] = dscr("bf_" + nm, [FFC, 128, KC, 128], BF16)
    for nm in ("f1d", "f2d"):
        Wbf[nm] = dscr("bf_" + nm, [D // 512, NQ, 128, 11, 512], BF16)
    Wbf["win"] = dscr("bf_win", [IN_W // 128, 128, KC, 128], BF16)
    Wbf["wdt"] = dscr("bf_wdt", [1, 128, KC, 128], BF16)
    Wbf["wa"] = dscr("bf_wa", [16, 128, KC, 128], BF16)
    Wbf["wo"] = dscr("bf_wo", [16, 128, KC, 128], BF16)
    Wbf["wb"] = dscr("bf_wb", [16, 128, 32, 128], BF16)
    x1_d = dscr("x1_d", [TOKH, D], F32)
    hT_d = dscr("hT_d", [128, KC, TOKH], BF16)
    xh_d = dscr("xh_d", [TOKH, 4096], BF16)
    BT_d = dscr("BT_d", [8, 128, TOKH], BF16)
    dtA_d = dscr("dtA_d", [TOKH, 256], F32)
    Sb_d = dscr("Sb_d", [NCH, 128, 4096], F32)
    decb_d = dscr("decb_d", [NCH, 128, 64], F32)
    Hent_d = dscr("Hent_d", [2, NCH, 128, 4096], BF16)
    gates_d = dscr("gates_d", [4, 128, D], F32)
    Hsave_d = dscr("Hsave_d", [128, 4096], F32)

    st = ExitStack()
    NW = 52000
    arena_t = st.enter_context(nc.sbuf_tensor("arena", [128, NW], F32))
    AR = Arena(arena_t, NW)
    pst = [st.enter_context(nc.psum_tensor(f"ps{i}", [128, 512], F32)) for i in range(8)]
    psb = [t.bitcast(BF16) for t in pst]
    s = Sched(nc)
    nc_holder.update(s=s, st=st, nc=nc)
    STOP = cfg.get("stop")

    def stop_if(name):
        if STOP == name:
            raise _Stop()
    bankctr = [0]

    def newbank(lo=0, hi=4):
        b = lo + bankctr[0] % (hi - lo)
        bankctr[0] += 1
        return b

    def PK(b):
        return ("ps", b)

    def mm(out, lhsT, rhs, start, stop, reads, writes):
        s.op("pe", lambda E: E.matmul(out, lhsT=lhsT, rhs=rhs, start=start, stop=stop), reads, writes)

    def dma(eng, out, in_, reads=(), writes=()):
        return s.op(eng, lambda E: E.dma_start(out=out, in_=in_), reads, writes, dma=True)

    CONST = AR.f32(NCONST)
    IDF, LE, GE, GT_, LT_, ONES = [CONST[:, i * 128:(i + 1) * 128] for i in range(6)]
    IDB = AR.bf(128)
    COLS = AR.f32(528)
    MCOL = AR.f32(192)
    AB = AR.f32(160)
    EPSC = AR.f32(1)
    SSQ = AR.f32(4)
    RMS = AR.f32(4)
    RSTD = AR.f32(4)
    DTB = AR.f32(128)
    ANEG = AR.f32(128)
    DSK = AR.f32(64)
    persist_small = AR.off
    HB_ = AR.f32(4096)
    persist_mark = AR.off
    C_CV, C_N, C_BM, C_CW, C_CB, C_BG, C_LN = 0, 32, 80, 176, 416, 464, 496

    dma("sp", CONST, consts_in, writes=["const"])
    s.op("dve", lambda E: E.tensor_copy(out=IDB, in_=IDF), reads=["const"], writes=["idb"])
    s.op("pool", lambda E: E.memset(EPSC, EPS), writes=["eps"])
    dma("sp", DTB, dt_bias.partition_broadcast(128), writes=["dtb"])
    dma("sp", ANEG, a_log.partition_broadcast(128), writes=["aneg"])
    dma("sp", DSK, d_skip.partition_broadcast(128), writes=["dsk"])
    s.op("act", lambda E: E.activation(out=ANEG, in_=ANEG, func=AF.Exp), reads=["aneg"], writes=["aneg"])
    s.op("dve", lambda E: E.tensor_scalar_mul(out=ANEG, in0=ANEG, scalar1=-1.0), reads=["aneg"], writes=["aneg"])

    def conv_k(nm, nblk, kc=KC):
        src = Wsrc[nm].rearrange("(k p) (b c) -> b p k c", p=128, c=128)
        for b0 in range(nblk):
            dma("pool", Wbf[nm][b0], src[b0], writes=[("W", nm, b0)])

    def conv_d(nm):
        src = Wsrc[nm].rearrange("(q f p) (d c) -> d q p f c", p=128, f=11, c=512)
        for d_ in range(D // 512):
            for q_ in range(NQ):
                dma("pool", Wbf[nm][d_, q_], src[d_, q_], writes=[("W", nm, d_, q_)])

    def wkey(nm, b):
        return ("W", nm, b)

    ROWS = AR.f32(5 * 128)
    rowsrc = [(cvec, 32), (norms, 48), None, (conv_w, 240), (conv_b, 48), (b_gate, 32)]
    bm2 = b_mod.rearrange("(j c p) -> j c p", c=16, p=128)
    pieces = [(cvec, 0, 32), (norms, 0, 48)]
    for j in (0, 1, 3, 4, 6, 7):
        pieces.append((bm2[j], 0, 16))
    pieces += [(conv_w, 0, 240), (conv_b, 0, 48), (b_gate, 0, 32), (ln_wb, 0, 32)]
    r = 0
    for (ap, r0, n) in pieces:
        done = 0
        while done < n:
            t = r // 128
            ro = r % 128
            m = min(n - done, 128 - ro)
            dma("sp", ROWS[ro:ro + m, t * 128:(t + 1) * 128], ap[done:done + m, :], writes=[("rows", t)])
            done += m
            r += m
    assert r == 528
    for t in range(5):
        n = min(128, 528 - t * 128)
        b = newbank()
        s.op("pe", lambda E, t=t, n=n, b=b: E.transpose(pst[b][:, 0:n], ROWS[0:n, t * 128:(t + 1) * 128], IDF[0:n, 0:n]),
             reads=[("rows", t), "const"], writes=[PK(b)])
        s.op("dve", lambda E, t=t, n=n, b=b: E.tensor_copy(out=COLS[:, t * 128:t * 128 + n], in_=pst[b][:, 0:n]),
             reads=[PK(b)], writes=["cols"])
    AR.off = persist_mark

    m0 = AR.off
    SC = AR.bf(32)
    SCB = AR.bf(2 * KC * 128)
    s.op("act", lambda E: E.activation(out=SC, in_=COLS[:, C_CV:C_CV + 32], func=AF.Silu), reads=["cols"], writes=["sc"])
    SCBv = SCB.rearrange("p (t k m) -> p t k m", t=2, k=KC)
    for t in range(2):
        s.op("dve", lambda E, t=t: E.tensor_copy(out=SCBv[:, t], in_=SC[:, t * 16:(t + 1) * 16].unsqueeze(2).to_broadcast([128, KC, 128])),
             reads=["sc"], writes=[("scb", t)])
    SCv = SC.rearrange("p (t k) -> p t k", t=2)
    WM = [AR.bf(KC * 128).rearrange("p (k c) -> p k c", k=KC) for _ in range(4)]
    wmsrc = w_mod.rearrange("(k p) (b c) -> b p k c", p=128, c=128)
    wmc = [0]

    def load_wm(blk):
        i = wmc[0] % 4
        wmc[0] += 1
        dma("pool", WM[i], wmsrc[blk], writes=[("wm", i)])
        return WM[i], ("wm", i)

    mb = newbank()
    for ji, j in enumerate((0, 1, 3, 4, 6, 7)):
        for c in range(16):
            wblk, wk = load_wm(j * 16 + c)
            idx = ji * 16 + c
            for k in range(KC):
                mm(pst[mb][:, idx * 2:idx * 2 + 2], wblk[:, k, :], SCv[:, :, k], k == 0, k == KC - 1,
                   reads=[wk, "sc"], writes=[PK(mb)])
    bmcol = COLS[:, C_BM:C_BM + 96]
    s.op("dve", lambda E: E.tensor_tensor(out=MCOL.rearrange("p (i t) -> p i t", t=2), in0=pst[mb][:, 0:192].rearrange("p (i t) -> p i t", t=2),
                                          in1=bmcol.unsqueeze(2).to_broadcast([128, 96, 2]), op=ALU.add),
         reads=[PK(mb), "cols"], writes=["mcol"])
    MC = MCOL.rearrange("p (j c t) -> p j c t", j=6, c=16)
    ABv = AB.rearrange("p (i c) -> p i c", c=16)
    NRM = COLS[:, C_N:C_N + 48].rearrange("p (i c) -> p i c", c=16)

    def mk_ab(ia, nidx, jscale, jshift, t):
        s.op("dve", lambda E: E.tensor_scalar_add(out=ABv[:, ia], in0=MC[:, jscale, :, t], scalar1=1.0), reads=["mcol"], writes=["ab"])
        s.op("dve", lambda E: E.tensor_tensor(out=ABv[:, ia], in0=ABv[:, ia], in1=NRM[:, nidx], op=ALU.mult), reads=["ab", "cols"], writes=["ab"])
        s.op("dve", lambda E: E.tensor_copy(out=ABv[:, ia + 1], in_=MC[:, jshift, :, t]), reads=["mcol"], writes=["ab"])

    mk_ab(0, 0, 1, 0, 0)
    mk_ab(2, 1, 3, 2, 0)
    mk_ab(4, 2, 5, 4, 0)
    mk_ab(6, 0, 1, 0, 1)
    mk_ab(8, 1, 3, 2, 1)
    BMB = [AR.f32(128) for _ in range(2)]
    GST = [AR.f32(128) for _ in range(2)]
    gc = 0
    for (j, t, gi, scl) in ((2, 0, 0, 0.5), (5, 0, 1, 1.0), (8, 0, 2, 0.5), (2, 1, 3, 0.5)):
        for c in range(16):
            wblk, wk = load_wm(j * 16 + c)
            b = newbank()
            for k in range(KC):
                mm(pst[b][:, 0:128], SCBv[:, t, k, :], wblk[:, k, :], k == 0, k == KC - 1, reads=[wk, ("scb", t)], writes=[PK(b)])
            i = gc % 2
            gc += 1
            dma("sp", BMB[i], b_mod[j * D + c * 128:j * D + (c + 1) * 128].partition_broadcast(128), writes=[("bmb", i)])
            s.op("dve", lambda E, i=i, b=b: E.tensor_tensor(out=GST[i], in0=pst[b][:, 0:128], in1=BMB[i], op=ALU.add),
                 reads=[PK(b), ("bmb", i)], writes=[("gst", i)])
            s.op("act", lambda E, i=i, scl=scl: E.activation(out=GST[i], in_=GST[i], func=AF.Copy, scale=scl), reads=[("gst", i)], writes=[("gst", i)])
            dma("sp", gates_d[gi, :, c * 128:(c + 1) * 128], GST[i], reads=[("gst", i)], writes=[("gates", gi)])

    conv_k("f1g", FFC)
    conv_k("f1u", FFC)
    conv_d("f1d")
    conv_k("win", IN_W // 128)
    conv_k("wdt", 1)
    conv_k("wa", 16)
    conv_k("wo", 16)
    srcwb = Wsrc["wb"].rearrange("(k p) (b c) -> b p k c", p=128, c=128)
    for b0 in range(16):
        dma("pool", Wbf["wb"][b0], srcwb[b0], writes=[("W", "wb", b0)])
    conv_k("f2g", FFC)
    conv_k("f2u", FFC)
    conv_d("f2d")
    s.barrier()
    stop_if("setup")
    AR.off = persist_mark

    class G:
        pass

    g = G()

    def alloc_common(T, wdn=11 * 512):
        ns = T // 128
        g.T = T
        g.ns = ns
        g.X = AR.f32(ns * D).rearrange("p (s d) -> p s d", s=ns)
        g.XN = AR.bf(D)
        g.hT = AR.bf(KC * T).rearrange("p (k t) -> p k t", k=KC)
        g.WP = [AR.bf(KC * 128).rearrange("p (k c) -> p k c", k=KC) for _ in range(6)]
        g.WD = [AR.bf(wdn) for _ in range(2)]
        g.wpc = 0
        g.wdc = 0
        g.TMP = [AR.f32(512) for _ in range(2)]
        g.tmpc = 0
        g.GSL = [AR.f32(512) for _ in range(2)]
        g.gslc = 0

    def load_wblk(nm, b):
        i = g.wpc % len(g.WP)
        g.wpc += 1
        dma("sp", g.WP[i], Wbf[nm][b], reads=[wkey(nm, b)], writes=[("wp", i)])
        return g.WP[i], ("wp", i)

    def load_wd(nm, d_, q):
        i = g.wdc % 2
        g.wdc += 1
        ap = g.WD[i].rearrange("p (f c) -> p f c", f=11)
        dma("sp", ap, Wbf[nm][d_, q], reads=[("W", nm, d_, q)], writes=[("wd", i)])
        return ap, ("wd", i)

    def load_wb32(b):
        i = g.wdc % 2
        g.wdc += 1
        ap = g.WD[i][:, 0:32 * 128].rearrange("p (k c) -> p k c", k=32)
        dma("sp", ap, Wbf["wb"][b], reads=[("W", "wb", b)], writes=[("wd", i)])
        return ap, ("wd", i)

    def rstd_of(src_ap, sub, n, xkey):
        s.op("pool", lambda E: E.memset(SSQ[:, sub:sub + 1], 0.0), writes=[("ssq", sub)])
        s.op("act", lambda E: E.activation(out=g.XN[:, 0:n], in_=src_ap, func=AF.Square, accum_out=SSQ[:, sub:sub + 1]),
             reads=[xkey], writes=["xn", ("ssq", sub)])
        s.op("act", lambda E: E.activation(out=RMS[:, sub:sub + 1], in_=SSQ[:, sub:sub + 1], func=AF.Sqrt, scale=1.0 / n, bias=EPSC),
             reads=[("ssq", sub), "eps"], writes=[("rms", sub)])
        s.op("dve", lambda E: E.reciprocal(out=RSTD[:, sub:sub + 1], in_=RMS[:, sub:sub + 1]), reads=[("rms", sub)], writes=[("rstd", sub)])

    def norm_to_hT(ia):
        Acol, Bcol = ABv[:, ia], ABv[:, ia + 1]
        ev = 0
        for sub in range(g.ns):
            rstd_of(g.X[:, sub, :], sub, D, ("X", sub))
            s.op("dve", lambda E, sub=sub: E.tensor_scalar_mul(out=g.XN, in0=g.X[:, sub, :], scalar1=RSTD[:, sub:sub + 1]),
                 reads=[("X", sub), ("rstd", sub)], writes=["xn"])
            for half in range(2):
                b = newbank()
                for c8 in range(8):
                    c = half * 8 + c8
                    s.op("pe", lambda E, b=b, c8=c8, c=c: E.transpose(psb[b][:, c8 * 128:(c8 + 1) * 128], g.XN[:, c * 128:(c + 1) * 128], IDB),
                         reads=["xn", "idb"], writes=[PK(b)])
                for c8 in range(8):
                    c = half * 8 + c8
                    o_ap = g.hT[:, c, sub * 128:(sub + 1) * 128]
                    i_ap = psb[b][:, c8 * 128:(c8 + 1) * 128]
                    if half == 0:
                        s.op("act", lambda E, o_ap=o_ap, i_ap=i_ap, c=c: E.activation(out=o_ap, in_=i_ap, func=AF.Identity, scale=Acol[:, c:c + 1], bias=Bcol[:, c:c + 1]),
                             reads=[PK(b), "ab"], writes=[("hT", c)])
                    else:
                        s.op("dve", lambda E, o_ap=o_ap, i_ap=i_ap, c=c: E.tensor_scalar(out=o_ap, in0=i_ap, scalar1=Acol[:, c:c + 1], scalar2=Bcol[:, c:c + 1], op0=ALU.mult, op1=ALU.add),
                             reads=[PK(b), "ab"], writes=[("hT", c)])
                    ev += 1

    def resid_update(bank, sub, dblk, gi):
        i = g.tmpc % 2
        g.tmpc += 1
        tmp = g.TMP[i]
        s.op("dve", lambda E: E.tensor_tensor(out=tmp, in0=pst[bank][:, 0:512], in1=g.gsl, op=ALU.mult),
             reads=[PK(bank), g.gslk], writes=[("tmp", i)])
        xs = g.X[:, sub, dblk * 512:(dblk + 1) * 512]
        s.op("pool", lambda E: E.tensor_tensor(out=xs, in0=xs, in1=tmp, op=ALU.add), reads=[("tmp", i), ("X", sub)], writes=[("X", sub)])

    def load_gsl(gi, dblk):
        i = g.gslc % 2
        g.gslc += 1
        dma("sp", g.GSL[i], gates_d[gi, :, dblk * 512:(dblk + 1) * 512], reads=[("gates", gi)], writes=[("gsl", i)])
        g.gsl = g.GSL[i]
        g.gslk = ("gsl", i)

    def ffn(ia, gi, wg, wu, wd):
        T, ns = g.T, g.ns
        norm_to_hT(ia)
        stop_if("f_norm")
        for f in range(FFC):
            wgb, kg = load_wblk(wg, f)
            wub, ku = load_wblk(wu, f)
            pg = newbank()
            pu = newbank()
            for k in range(KC):
                mm(pst[pg][:, 0:T], wgb[:, k, :], g.hT[:, k, :], k == 0, k == KC - 1, reads=[kg, ("hT", k)], writes=[PK(pg)])
            for k in range(KC):
                mm(pst[pu][:, 0:T], wub[:, k, :], g.hT[:, k, :], k == 0, k == KC - 1, reads=[ku, ("hT", k)], writes=[PK(pu)])
            sg = g.SG[f % 2]
            s.op("act", lambda E, sg=sg, pg=pg: E.activation(out=sg[:, 0:T], in_=pst[pg][:, 0:T], func=AF.Silu), reads=[PK(pg)], writes=[("sg", f % 2)])
            s.op("dve", lambda E, sg=sg, pu=pu, f=f: E.tensor_tensor(out=g.GTt[:, f, :], in0=pst[pu][:, 0:T], in1=sg[:, 0:T], op=ALU.mult),
                 reads=[PK(pu), ("sg", f % 2)], writes=[("gT", f)] + (["alias_gt"] if f == 0 else []))
        stop_if("f_gu")
        for dblk in range(D // 512):
            load_gsl(gi, dblk)
            for q in range(NQ):
                wdb, kd = load_wd(wd, dblk, q)
                for sub in range(ns):
                    bank = 4 + sub
                    for fi in range(11):
                        f = q * 11 + fi
                        mm(pst[bank][:, 0:512], g.GTt[:, f, sub * 128:(sub + 1) * 128], wdb[:, fi, :], q == 0 and fi == 0, q == NQ - 1 and fi == 10,
                           reads=[kd, ("gT", f)], writes=[PK(bank)])
            for sub in range(ns):
                resid_update(4 + sub, sub, dblk, gi)

    def conv_chunk(bank, ch, nseg, segw, T, post, postkey, cengine):
        i = g.prec % 2
        g.prec += 1
        pre = g.PRE[i][:, 0:nseg * (segw + 4)].rearrange("p (s w) -> p s w", s=nseg)
        ca = g.CA[i].rearrange("p (s w) -> p s w", s=nseg)
        s.op("act", lambda E: E.activation(out=pre[:, :, 2:2 + segw], in_=pst[bank][:, 0:T].rearrange("p (s w) -> p s w", s=nseg), func=AF.Copy),
             reads=[PK(bank)], writes=[("pre", i)])
        E_ = cengine
        s.op(E_, lambda E: E.tensor_scalar_mul(out=ca, in0=pre[:, :, 0:segw], scalar1=COLS[:, C_CW + ch:C_CW + ch + 1]),
             reads=[("pre", i), "cols"], writes=[("ca", i)])
        for t in range(1, 5):
            cw = COLS[:, C_CW + t * 48 + ch:C_CW + t * 48 + ch + 1]
            s.op("dve", lambda E, t=t, cw=cw: E.scalar_tensor_tensor(out=ca, in0=pre[:, :, t:t + segw], scalar=cw, in1=ca, op0=ALU.mult, op1=ALU.add),
                 reads=[("pre", i), ("ca", i), "cols"], writes=[("ca", i)])
        s.op("act", lambda E: E.activation(out=post, in_=g.CA[i][:, 0:T], func=AF.Silu, bias=COLS[:, C_CB + ch:C_CB + ch + 1]),
             reads=[("ca", i), "cols"], writes=[postkey])

    def phaseA_alloc(T, need_h2):
        alloc_common(T)
        ns = g.ns
        o_ = AR.off
        flat = AR.bf(max(FFC * T, ns * 5120, 8192))
        g.GTraw = arena_t[:, o_:o_ + 4096]
        g.GTt = flat[:, 0:FFC * T].rearrange("p (f t) -> p f t", f=FFC)
        g.XHT = flat[:, 0:ns * 5120].rearrange("p (s c) -> p s c", s=ns)
        g.SG = [AR.bf(T) for _ in range(2)]
        g.PRE = [AR.bf(T + 64) for _ in range(2)]
        g.CA = [AR.f32(T) for _ in range(2)]
        g.POST = [AR.bf(T) for _ in range(4)]
        g.prec = 0
        g.postc = 0
        g.DT = AR.f32(ns * 128).rearrange("p (s c) -> p s c", s=ns)
        g.AA = AR.f32(ns * 128).rearrange("p (s c) -> p s c", s=ns)
        g.WGT = AR.f32(128)
        g.DEC = AR.f32(128)
        g.WX = [AR.bf(512) for _ in range(2)]
        g.wxc = 0
        g.SST = [AR.f32(512) for _ in range(2)]
        g.sstc = 0
        g.HS = [AR.bf(512) for _ in range(2)]
        g.hsc = 0
        g.H2 = AR.f32(4096) if need_h2 else None

    def zero_pads(nseg, segw):
        for i in range(2):
            s.op("pool", lambda E, i=i: E.memset(g.PRE[i], 0.0), writes=[("pre", i)])

    def phaseA_tile(src_ap, tok0, mode, gi, iaF, iaM, nseg, segw, own_tok0=None, sub_order=None):
        T, ns = g.T, g.ns
        dma("sp", g.X, src_ap[tok0:tok0 + T, :].rearrange("(s p) d -> p s d", p=128), writes=[("X", sub) for sub in range(ns)])
        stop_if("t_load")
        ffn(iaF, gi, "f1g", "f1u", "f1d")
        stop_if("t_ffn")
        if mode == "own":
            dma("sp", x1_d[own_tok0:own_tok0 + T, :].rearrange("(s p) d -> p s d", p=128), g.X, reads=[("X", sub) for sub in range(ns)], writes=["x1d"])
        norm_to_hT(iaM)
        if mode == "own":
            dma("sp", hT_d[:, :, own_tok0:own_tok0 + T], g.hT, reads=[("hT", c) for c in range(KC)], writes=["hTd"])
        stop_if("t_norm")
        dfr = Defer()
        for ch in range(40):
            dfr.tick()
            wblk, wk = load_wblk("win", BLK_XB + ch)
            b = newbank()
            for k in range(KC):
                mm(pst[b][:, 0:T], wblk[:, k, :], g.hT[:, k, :], k == 0, k == KC - 1, reads=[wk, ("hT", k)], writes=[PK(b)])
            pi = g.postc % 4
            g.postc += 1
            post = g.POST[pi]
            conv_chunk(b, ch, nseg, segw, T, post, ("post", pi), "pool" if ch % 2 else "dve")

            def tr(ch=ch, pi=pi, post=post):
                tb = newbank()
                for sub in range(ns):
                    s.op("pe", lambda E, tb=tb, sub=sub, post=post: E.transpose(psb[tb][:, sub * 128:(sub + 1) * 128], post[:, sub * 128:(sub + 1) * 128], IDB),
                         reads=[("post", pi), "idb"], writes=[PK(tb)])
                s.op("act", lambda E, tb=tb, ch=ch: E.activation(out=g.XHT[:, :, ch * 128:(ch + 1) * 128], in_=psb[tb][:, 0:T].rearrange("p (s c) -> p s c", s=ns), func=AF.Copy),
                     reads=[PK(tb)], writes=[("xht", ch)])
                if mode == "own" and ch >= 32:
                    dma("sp", BT_d[ch - 32, :, own_tok0:own_tok0 + T], post, reads=[("post", pi)], writes=["BTd"])
            dfr.push(tr, 3)
        dfr.flush()
        if mode == "own":
            dma("sp", xh_d[own_tok0:own_tok0 + T, :].rearrange("(s p) c -> p s c", p=128), g.XHT[:, :, 0:4096],
                reads=[("xht", ch) for ch in range(32)] + ["alias_gt"], writes=["xhd"])
        stop_if("t_xb")
        wblk, wk = load_wblk("wdt", 0)
        for sub in range(ns):
            b = newbank()
            for k in range(KC):
                mm(pst[b][:, 0:128], g.hT[:, k, sub * 128:(sub + 1) * 128], wblk[:, k, :], k == 0, k == KC - 1, reads=[wk, ("hT", k)], writes=[PK(b)])
            s.op("dve", lambda E, b=b, sub=sub: E.tensor_tensor(out=g.DT[:, sub, :], in0=pst[b][:, 0:128], in1=DTB, op=ALU.add), reads=[PK(b), "dtb"], writes=[("dt", sub)])
            s.op("act", lambda E, sub=sub: E.activation(out=g.DT[:, sub, :], in_=g.DT[:, sub, :], func=AF.Exp), reads=[("dt", sub)], writes=[("dt", sub)])
            s.op("act", lambda E, sub=sub: E.activation(out=g.DT[:, sub, :], in_=g.DT[:, sub, :], func=AF.Ln, bias=1.0), reads=[("dt", sub)], writes=[("dt", sub)])
            s.op("dve", lambda E, sub=sub: E.tensor_tensor(out=g.AA[:, sub, :], in0=g.DT[:, sub, :], in1=ANEG, op=ALU.mult), reads=[("dt", sub), "aneg"], writes=[("aa", sub)])
        if mode == "own":
            dv = dtA_d[own_tok0:own_tok0 + T, :].rearrange("(s p) c -> p s c", p=128)
            dma("sp", dv[:, :, 0:128], g.DT, reads=[("dt", sub) for sub in range(ns)], writes=["dtAd"])
            dma("sp", dv[:, :, 128:256], g.AA, reads=[("aa", sub) for sub in range(ns)], writes=["dtAd2"])
        stop_if("t_dt")
        passes = {"ctx": [(0, list(range(ns)), "chainF"), (1, list(range(ns))[::-1], "chainB")],
                  "other": [(1, list(range(ns))[::-1], "chainB")],
                  "own": [(0, list(range(ns)), "chainF"), (1, list(range(ns)), "store")]}[mode]
        for (dr, subs, act) in passes:
            Hbuf = HB_ if not (mode == "ctx" and dr == 0) else g.H2
            hkey = "HB" if Hbuf is HB_ else "H2"
            for sub in subs:
                b = newbank()
                lhs = GT_ if dr == 0 else LT_
                mm(pst[b][:, 0:64], lhs, g.AA[:, sub, dr * 64:(dr + 1) * 64], True, True, reads=["const", ("aa", sub)], writes=[PK(b)])
                mm(pst[b][:, 64:128], ONES, g.AA[:, sub, dr * 64:(dr + 1) * 64], True, True, reads=["const", ("aa", sub)], writes=[PK(b)])
                s.op("act", lambda E, b=b: E.activation(out=g.WGT, in_=pst[b][:, 0:128], func=AF.Exp), reads=[PK(b)], writes=["wgt"])
                s.op("dve", lambda E, sub=sub, dr=dr: E.tensor_tensor(out=g.DEC[:, 0:64], in0=g.WGT[:, 0:64], in1=g.DT[:, sub, dr * 64:(dr + 1) * 64], op=ALU.mult),
                     reads=["wgt", ("dt", sub)], writes=["dec"])
                if act == "store":
                    c = own_tok0 // 128 + sub
                    dma("sp", decb_d[c], g.WGT[:, 64:128], reads=["wgt"], writes=["decbd"])
                if act == "chainF" and mode == "own":
                    c = own_tok0 // 128 + sub
                for gg in range(8):
                    wi = g.wxc % 2
                    g.wxc += 1
                    wx = g.WX[wi]
                    s.op("pool", lambda E, wx=wx, sub=sub, gg=gg: E.tensor_tensor(
                        out=wx.rearrange("p (h d) -> p h d", h=8), in0=g.XHT[:, sub, gg * 512:(gg + 1) * 512].rearrange("p (h d) -> p h d", h=8),
                        in1=g.DEC[:, gg * 8:(gg + 1) * 8].unsqueeze(2).to_broadcast([128, 8, 64]), op=ALU.mult),
                        reads=[("xht", gg * 4 + q) for q in range(4)] + ["dec", "alias_gt"], writes=[("wx", wi)])
                    sb_ = newbank()
                    mm(pst[sb_][:, 0:512], g.XHT[:, sub, 4096 + gg * 128:4096 + (gg + 1) * 128], wx, True, True,
                       reads=[("xht", 32 + gg), ("wx", wi), "alias_gt"], writes=[PK(sb_)])
                    hs = Hbuf[:, gg * 512:(gg + 1) * 512]
                    if act == "store":
                        si = g.sstc % 2
                        g.sstc += 1
                        s.op("act", lambda E, si=si, sb_=sb_: E.activation(out=g.SST[si], in_=pst[sb_][:, 0:512], func=AF.Copy), reads=[PK(sb_)], writes=[("sst", si)])
                        dma("sp", Sb_d[c, :, gg * 512:(gg + 1) * 512], g.SST[si], reads=[("sst", si)], writes=["Sbd"])
                    else:
                        if act == "chainF" and mode == "own":
                            hi = g.hsc % 2
                            g.hsc += 1
                            s.op("act", lambda E, hi=hi, hs=hs: E.activation(out=g.HS[hi], in_=hs, func=AF.Copy), reads=[(hkey, gg)], writes=[("hs", hi)])
                            dma("sp", Hent_d[0, c, :, gg * 512:(gg + 1) * 512], g.HS[hi], reads=[("hs", hi)], writes=["Hentd"])
                        s.op("pool", lambda E, hs=hs, gg=gg: E.tensor_tensor(
                            out=hs.rearrange("p (h d) -> p h d", h=8), in0=hs.rearrange("p (h d) -> p h d", h=8),
                            in1=g.WGT[:, 64 + gg * 8:64 + (gg + 1) * 8].unsqueeze(2).to_broadcast([128, 8, 64]), op=ALU.mult),
                            reads=[(hkey, gg), "wgt"], writes=[(hkey, gg)])
                        s.op("dve", lambda E, hs=hs, sb_=sb_: E.tensor_tensor(out=hs, in0=hs, in1=pst[sb_][:, 0:512], op=ALU.add),
                             reads=[(hkey, gg), PK(sb_)], writes=[(hkey, gg)])

    HKEYS = [("HB", gg) for gg in range(8)]
    AR.off = persist_mark
    phaseA_alloc(CTX, True)
    zero_pads(1, CTX)
    s.op("pool", lambda E: E.memset(HB_, 0.0), writes=HKEYS)
    s.op("pool", lambda E: E.memset(g.H2, 0.0), writes=[("H2", gg) for gg in range(8)])
    phaseA_tile(ctx_in, 0, "ctx", 3, 6, 8, 1, CTX)
    dma("sp", Hsave_d, g.H2, reads=[("H2", gg) for gg in range(8)], writes=["hsave"])
    s.barrier()
    stop_if("A0")
    AR.off = persist_mark
    phaseA_alloc(TA, False)
    zero_pads(TA // SEGW, SEGW)
    for t in reversed(range(TOKH // TA)):
        phaseA_tile(x_in, TOKH + t * TA, "other", 0, 0, 2, TA // SEGW, SEGW)
    s.barrier()
    dma("sp", g.GTraw, Hsave_d, reads=["hsave"], writes=["h2tmp"])
    dma("sp", Hsave_d, HB_, reads=HKEYS, writes=["hsave"])
    s.op("dve", lambda E: E.tensor_copy(out=HB_, in_=g.GTraw), reads=["h2tmp"], writes=HKEYS)
    s.barrier()
    stop_if("AO")
    for t in range(TOKH // TA):
        phaseA_tile(x_in, t * TA, "own", 0, 0, 2, TA // SEGW, SEGW, own_tok0=t * TA)
    s.barrier()
    stop_if("AW")
    AR.off = persist_mark
    dma("sp", HB_, Hsave_d, writes=HKEYS)
    SBUFS = [AR.f32(4096) for _ in range(2)]
    DCB = [AR.f32(64) for _ in range(2)]
    HSB = [AR.bf(4096) for _ in range(2)]
    for n_, c in enumerate(reversed(range(NCH))):
        i = n_ % 2
        dma("sp", SBUFS[i], Sb_d[c], reads=["Sbd"], writes=[("sbuf", i)])
        dma("sp", DCB[i], decb_d[c], reads=["decbd"], writes=[("dcb", i)])
        s.op("act", lambda E, i=i: E.activation(out=HSB[i], in_=HB_, func=AF.Copy), reads=HKEYS, writes=[("hsb", i)])
        dma("sp", Hent_d[1, c], HSB[i], reads=[("hsb", i)], writes=["Hentd"])
        for hf in range(2):
            eng = "dve" if hf == 0 else "pool"
            hs = HB_[:, hf * 2048:(hf + 1) * 2048]
            s.op(eng, lambda E, hs=hs, i=i, hf=hf: E.tensor_tensor(out=hs.rearrange("p (h d) -> p h d", h=32), in0=hs.rearrange("p (h d) -> p h d", h=32),
                                                                in1=DCB[i][:, hf * 32:(hf + 1) * 32].unsqueeze(2).to_broadcast([128, 32, 64]), op=ALU.mult),
                 reads=[("dcb", i)] + HKEYS[hf * 4:(hf + 1) * 4], writes=HKEYS[hf * 4:(hf + 1) * 4])
            s.op(eng, lambda E, hs=hs, i=i, hf=hf: E.tensor_tensor(out=hs, in0=hs, in1=SBUFS[i][:, hf * 2048:(hf + 1) * 2048], op=ALU.add),
                 reads=[("sbuf", i)] + HKEYS[hf * 4:(hf + 1) * 4], writes=HKEYS[hf * 4:(hf + 1) * 4])
    s.barrier()

    stop_if("SC")
    AR.off = persist_small
    alloc_common(TB, wdn=32 * 128)
    T, ns = g.T, g.ns
    g.PRE = [AR.bf(T + 64) for _ in range(2)]
    g.CA = [AR.f32(T) for _ in range(2)]
    g.prec = 0
    zero_pads(T // SEGW, SEGW)
    DTA = AR.f32(ns * 256).rearrange("p (s c) -> p s c", s=ns)
    EACS = AR.f32(ns * 128).rearrange("p (s c) -> p s c", s=ns)
    MRG = AR.bf(KC * T).rearrange("p (k t) -> p k t", k=KC)
    GBUF = [AR.bf(T) for _ in range(2)]
    WST = AR.bf(8 * 128).rearrange("p (g i) -> p g i", g=8)
    BSB = AR.f32(8 * 128).rearrange("p (g i) -> p g i", g=8)
    BS2 = AR.f32(16 * 128).rearrange("p (f i) -> p f i", f=16)
    ONB = AR.bf(128)
    BNS = AR.f32(4 * 6)
    BNA = AR.f32(2)
    LNS = AR.f32(4)
    TSB = [AR.f32(T) for _ in range(2)]
    regR = AR.off
    XHG = [AR.bf(ns * 512).rearrange("p (s c) -> p s c", s=ns) for _ in range(3)]
    BTG = [AR.bf(T) for _ in range(3)]
    HEG = [[[AR.bf(512) for _ in range(ns)] for _ in range(2)] for _ in range(3)]
    SSN = [AR.f32(512) for _ in range(3)]
    CTG = [AR.bf(T) for _ in range(3)]
    ZS = [AR.bf(ns * 512).rearrange("p (s c) -> p s c", s=ns) for _ in range(3)]
    CBM = [AR.bf(128) for _ in range(2)]
    LH8 = [AR.f32(1024) for _ in range(2)]
    E8 = [AR.bf(1024) for _ in range(2)]
    M8 = [AR.bf(1024) for _ in range(2)]
    XDT = [AR.bf(512) for _ in range(2)]
    TY = [AR.f32(512) for _ in range(3)]
    YT = AR.f32(512)
    YN = [AR.bf(512) for _ in range(3)]
    YNT = AR.bf(32 * T).rearrange("p (k t) -> p k t", k=32)
    endR1 = AR.off
    AR.off = regR
    UT = AR.bf(KC * T).rearrange("p (k t) -> p k t", k=KC)
    VF = AR.f32(ns * D).rearrange("p (s d) -> p s d", s=ns)
    VNB = AR.bf(ns * D).rearrange("p (s d) -> p s d", s=ns)
    AR.off = max(AR.off, endR1)
    dma("pool", WST, wsT_in, writes=["wst"])
    dma("sp", BSB.rearrange("p g i -> p (g i)"), bs_in.partition_broadcast(128), writes=["bsb"])
    s.op("dve", lambda E: E.tensor_copy(out=ONB, in_=ONES), reads=["const"], writes=["onb"])
    LNWC = COLS[:, C_LN:C_LN + 16]
    LNBC = COLS[:, C_LN + 16:C_LN + 32]
    for hh in range(2):
        b = newbank()
        mm(pst[b][:, 0:512], ONB, WST[:, hh * 4:(hh + 1) * 4, :].rearrange("p g i -> p (g i)"), True, True, reads=["onb", "wst"], writes=[PK(b)])
        for f4 in range(8):
            fc = hh * 8 + f4
            gq = f4 // 2
            s.op("dve", lambda E, b=b, fc=fc, gq=gq: E.scalar_tensor_tensor(out=BS2[:, fc, :], in0=pst[b][:, gq * 128:(gq + 1) * 128], scalar=LNBC[:, fc:fc + 1],
                                                                         in1=BSB[:, fc // 2, :], op0=ALU.mult, op1=ALU.add),
                 reads=[PK(b), "cols", "bsb"], writes=["bs2"])
    BGC = COLS[:, C_BG:C_BG + 32]

    for tix in range(TOKH // TB):
        tok0 = tix * TB
        dma("sp", g.hT, hT_d[:, :, tok0:tok0 + T], reads=["hTd"], writes=[("hT", c) for c in range(KC)])
        dma("sp", DTA, dtA_d[tok0:tok0 + T, :].rearrange("(s p) c -> p s c", p=128), reads=["dtAd", "dtAd2"], writes=["dta"])
        for sub in range(ns):
            b = newbank()
            mm(pst[b][:, 0:64], LE, DTA[:, sub, 128:192], True, True, reads=["const", "dta"], writes=[PK(b)])
            mm(pst[b][:, 64:128], GE, DTA[:, sub, 192:256], True, True, reads=["const", "dta"], writes=[PK(b)])
            s.op("act", lambda E, b=b, sub=sub: E.activation(out=EACS[:, sub, :], in_=pst[b][:, 0:128], func=AF.Exp), reads=[PK(b)], writes=[("eacs", sub)])
        def ssd_proj(gg):
            gi2 = gg % 3
            dma("sp", XHG[gi2], xh_d[tok0:tok0 + T, gg * 512:(gg + 1) * 512].rearrange("(s p) c -> p s c", p=128), reads=["xhd"], writes=[("xhg", gi2)])
            dma("sp", BTG[gi2], BT_d[gg, :, tok0:tok0 + T], reads=["BTd"], writes=[("btg", gi2)])
            for dr in range(2):
                for sub in range(ns):
                    dma("sp", HEG[gi2][dr][sub], Hent_d[dr, tok0 // 128 + sub, :, gg * 512:(gg + 1) * 512], reads=["Hentd"], writes=[("heg", gi2, dr, sub)])
            dma("sp", SSN[gi2], ssm_norm[gg * 512:(gg + 1) * 512].partition_broadcast(128), writes=[("ssn", gi2)])
            wblk, wk = load_wblk("win", BLK_C + gg)
            b = newbank()
            for k in range(KC):
                mm(pst[b][:, 0:T], wblk[:, k, :], g.hT[:, k, :], k == 0, k == KC - 1, reads=[wk, ("hT", k)], writes=[PK(b)])
            conv_chunk(b, 40 + gg, T // SEGW, SEGW, T, CTG[gi2], ("ctg", gi2), "pool")
            zb = [4 + sub for sub in range(ns)]
            for j in range(4):
                wblk, wk = load_wblk("win", BLK_Z + gg * 4 + j)
                for sub in range(ns):
                    for k in range(KC):
                        mm(pst[zb[sub]][:, j * 128:(j + 1) * 128], g.hT[:, k, sub * 128:(sub + 1) * 128], wblk[:, k, :], k == 0, k == KC - 1,
                           reads=[wk, ("hT", k)], writes=[PK(zb[sub])])
            for sub in range(ns):
                s.op("act", lambda E, sub=sub, gi2=gi2: E.activation(out=ZS[gi2][:, sub, :], in_=pst[zb[sub]][:, 0:512], func=AF.Silu),
                     reads=[PK(zb[sub])], writes=[("zs", gi2, sub)])

        def stageA(gg, sub, dr):
            gi2 = gg % 3
            if dr == 0:
                b = newbank()
                mm(pst[b][:, 0:128], BTG[gi2][:, sub * 128:(sub + 1) * 128], CTG[gi2][:, sub * 128:(sub + 1) * 128], True, True,
                   reads=[("btg", gi2), ("ctg", gi2)], writes=[PK(b)])
                s.op("dve", lambda E, b=b: E.tensor_tensor(out=CBM[0], in0=pst[b][:, 0:128], in1=LE, op=ALU.mult), reads=[PK(b), "const"], writes=[("cbm", 0)])
                s.op("dve", lambda E, b=b: E.tensor_tensor(out=CBM[1], in0=pst[b][:, 0:128], in1=GE, op=ALU.mult), reads=[PK(b), "const"], writes=[("cbm", 1)])
            UTm, TRI = (GT_, LE) if dr == 0 else (LT_, GE)
            acol = DTA[:, sub, 128 + dr * 64 + gg * 8:128 + dr * 64 + gg * 8 + 8]
            dcol = DTA[:, sub, dr * 64 + gg * 8:dr * 64 + gg * 8 + 8]
            s.op("pool", lambda E: E.tensor_tensor(
                out=LH8[dr].rearrange("p (h j) -> p h j", h=8), in0=acol.unsqueeze(2).to_broadcast([128, 8, 128]),
                in1=UTm.unsqueeze(1).to_broadcast([128, 8, 128]), op=ALU.mult), reads=["dta", "const"], writes=[("lh8", dr)])
            db = [newbank(), newbank()]
            for h in range(8):
                mm(pst[db[h // 4]][:, (h % 4) * 128:(h % 4 + 1) * 128], LH8[dr][:, h * 128:(h + 1) * 128], TRI, True, True,
                   reads=[("lh8", dr), "const"], writes=[PK(db[h // 4])])
            for hh in range(2):
                s.op("act", lambda E, hh=hh: E.activation(out=E8[dr][:, hh * 512:(hh + 1) * 512], in_=pst[db[hh]][:, 0:512], func=AF.Exp),
                     reads=[PK(db[hh])], writes=[("e8", dr, hh)])
            s.op("dve", lambda E: E.tensor_tensor(out=M8[dr].rearrange("p (h i) -> p h i", h=8), in0=E8[dr].rearrange("p (h i) -> p h i", h=8),
                                                  in1=CBM[dr].unsqueeze(1).to_broadcast([128, 8, 128]), op=ALU.mult),
                 reads=[("e8", dr, 0), ("e8", dr, 1), ("cbm", dr)], writes=[("m8", dr)])
            s.op("pool", lambda E: E.tensor_tensor(
                out=XDT[dr].rearrange("p (h d) -> p h d", h=8), in0=XHG[gi2][:, sub, :].rearrange("p (h d) -> p h d", h=8),
                in1=dcol.unsqueeze(2).to_broadcast([128, 8, 64]), op=ALU.mult), reads=[("xhg", gi2), "dta"], writes=[("xdt", dr)])

        def stageB(gg, sub, dr):
            gi2 = gg % 3
            yb = 6 + (sub % 2)
            ecol = EACS[:, sub, dr * 64 + gg * 8:dr * 64 + gg * 8 + 8]
            for h in range(8):
                mm(pst[yb][:, h * 64:(h + 1) * 64], M8[dr][:, h * 128:(h + 1) * 128], XDT[dr][:, h * 64:(h + 1) * 64], dr == 0 and h == 0, dr == 1 and h == 7,
                   reads=[("m8", dr), ("xdt", dr)], writes=[PK(yb)])
            ob = newbank()
            mm(pst[ob][:, 0:512], CTG[gi2][:, sub * 128:(sub + 1) * 128], HEG[gi2][dr][sub], True, True,
               reads=[("ctg", gi2), ("heg", gi2, dr, sub)], writes=[PK(ob)])
            s.op("dve", lambda E: E.tensor_tensor(
                out=TY[dr].rearrange("p (h d) -> p h d", h=8), in0=pst[ob][:, 0:512].rearrange("p (h d) -> p h d", h=8),
                in1=ecol.unsqueeze(2).to_broadcast([128, 8, 64]), op=ALU.mult), reads=[PK(ob), ("eacs", sub)], writes=[("ty", dr)])

        yn_ctr = [0]

        def epilogue(gg, sub):
            gi2 = gg % 3
            yb = 6 + (sub % 2)
            yi = yn_ctr[0] % 3
            yn_ctr[0] += 1
            yn = YN[yi]
            s.op("pool", lambda E: E.tensor_tensor(
                out=TY[2].rearrange("p (h d) -> p h d", h=8), in0=XHG[gi2][:, sub, :].rearrange("p (h d) -> p h d", h=8),
                in1=DSK[:, gg * 8:(gg + 1) * 8].unsqueeze(2).to_broadcast([128, 8, 64]), op=ALU.mult), reads=[("xhg", gi2), "dsk"], writes=[("ty", 2)])
            s.op("dve", lambda E: E.tensor_tensor(out=YT, in0=pst[yb][:, 0:512], in1=TY[0], op=ALU.add), reads=[PK(yb), ("ty", 0)], writes=["yt"])
            s.op("pool", lambda E: E.tensor_tensor(out=YT, in0=YT, in1=TY[1], op=ALU.add), reads=["yt", ("ty", 1)], writes=["yt"])
            s.op("pool", lambda E: E.tensor_tensor(out=YT, in0=YT, in1=TY[2], op=ALU.add), reads=["yt", ("ty", 2)], writes=["yt"])
            s.op("pool", lambda E: E.tensor_tensor(out=YT, in0=YT, in1=ZS[gi2][:, sub, :], op=ALU.mult), reads=["yt", ("zs", gi2, sub)], writes=["yt"])
            rstd_of(YT, 0, 512, "yt")
            s.op("dve", lambda E: E.scalar_tensor_tensor(out=yn, in0=YT, scalar=RSTD[:, 0:1], in1=SSN[gi2], op0=ALU.mult, op1=ALU.mult),
                 reads=["yt", ("rstd", 0), ("ssn", gi2)], writes=[("yn", yi)])

            def tr():
                tb = newbank()
                for q in range(4):
                    s.op("pe", lambda E, q=q: E.transpose(psb[tb][:, q * 128:(q + 1) * 128], yn[:, q * 128:(q + 1) * 128], IDB), reads=[("yn", yi), "idb"], writes=[PK(tb)])
                s.op("act", lambda E: E.activation(out=YNT[:, gg * 4:(gg + 1) * 4, sub * 128:(sub + 1) * 128],
                                                   in_=psb[tb][:, 0:512].rearrange("p (q c) -> p q c", q=4), func=AF.Copy),
                     reads=[PK(tb)], writes=[("ynt", gg)])
            return tr

        dfr = Defer()
        ssd_proj(0)
        prev = None
        for gg in range(8):
            for sub in range(ns):
                for dr in range(2):
                    if sub == 0 and dr == 0 and gg + 1 < 8:
                        ssd_proj(gg + 1)
                    stageA(gg, sub, dr)
                    if prev is not None:
                        stageB(*prev)
                        if prev[2] == 1:
                            dfr.push(epilogue(prev[0], prev[1]), 2)
                    prev = (gg, sub, dr)
                    dfr.tick()
        stageB(*prev)
        dfr.push(epilogue(prev[0], prev[1]), 1)
        dfr.flush()
        for dc in range(16):
            wbb, kb = load_wb32(dc)
            b = newbank()
            for k in range(32):
                mm(pst[b][:, 0:T], wbb[:, k, :], YNT[:, k, :], k == 0, k == 31, reads=[kb, ("ynt", k // 4)], writes=[PK(b)])
            wblk, wk = load_wblk("win", BLK_GB + dc)
            b2 = newbank()
            for k in range(KC):
                mm(pst[b2][:, 0:T], wblk[:, k, :], g.hT[:, k, :], k == 0, k == KC - 1, reads=[wk, ("hT", k)], writes=[PK(b2)])
            gi_ = dc % 2
            s.op("act", lambda E, b2=b2, gi_=gi_, dc=dc: E.activation(out=GBUF[gi_], in_=pst[b2][:, 0:T], func=AF.Sigmoid, bias=BGC[:, 16 + dc:17 + dc]),
                 reads=[PK(b2), "cols"], writes=[("gbuf", gi_)])
            s.op("dve", lambda E, b=b, gi_=gi_, dc=dc: E.tensor_tensor(out=MRG[:, dc, :], in0=pst[b][:, 0:T], in1=GBUF[gi_], op=ALU.mult),
                 reads=[PK(b), ("gbuf", gi_)], writes=[("mrg", dc)])
        s.barrier()
        for fc in range(16):
            wblk, wk = load_wblk("win", BLK_U + fc)
            b = newbank()
            for k in range(KC):
                mm(pst[b][:, 0:T], wblk[:, k, :], g.hT[:, k, :], k == 0, k == KC - 1, reads=[wk, ("hT", k)], writes=[PK(b)])
            s.op("act", lambda E, b=b, fc=fc: E.activation(out=UT[:, fc, :], in_=pst[b][:, 0:T], func=AF.Gelu), reads=[PK(b)], writes=[("ut", fc)])
        for jg in range(4):
            vb = [4 + sub for sub in range(ns)]
            for j in range(4):
                wblk, wk = load_wblk("win", BLK_V + jg * 4 + j)
                for sub in range(ns):
                    for k in range(KC):
                        mm(pst[vb[sub]][:, j * 128:(j + 1) * 128], g.hT[:, k, sub * 128:(sub + 1) * 128], wblk[:, k, :], k == 0, k == KC - 1,
                           reads=[wk, ("hT", k)], writes=[PK(vb[sub])])
            for sub in range(ns):
                s.op("act", lambda E, sub=sub, jg=jg: E.activation(out=VF[:, sub, jg * 512:(jg + 1) * 512], in_=pst[vb[sub]][:, 0:512], func=AF.Gelu),
                     reads=[PK(vb[sub])], writes=[("vf", sub)])
        for sub in range(ns):
            for q in range(4):
                s.op("dve", lambda E, sub=sub, q=q: E.bn_stats(out=BNS[:, q * 6:(q + 1) * 6], in_=VF[:, sub, q * 512:(q + 1) * 512]), reads=[("vf", sub)], writes=["bns"])
            s.op("dve", lambda E: E.bn_aggr(out=BNA, in_=BNS.rearrange("p (q s) -> p q s", q=4)), reads=["bns"], writes=["bna"])
            s.op("act", lambda E: E.activation(out=LNS[:, 0:1], in_=BNA[:, 1:2], func=AF.Sqrt, bias=EPSC), reads=["bna", "eps"], writes=["lns"])
            s.op("dve", lambda E: E.reciprocal(out=LNS[:, 1:2], in_=LNS[:, 0:1]), reads=["lns"], writes=["lns"])
            s.op("dve", lambda E: E.scalar_tensor_tensor(out=LNS[:, 2:3], in0=BNA[:, 0:1], scalar=-1.0, in1=LNS[:, 1:2], op0=ALU.mult, op1=ALU.mult),
                 reads=["lns", "bna"], writes=["lns"])
            s.op("act", lambda E, sub=sub: E.activation(out=VNB[:, sub, :], in_=VF[:, sub, :], func=AF.Identity, scale=LNS[:, 1:2], bias=LNS[:, 2:3]),
                 reads=["lns", ("vf", sub)], writes=[("vnb", sub)])
        for fc in range(16):
            b = newbank()
            for sub in range(ns):
                mm(pst[b][:, sub * 128:(sub + 1) * 128], VNB[:, sub, fc * 128:(fc + 1) * 128], WST[:, fc // 2, :], True, True,
                   reads=[("vnb", sub), "wst"], writes=[PK(b)])
            ti = fc % 2
            s.op("dve", lambda E, b=b, ti=ti, fc=fc: E.scalar_tensor_tensor(out=TSB[ti].rearrange("p (s i) -> p s i", s=ns), in0=pst[b][:, 0:T].rearrange("p (s i) -> p s i", s=ns),
                                                                          scalar=LNWC[:, fc:fc + 1], in1=BS2[:, fc, :].unsqueeze(1).to_broadcast([128, ns, 128]), op0=ALU.mult, op1=ALU.add),
                 reads=[PK(b), "bs2", "cols"], writes=[("tsb", ti)])
            s.op("pool", lambda E, ti=ti, fc=fc: E.tensor_tensor(out=UT[:, fc, :], in0=UT[:, fc, :], in1=TSB[ti], op=ALU.mult),
                 reads=[("tsb", ti), ("ut", fc)], writes=[("ut", fc)])
        for dc in range(16):
            wblk, wk = load_wblk("wa", dc)
            b = newbank()
            for k in range(KC):
                mm(pst[b][:, 0:T], wblk[:, k, :], UT[:, k, :], k == 0, k == KC - 1, reads=[wk, ("ut", k)], writes=[PK(b)])
            wblk2, wk2 = load_wblk("win", BLK_GA + dc)
            b2 = newbank()
            for k in range(KC):
                mm(pst[b2][:, 0:T], wblk2[:, k, :], g.hT[:, k, :], k == 0, k == KC - 1, reads=[wk2, ("hT", k)], writes=[PK(b2)])
            gi_ = dc % 2
            s.op("act", lambda E, b2=b2, gi_=gi_, dc=dc: E.activation(out=GBUF[gi_], in_=pst[b2][:, 0:T], func=AF.Sigmoid, bias=BGC[:, dc:dc + 1]),
                 reads=[PK(b2), "cols"], writes=[("gbuf", gi_)])
            ti = dc % 2
            s.op("dve", lambda E, b=b, gi_=gi_, ti=ti: E.tensor_tensor(out=TSB[ti], in0=pst[b][:, 0:T], in1=GBUF[gi_], op=ALU.mult),
                 reads=[PK(b), ("gbuf", gi_)], writes=[("tsb", ti)])
            s.op("pool", lambda E, ti=ti, dc=dc: E.tensor_tensor(out=MRG[:, dc, :], in0=MRG[:, dc, :], in1=TSB[ti], op=ALU.add),
                 reads=[("tsb", ti), ("mrg", dc)], writes=[("mrg", dc)])
        dma("sp", g.X, x1_d[tok0:tok0 + T, :].rearrange("(s p) d -> p s d", p=128), reads=["x1d"], writes=[("X", sub) for sub in range(ns)])
        for dblk in range(4):
            load_gsl(1, dblk)
            ob = [4 + sub for sub in range(ns)]
            for j in range(4):
                wblk, wk = load_wblk("wo", dblk * 4 + j)
                for sub in range(ns):
                    for k in range(KC):
                        mm(pst[ob[sub]][:, j * 128:(j + 1) * 128], MRG[:, k, sub * 128:(sub + 1) * 128], wblk[:, k, :], k == 0, k == KC - 1,
                           reads=[wk, ("mrg", k)], writes=[PK(ob[sub])])
            for sub in range(ns):
                resid_update(ob[sub], sub, dblk, 1)
        dma("sp", x1_d[tok0:tok0 + T, :].rearrange("(s p) d -> p s d", p=128), g.X, reads=[("X", sub) for sub in range(ns)], writes=["x1d"])
        s.barrier()

    stop_if("B")
    AR.off = persist_small
    alloc_common(TC)
    T, ns = g.T, g.ns
    g.GTt = AR.bf(FFC * T).rearrange("p (f t) -> p f t", f=FFC)
    g.SG = [AR.bf(T) for _ in range(2)]
    NFB = AR.f32(D)
    OUTB = AR.f32(D)
    dma("sp", NFB, norm_final.partition_broadcast(128), writes=["nfb"])
    outs = []
    for tix in range(TOKH // TC):
        tok0 = tix * TC
        dma("sp", g.X, x1_d[tok0:tok0 + T, :].rearrange("(s p) d -> p s d", p=128), reads=["x1d"], writes=[("X", sub) for sub in range(ns)])
        ffn(4, 2, "f2g", "f2u", "f2d")
        for sub in range(ns):
            rstd_of(g.X[:, sub, :], sub, D, ("X", sub))
            s.op("dve", lambda E, sub=sub: E.scalar_tensor_tensor(out=OUTB, in0=g.X[:, sub, :], scalar=RSTD[:, sub:sub + 1], in1=NFB, op0=ALU.mult, op1=ALU.mult),
                 reads=[("X", sub), ("rstd", sub), "nfb"], writes=["outb"])
            outs.append(dma("sp", out_d[tok0 + sub * 128:tok0 + (sub + 1) * 128, :], OUTB, reads=["outb"], writes=["outd"]))
    s.final_wait("sp", outs)
    stats = s.emit(st)
    st.close()
    return nc, stats


FULL_CFG = dict(FF=5632, TOKH=4096, TA=512, TB=256, TC=512)


def make_in_maps(inp, cfg):
    TOKH = cfg["TOKH"]
    x = np.asarray(inp["x"], np.float32)
    ctx = np.asarray(inp["ctx"], np.float32)
    B = x.shape[0]
    f32 = lambda a: np.ascontiguousarray(np.asarray(a, np.float32))
    consts = make_consts()
    shared = {
        "w_mod": f32(inp["w_mod"][0]), "b_mod": f32(inp["b_mod"][0]),
        "norms": f32(np.concatenate([inp["norm_ffn1"][0], inp["norm_mix"][0], inp["norm_ffn2"][0]]).reshape(48, 128)),
        "norm_final": f32(inp["norm_final"]),
        "ffn1_gate": f32(inp["ffn1_gate"][0]), "ffn1_up": f32(inp["ffn1_up"][0]), "ffn1_down": f32(inp["ffn1_down"][0]),
        "ffn2_gate": f32(inp["ffn2_gate"][0]), "ffn2_up": f32(inp["ffn2_up"][0]), "ffn2_down": f32(inp["ffn2_down"][0]),
        "w_in": f32(inp["w_in"][0]), "w_a": f32(inp["w_a"][0]), "w_b": f32(inp["w_b"][0]), "w_out": f32(inp["w_out"][0]),
        "b_gate": f32(inp["b_gate"][0]).reshape(32, 128), "gmlp_ln_wb": f32(np.concatenate([inp["gmlp_ln_w"][0], inp["gmlp_ln_b"][0]]).reshape(32, 128)),
        "conv_b": f32(inp["conv_b"][0]).reshape(48, 128), "d_skip": f32(inp["d_skip"][0]), "ssm_norm": f32(inp["ssm_norm"][0]),
        "consts": consts,
    }
    win = shared["w_in"]
    ws = np.asarray(inp["gmlp_ws"][0], np.float32)
    bs = np.asarray(inp["gmlp_bs"][0], np.float32)
    cw = np.asarray(inp["conv_w"][0], np.float32)
    al = np.asarray(inp["a_log"][0], np.float32)
    db = np.asarray(inp["dt_bias"][0], np.float32)
    wdt = win[:, OFF_DT:OFF_DT + 128]
    per_s = []
    for s_ in range(2):
        if s_ == 0:
            d = {"w_dt": f32(wdt), "gmlp_wsT": f32(ws.transpose(2, 0, 1)), "gmlp_bs": f32(bs.reshape(-1)),
                 "conv_w": f32(cw.reshape(240, 128)), "a_log": f32(al.reshape(-1)), "dt_bias": f32(db.reshape(-1))}
        else:
            wsf = ws[:, ::-1, ::-1]
            d = {"w_dt": f32(np.concatenate([wdt[:, 64:128], wdt[:, 0:64]], axis=1)),
                 "gmlp_wsT": f32(wsf.transpose(2, 0, 1)), "gmlp_bs": f32(bs[:, ::-1].reshape(-1)),
                 "conv_w": f32(cw[::-1].reshape(240, 128)), "a_log": f32(al[::-1].reshape(-1)), "dt_bias": f32(db[::-1].reshape(-1))}
        per_s.append(d)
    in_maps = []
    cc = np.asarray(inp["c_ctx"], np.float32)
    for core in range(2 * B):
        b, s_ = core // 2, core % 2
        xb = x[b] if s_ == 0 else x[b, ::-1]
        cb = ctx[b] if s_ == 0 else ctx[b, ::-1]
        m = dict(shared)
        m.update(per_s[s_])
        m["x"] = f32(xb)
        m["ctx"] = f32(cb)
        m["cvec"] = f32(np.concatenate([np.asarray(inp["c"], np.float32)[b], cc]).reshape(32, 128))
        in_maps.append(m)
    return in_maps


_CACHE = {}


def run(inp, cfg):
    key = tuple(sorted(cfg.items()))
    if key not in _CACHE:
        _CACHE[key] = build(cfg)[0]
    nc = _CACHE[key]
    in_maps = make_in_maps(inp, cfg)
    res = run_bass_kernel_spmd(nc, in_maps, core_ids=list(range(len(in_maps))))
    TOKH = cfg["TOKH"]
    B = len(in_maps) // 2
    out = np.empty((B, 2 * TOKH, D), np.float32)
    for core in range(2 * B):
        b, s_ = core // 2, core % 2
        o = np.asarray(res.results[core]["out"], np.float32)
        if s_ == 0:
            out[b, 0:TOKH] = o
        else:
            out[b, TOKH:] = o[::-1]
    return out


def kernel(**inputs):
    return run(inputs, FULL_CFG)
```

```python
import numpy as np
from contextlib import ExitStack
import concourse.bass as bass
import concourse.mybir as mybir
from concourse.bass_utils import run_bass_kernel_spmd

F32 = mybir.dt.float32
BF16 = mybir.dt.bfloat16
AF = mybir.ActivationFunctionType
ALU = mybir.AluOpType
P = 128
D = 2048
KC = 16
CTX = 256
EPS = 1e-6
OFF_U, OFF_V, OFF_Z, OFF_XB, OFF_C, OFF_DT, OFF_GATE, IN_W = 0, 2048, 4096, 8192, 13312, 14336, 14464, 18560
BLK_U, BLK_V, BLK_Z, BLK_XB, BLK_C, BLK_GA, BLK_GB = 0, 16, 32, 64, 104, 113, 129

EPOCH = 8000
NSLOT = 12
DEBUG = False
NAMES = {}


class Op:
    __slots__ = ("eng", "fn", "deps", "signal", "is_dma", "slot", "dval", "sigcnt", "line")

    def __init__(self, eng, fn, is_dma):
        self.eng = eng
        self.fn = fn
        self.deps = []
        self.signal = False
        self.is_dma = is_dma
        self.slot = None
        self.dval = None
        self.sigcnt = None
        self.line = None


class _Rec:
    def __getattr__(self, name):
        def f(*a, **k):
            self.call = (name, a, k)
            return None
        return f


class Sched:
    ENGS = ("pe", "act", "dve", "pool", "sp")

    def __init__(self, nc):
        self.nc = nc
        self.ops = {e: [] for e in self.ENGS}
        self.last_w = {}
        self.readers = {}
        self.dma_cnt = {e: 0 for e in self.ENGS}
        self.dma_last = {e: [None] * NSLOT for e in self.ENGS}

    def op(self, eng, fn, reads=(), writes=(), dma=False):
        rec = _Rec()
        fn(rec)
        _n, _a, _k = rec.call
        o = Op(eng, (lambda E, _n=_n, _a=_a, _k=_k: getattr(E, _n)(*_a, **_k)), dma)
        if DEBUG:
            import sys as _sys
            f = _sys._getframe(1)
            ln = []
            while f is not None and len(ln) < 4:
                ln.append(f.f_lineno)
                f = f.f_back
            o.line = ln
        deps = []
        for k in reads:
            w = self.last_w.get(k)
            if w is not None:
                deps.append(w)
            if isinstance(k, tuple) and k[0] == "ps":
                rl = self.readers.get(k)
                if rl:
                    deps.extend(v for e_, v in rl.items() if e_ != eng)
        for k in writes:
            w = self.last_w.get(k)
            if w is not None:
                deps.append(w)
            rl = self.readers.get(k)
            if rl:
                deps.extend(rl.values())
        if dma:
            c = self.dma_cnt[eng]
            o.slot = c % NSLOT
            o.dval = 16 * (c // NSLOT + 1)
            prev = self.dma_last[eng][o.slot]
            if prev is not None:
                deps.append(prev)
            self.dma_last[eng][o.slot] = o
            self.dma_cnt[eng] = c + 1
        seen = set()
        for d in deps:
            if d is o or id(d) in seen:
                continue
            seen.add(id(d))
            if (not d.is_dma) and d.eng == eng and eng == "pe":
                continue
            o.deps.append(d)
            if not d.is_dma:
                d.signal = True
        for k in writes:
            self.last_w[k] = o
            self.readers[k] = {}
        for k in reads:
            rl = self.readers.setdefault(k, {})
            rl[("dma", id(o)) if dma else eng] = o
        self.ops[eng].append(o)
        return o

    def barrier(self):
        lasts = []
        for e in self.ENGS:
            for o in reversed(self.ops[e]):
                if not o.is_dma and o.fn is not None:
                    lasts.append(o)
                    break
            if e == "pool":
                continue
            for o in self.dma_last[e]:
                if o is not None:
                    lasts.append(o)
        for e in self.ENGS:
            o = Op(e, None, False)
            for d in lasts:
                if d.eng == e and not d.is_dma:
                    continue
                o.deps.append(d)
                if not d.is_dma:
                    d.signal = True
            self.ops[e].append(o)
        self.last_w = {k: v for k, v in self.last_w.items() if isinstance(k, tuple) and k[0] == "W"}
        self.readers = {}

    def final_wait(self, eng, ops):
        o = Op(eng, None, False)
        for d in ops:
            o.deps.append(d)
            if not d.is_dma:
                d.signal = True
        self.ops[eng].append(o)

    def emit(self, stack):
        nc = self.nc
        nsig = {}
        for e in self.ENGS:
            c = 0
            for o in self.ops[e]:
                if o.signal and not o.is_dma:
                    c += 1
                    o.sigcnt = c
            nsig[e] = c
        esem = {}
        for e in self.ENGS:
            ne = (nsig[e] + EPOCH - 1) // EPOCH
            esem[e] = [stack.enter_context(nc.semaphore(f"s_{e}_{i}")) for i in range(ne)]
        dsem = {}
        for e in self.ENGS:
            n = min(self.dma_cnt[e], NSLOT)
            dsem[e] = [stack.enter_context(nc.semaphore(f"d_{e}_{i}")) for i in range(n)]
        engobj = {"pe": "tensor", "act": "scalar", "dve": "vector", "pool": "gpsimd", "sp": "sync"}
        stats = {}

        def run(ename, E):
            waited = {}
            maxep = {}
            nw = 0
            for o in self.ops[ename]:
                for d in o.deps:
                    if d.is_dma:
                        sem = dsem[d.eng][d.slot]
                        val = d.dval
                        key = ("d", d.eng, d.slot)
                    else:
                        ep = (d.sigcnt - 1) // EPOCH
                        val = (d.sigcnt - 1) % EPOCH + 1
                        sem = esem[d.eng][ep]
                        key = ("e", d.eng, ep)
                        if maxep.get(d.eng, -1) > ep:
                            continue
                        maxep[d.eng] = ep
                    if waited.get(key, 0) >= val:
                        continue
                    waited[key] = val
                    E.wait_ge(sem, val)
                    nw += 1
                if o.fn is None:
                    continue
                ins = o.fn(E)
                if DEBUG:
                    try:
                        NAMES[ins.ins.name] = o.line
                    except Exception:
                        pass
                if o.is_dma:
                    ins.then_inc(dsem[ename][o.slot], 16)
                elif o.signal:
                    ep = (o.sigcnt - 1) // EPOCH
                    ins.then_inc(esem[ename][ep], 1)
            stats[ename] = (len(self.ops[ename]), nw)

        with nc.Block() as block:
            for ename in self.ENGS:
                getattr(block, engobj[ename])(lambda E, ename=ename: run(ename, E))
        return stats


class Defer:
    def __init__(self):
        self.q = []

    def push(self, fn, delay):
        self.q.append([delay, fn])

    def tick(self):
        for it in self.q:
            it[0] -= 1
        ready = [it for it in self.q if it[0] <= 0]
        self.q = [it for it in self.q if it[0] > 0]
        for it in ready:
            it[1]()

    def flush(self):
        q, self.q = self.q, []
        for it in q:
            it[1]()


class Arena:
    def __init__(self, ap, nwords):
        self.ap = ap
        self.n = nwords
        self.off = 0

    def f32(self, n):
        o = self.off
        self.off += n
        assert self.off <= self.n, ("SBUF arena overflow", self.off, self.n)
        return self.ap[:, o:o + n]

    def bf(self, n):
        w = (n + 1) // 2
        o = self.off
        self.off += w
        assert self.off <= self.n, ("SBUF arena overflow", self.off, self.n)
        return self.ap[:, o:o + w].bitcast(BF16)[:, 0:n]


NCONST = 6 * 128


def make_consts():
    i = np.arange(128)
    p, q = i[:, None], i[None, :]
    mats = [p == q, p <= q, p >= q, p > q, p < q, np.ones((128, 128), bool)]
    return np.concatenate([m.astype(np.float32) for m in mats], axis=1)


class _Stop(Exception):
    pass


def build(cfg):
    nc_holder = {}
    try:
        return _build(cfg, nc_holder)
    except _Stop:
        s, st, nc = nc_holder["s"], nc_holder["st"], nc_holder["nc"]
        s.barrier()
        stats = s.emit(st)
        st.close()
        return nc, stats


def _build(cfg, nc_holder):
    FF = cfg["FF"]
    FFC = FF // 128
    TOKH = cfg["TOKH"]
    TA = cfg["TA"]
    TB = cfg["TB"]
    TC = cfg["TC"]
    SEGW = 64
    NCH = TOKH // 128
    NQ = FFC // 11
    assert FFC % 11 == 0 and TOKH % TA == 0 and TOKH % TB == 0 and TOKH % TC == 0
    nc = bass.Bass("TRN2", target_bir_lowering=False)

    def din(name, shape):
        return nc.dram_tensor(name, list(shape), F32, kind="ExternalInput").ap()

    x_in = din("x", [2 * TOKH, D])
    ctx_in = din("ctx", [CTX, D])
    cvec = din("cvec", [32, 128])
    w_mod = din("w_mod", [D, 9 * D])
    b_mod = din("b_mod", [9 * D])
    norms = din("norms", [48, 128])
    norm_final = din("norm_final", [D])
    Wsrc = {
        "f1g": din("ffn1_gate", [D, FF]), "f1u": din("ffn1_up", [D, FF]), "f1d": din("ffn1_down", [FF, D]),
        "f2g": din("ffn2_gate", [D, FF]), "f2u": din("ffn2_up", [D, FF]), "f2d": din("ffn2_down", [FF, D]),
        "win": din("w_in", [D, IN_W]), "wdt": din("w_dt", [D, 128]),
        "wa": din("w_a", [D, D]), "wb": din("w_b", [2 * D, D]), "wo": din("w_out", [D, D]),
    }
    b_gate = din("b_gate", [32, 128])
    ln_wb = din("gmlp_ln_wb", [32, 128])
    wsT_in = din("gmlp_wsT", [128, 8, 128])
    bs_in = din("gmlp_bs", [8 * 128])
    conv_w = din("conv_w", [240, 128])
    conv_b = din("conv_b", [48, 128])
    a_log = din("a_log", [128])
    dt_bias = din("dt_bias", [128])
    d_skip = din("d_skip", [64])
    ssm_norm = din("ssm_norm", [2 * D])
    consts_in = din("consts", [128, NCONST])
    out_d = nc.dram_tensor("out", [TOKH, D], F32, kind="ExternalOutput").ap()

    def dscr(name, shape, dt):
        return nc.dram_tensor(name, list(shape), dt).ap()

    Wbf = {}
    for nm in ("f1g", "f1u", "f2g", "f2u"):
        Wbf[nm] = dscr("bf_" + nm, [FFC, 128, KC, 128], BF16)
    for nm in ("f1d", "f2d"):
        Wbf[nm] = dscr("bf_" + nm, [D // 512, NQ, 128, 11, 512], BF16)
    Wbf["win"] = dscr("bf_win", [IN_W // 128, 128, KC, 128], BF16)
    Wbf["wdt"] = dscr("bf_wdt", [1, 128, KC, 128], BF16)
    Wbf["wa"] = dscr("bf_wa", [16, 128, KC, 128], BF16)
    Wbf["wo"] = dscr("bf_wo", [16, 128, KC, 128], BF16)
    Wbf["wb"] = dscr("bf_wb", [16, 128, 32, 128], BF16)
    x1_d = dscr("x1_d", [TOKH, D], F32)
    hT_d = dscr("hT_d", [128, KC, TOKH], BF16)
    xh_d = dscr("xh_d", [TOKH, 4096], BF16)
    BT_d = dscr("BT_d", [8, 128, TOKH], BF16)
    dtA_d = dscr("dtA_d", [TOKH, 256], F32)
    Sb_d = dscr("Sb_d", [NCH, 128, 4096], F32)
    decb_d = dscr("decb_d", [NCH, 128, 64], F32)
    Hent_d = dscr("Hent_d", [2, NCH, 128, 4096], BF16)
    gates_d = dscr("gates_d", [4, 128, D], F32)
    Hsave_d = dscr("Hsave_d", [128, 4096], F32)

    st = ExitStack()
    NW = 52000
    arena_t = st.enter_context(nc.sbuf_tensor("arena", [128, NW], F32))
    AR = Arena(arena_t, NW)
    pst = [st.enter_context(nc.psum_tensor(f"ps{i}", [128, 512], F32)) for i in range(8)]
    psb = [t.bitcast(BF16) for t in pst]
    s = Sched(nc)
    nc_holder.update(s=s, st=st, nc=nc)
    STOP = cfg.get("stop")

    def stop_if(name):
        if STOP == name:
            raise _Stop()
    bankctr = [0]

    def newbank(lo=0, hi=4):
        b = lo + bankctr[0] % (hi - lo)
        bankctr[0] += 1
        return b

    def PK(b):
        return ("ps", b)

    def mm(out, lhsT, rhs, start, stop, reads, writes):
        s.op("pe", lambda E: E.matmul(out, lhsT=lhsT, rhs=rhs, start=start, stop=stop), reads, writes)

    def dma(eng, out, in_, reads=(), writes=()):
        return s.op(eng, lambda E: E.dma_start(out=out, in_=in_), reads, writes, dma=True)

    CONST = AR.f32(NCONST)
    IDF, LE, GE, GT_, LT_, ONES = [CONST[:, i * 128:(i + 1) * 128] for i in range(6)]
    IDB = AR.bf(128)
    COLS = AR.f32(528)
    MCOL = AR.f32(192)
    AB = AR.f32(160)
    EPSC = AR.f32(1)
    SSQ = AR.f32(4)
    RMS = AR.f32(4)
    RSTD = AR.f32(4)
    DTB = AR.f32(128)
    ANEG = AR.f32(128)
    DSK = AR.f32(64)
    persist_small = AR.off
    HB_ = AR.f32(4096)
    persist_mark = AR.off
    C_CV, C_N, C_BM, C_CW, C_CB, C_BG, C_LN = 0, 32, 80, 176, 416, 464, 496

    dma("sp", CONST, consts_in, writes=["const"])
    s.op("dve", lambda E: E.tensor_copy(out=IDB, in_=IDF), reads=["const"], writes=["idb"])
    s.op("pool", lambda E: E.memset(EPSC, EPS), writes=["eps"])
    dma("sp", DTB, dt_bias.partition_broadcast(128), writes=["dtb"])
    dma("sp", ANEG, a_log.partition_broadcast(128), writes=["aneg"])
    dma("sp", DSK, d_skip.partition_broadcast(128), writes=["dsk"])
    s.op("act", lambda E: E.activation(out=ANEG, in_=ANEG, func=AF.Exp), reads=["aneg"], writes=["aneg"])
    s.op("dve", lambda E: E.tensor_scalar_mul(out=ANEG, in0=ANEG, scalar1=-1.0), reads=["aneg"], writes=["aneg"])

    def conv_k(nm, nblk, kc=KC):
        src = Wsrc[nm].rearrange("(k p) (b c) -> b p k c", p=128, c=128)
        for b0 in range(nblk):
            dma("pool", Wbf[nm][b0], src[b0], writes=[("W", nm, b0)])

    def conv_d(nm):
        src = Wsrc[nm].rearrange("(q f p) (d c) -> d q p f c", p=128, f=11, c=512)
        for d_ in range(D // 512):
            for q_ in range(NQ):
                dma("pool", Wbf[nm][d_, q_], src[d_, q_], writes=[("W", nm, d_, q_)])

    def wkey(nm, b):
        return ("W", nm, b)

    ROWS = AR.f32(5 * 128)
    rowsrc = [(cvec, 32), (norms, 48), None, (conv_w, 240), (conv_b, 48), (b_gate, 32)]
    bm2 = b_mod.rearrange("(j c p) -> j c p", c=16, p=128)
    pieces = [(cvec, 0, 32), (norms, 0, 48)]
    for j in (0, 1, 3, 4, 6, 7):
        pieces.append((bm2[j], 0, 16))
    pieces += [(conv_w, 0, 240), (conv_b, 0, 48), (b_gate, 0, 32), (ln_wb, 0, 32)]
    r = 0
    for (ap, r0, n) in pieces:
        done = 0
        while done < n:
            t = r // 128
            ro = r % 128
            m = min(n - done, 128 - ro)
            dma("sp", ROWS[ro:ro + m, t * 128:(t + 1) * 128], ap[done:done + m, :], writes=[("rows", t)])
            done += m
            r += m
    assert r == 528
    for t in range(5):
        n = min(128, 528 - t * 128)
        b = newbank()
        s.op("pe", lambda E, t=t, n=n, b=b: E.transpose(pst[b][:, 0:n], ROWS[0:n, t * 128:(t + 1) * 128], IDF[0:n, 0:n]),
             reads=[("rows", t), "const"], writes=[PK(b)])
        s.op("dve", lambda E, t=t, n=n, b=b: E.tensor_copy(out=COLS[:, t * 128:t * 128 + n], in_=pst[b][:, 0:n]),
             reads=[PK(b)], writes=["cols"])
    AR.off = persist_mark

    conv_k("f1g", FFC)
    conv_k("f1u", FFC)
    conv_d("f1d")
    m0 = AR.off
    SC = AR.bf(32)
    SCB = AR.bf(2 * KC * 128)
    s.op("act", lambda E: E.activation(out=SC, in_=COLS[:, C_CV:C_CV + 32], func=AF.Silu), reads=["cols"], writes=["sc"])
    SCBv = SCB.rearrange("p (t k m) -> p t k m", t=2, k=KC)
    for t in range(2):
        s.op("dve", lambda E, t=t: E.tensor_copy(out=SCBv[:, t], in_=SC[:, t * 16:(t + 1) * 16].unsqueeze(2).to_broadcast([128, KC, 128])),
             reads=["sc"], writes=[("scb", t)])
    SCv = SC.rearrange("p (t k) -> p t k", t=2)
    WM = [AR.bf(KC * 128).rearrange("p (k c) -> p k c", k=KC) for _ in range(4)]
    wmsrc = w_mod.rearrange("(k p) (b c) -> b p k c", p=128, c=128)
    wmc = [0]

    def load_wm(blk):
        i = wmc[0] % 4
        wmc[0] += 1
        dma("pool", WM[i], wmsrc[blk], writes=[("wm", i)])
        return WM[i], ("wm", i)

    mb = newbank()
    for ji, j in enumerate((0, 1, 3, 4, 6, 7)):
        for c in range(16):
            wblk, wk = load_wm(j * 16 + c)
            idx = ji * 16 + c
            for k in range(KC):
                mm(pst[mb][:, idx * 2:idx * 2 + 2], wblk[:, k, :], SCv[:, :, k], k == 0, k == KC - 1,
                   reads=[wk, "sc"], writes=[PK(mb)])
    bmcol = COLS[:, C_BM:C_BM + 96]
    s.op("dve", lambda E: E.tensor_tensor(out=MCOL.rearrange("p (i t) -> p i t", t=2), in0=pst[mb][:, 0:192].rearrange("p (i t) -> p i t", t=2),
                                          in1=bmcol.unsqueeze(2).to_broadcast([128, 96, 2]), op=ALU.add),
         reads=[PK(mb), "cols"], writes=["mcol"])
    MC = MCOL.rearrange("p (j c t) -> p j c t", j=6, c=16)
    ABv = AB.rearrange("p (i c) -> p i c", c=16)
    NRM = COLS[:, C_N:C_N + 48].rearrange("p (i c) -> p i c", c=16)

    def mk_ab(ia, nidx, jscale, jshift, t):
        s.op("dve", lambda E: E.tensor_scalar_add(out=ABv[:, ia], in0=MC[:, jscale, :, t], scalar1=1.0), reads=["mcol"], writes=["ab"])
        s.op("dve", lambda E: E.tensor_tensor(out=ABv[:, ia], in0=ABv[:, ia], in1=NRM[:, nidx], op=ALU.mult), reads=["ab", "cols"], writes=["ab"])
        s.op("dve", lambda E: E.tensor_copy(out=ABv[:, ia + 1], in_=MC[:, jshift, :, t]), reads=["mcol"], writes=["ab"])

    mk_ab(0, 0, 1, 0, 0)
    mk_ab(2, 1, 3, 2, 0)
    mk_ab(4, 2, 5, 4, 0)
    mk_ab(6, 0, 1, 0, 1)
    mk_ab(8, 1, 3, 2, 1)
    BMB = [AR.f32(128) for _ in range(2)]
    GST = [AR.f32(128) for _ in range(2)]
    gc = 0
    for (j, t, gi, scl) in ((2, 0, 0, 0.5), (5, 0, 1, 1.0), (8, 0, 2, 0.5), (2, 1, 3, 0.5)):
        for c in range(16):
            wblk, wk = load_wm(j * 16 + c)
            b = newbank()
            for k in range(KC):
                mm(pst[b][:, 0:128], SCBv[:, t, k, :], wblk[:, k, :], k == 0, k == KC - 1, reads=[wk, ("scb", t)], writes=[PK(b)])
            i = gc % 2
            gc += 1
            dma("sp", BMB[i], b_mod[j * D + c * 128:j * D + (c + 1) * 128].partition_broadcast(128), writes=[("bmb", i)])
            s.op("dve", lambda E, i=i, b=b: E.tensor_tensor(out=GST[i], in0=pst[b][:, 0:128], in1=BMB[i], op=ALU.add),
                 reads=[PK(b), ("bmb", i)], writes=[("gst", i)])
            s.op("act", lambda E, i=i, scl=scl: E.activation(out=GST[i], in_=GST[i], func=AF.Copy, scale=scl), reads=[("gst", i)], writes=[("gst", i)])
            dma("sp", gates_d[gi, :, c * 128:(c + 1) * 128], GST[i], reads=[("gst", i)], writes=[("gates", gi)])

    conv_k("win", IN_W // 128)
    conv_k("wdt", 1)
    conv_k("wa", 16)
    conv_k("wo", 16)
    srcwb = Wsrc["wb"].rearrange("(k p) (b c) -> b p k c", p=128, c=128)
    for b0 in range(16):
        dma("pool", Wbf["wb"][b0], srcwb[b0], writes=[("W", "wb", b0)])
    conv_k("f2g", FFC)
    conv_k("f2u", FFC)
    conv_d("f2d")
    s.barrier()
    stop_if("setup")
    AR.off = persist_mark

    class G:
        pass

    g = G()

    def alloc_common(T, wdn=11 * 512):
        ns = T // 128
        g.T = T
        g.ns = ns
        g.X = AR.f32(ns * D).rearrange("p (s d) -> p s d", s=ns)
        g.XN = AR.bf(D)
        g.hT = AR.bf(KC * T).rearrange("p (k t) -> p k t", k=KC)
        g.WP = [AR.bf(KC * 128).rearrange("p (k c) -> p k c", k=KC) for _ in range(6)]
        g.WD = [AR.bf(wdn) for _ in range(2)]
        g.wpc = 0
        g.wdc = 0
        g.TMP = [AR.f32(512) for _ in range(2)]
        g.tmpc = 0
        g.GSL = [AR.f32(512) for _ in range(2)]
        g.gslc = 0

    def load_wblk(nm, b):
        i = g.wpc % len(g.WP)
        g.wpc += 1
        dma("sp", g.WP[i], Wbf[nm][b], reads=[wkey(nm, b)], writes=[("wp", i)])
        return g.WP[i], ("wp", i)

    def load_wd(nm, d_, q):
        i = g.wdc % 2
        g.wdc += 1
        ap = g.WD[i].rearrange("p (f c) -> p f c", f=11)
        dma("sp", ap, Wbf[nm][d_, q], reads=[("W", nm, d_, q)], writes=[("wd", i)])
        return ap, ("wd", i)

    def load_wb32(b):
        i = g.wdc % 2
        g.wdc += 1
        ap = g.WD[i][:, 0:32 * 128].rearrange("p (k c) -> p k c", k=32)
        dma("sp", ap, Wbf["wb"][b], reads=[("W", "wb", b)], writes=[("wd", i)])
        return ap, ("wd", i)

    def rstd_of(src_ap, sub, n, xkey, junk=None, junk_keys=None):
        s.op("pool", lambda E: E.memset(SSQ[:, sub:sub + 1], 0.0), writes=[("ssq", sub)])
        jo = g.XN[:, 0:n] if junk is None else junk
        jk = ["xn"] if junk is None else junk_keys
        s.op("act", lambda E: E.activation(out=jo, in_=src_ap, func=AF.Square, accum_out=SSQ[:, sub:sub + 1]),
             reads=[xkey], writes=jk + [("ssq", sub)])
        s.op("act", lambda E: E.activation(out=RMS[:, sub:sub + 1], in_=SSQ[:, sub:sub + 1], func=AF.Sqrt, scale=1.0 / n, bias=EPSC),
             reads=[("ssq", sub), "eps"], writes=[("rms", sub)])
        s.op("dve", lambda E: E.reciprocal(out=RSTD[:, sub:sub + 1], in_=RMS[:, sub:sub + 1]), reads=[("rms", sub)], writes=[("rstd", sub)])

    def norm_to_hT(ia):
        Acol, Bcol = ABv[:, ia], ABv[:, ia + 1]
        ev = 0
        nj = (D + g.T - 1) // g.T
        for sub in range(g.ns):
            rstd_of(g.X[:, sub, :], sub, D, ("X", sub), junk=g.JUNK, junk_keys=[("gT", f) for f in range(min(nj, FFC))] + (["alias_gt"] if sub == 0 else []))
            s.op("dve", lambda E, sub=sub: E.tensor_scalar_mul(out=g.XN, in0=g.X[:, sub, :], scalar1=RSTD[:, sub:sub + 1]),
                 reads=[("X", sub), ("rstd", sub)], writes=["xn"])
            for half in range(2):
                b = newbank()
                for c8 in range(8):
                    c = half * 8 + c8
                    s.op("pe", lambda E, b=b, c8=c8, c=c: E.transpose(psb[b][:, c8 * 128:(c8 + 1) * 128], g.XN[:, c * 128:(c + 1) * 128], IDB),
                         reads=["xn", "idb"], writes=[PK(b)])
                for c8 in range(8):
                    c = half * 8 + c8
                    o_ap = g.hT[:, c, sub * 128:(sub + 1) * 128]
                    i_ap = psb[b][:, c8 * 128:(c8 + 1) * 128]
                    if half == 0:
                        s.op("act", lambda E, o_ap=o_ap, i_ap=i_ap, c=c: E.activation(out=o_ap, in_=i_ap, func=AF.Identity, scale=Acol[:, c:c + 1], bias=Bcol[:, c:c + 1]),
                             reads=[PK(b), "ab"], writes=[("hT", c)])
                    else:
                        s.op("dve", lambda E, o_ap=o_ap, i_ap=i_ap, c=c: E.tensor_scalar(out=o_ap, in0=i_ap, scalar1=Acol[:, c:c + 1], scalar2=Bcol[:, c:c + 1], op0=ALU.mult, op1=ALU.add),
                             reads=[PK(b), "ab"], writes=[("hT", c)])
                    ev += 1

    def resid_update(bank, sub, dblk, gi):
        i = g.tmpc % 2
        g.tmpc += 1
        tmp = g.TMP[i]
        s.op("dve", lambda E: E.tensor_tensor(out=tmp, in0=pst[bank][:, 0:512], in1=g.gsl, op=ALU.mult),
             reads=[PK(bank), g.gslk], writes=[("tmp", i)])
        xs = g.X[:, sub, dblk * 512:(dblk + 1) * 512]
        s.op("pool", lambda E: E.tensor_tensor(out=xs, in0=xs, in1=tmp, op=ALU.add), reads=[("tmp", i), ("X", sub)], writes=[("X", sub)])

    def load_gsl(gi, dblk):
        i = g.gslc % 2
        g.gslc += 1
        dma("sp", g.GSL[i], gates_d[gi, :, dblk * 512:(dblk + 1) * 512], reads=[("gates", gi)], writes=[("gsl", i)])
        g.gsl = g.GSL[i]
        g.gslk = ("gsl", i)

    def ffn(ia, gi, wg, wu, wd):
        T, ns = g.T, g.ns
        norm_to_hT(ia)
        stop_if("f_norm")
        for f in range(FFC):
            wgb, kg = load_wblk(wg, f)
            wub, ku = load_wblk(wu, f)
            pg = newbank()
            pu = newbank()
            for k in range(KC):
                mm(pst[pg][:, 0:T], wgb[:, k, :], g.hT[:, k, :], k == 0, k == KC - 1, reads=[kg, ("hT", k)], writes=[PK(pg)])
            for k in range(KC):
                mm(pst[pu][:, 0:T], wub[:, k, :], g.hT[:, k, :], k == 0, k == KC - 1, reads=[ku, ("hT", k)], writes=[PK(pu)])
            sg = g.SG[f % 2]
            s.op("act", lambda E, sg=sg, pg=pg: E.activation(out=sg[:, 0:T], in_=pst[pg][:, 0:T], func=AF.Silu), reads=[PK(pg)], writes=[("sg", f % 2)])
            s.op("dve", lambda E, sg=sg, pu=pu, f=f: E.tensor_tensor(out=g.GTt[:, f, :], in0=pst[pu][:, 0:T], in1=sg[:, 0:T], op=ALU.mult),
                 reads=[PK(pu), ("sg", f % 2)], writes=[("gT", f)] + (["alias_gt"] if f == 0 else []))
        stop_if("f_gu")
        for dblk in range(D // 512):
            load_gsl(gi, dblk)
            for q in range(NQ):
                wdb, kd = load_wd(wd, dblk, q)
                for sub in range(ns):
                    bank = 4 + sub
                    for fi in range(11):
                        f = q * 11 + fi
                        mm(pst[bank][:, 0:512], g.GTt[:, f, sub * 128:(sub + 1) * 128], wdb[:, fi, :], q == 0 and fi == 0, q == NQ - 1 and fi == 10,
                           reads=[kd, ("gT", f)], writes=[PK(bank)])
            for sub in range(ns):
                resid_update(4 + sub, sub, dblk, gi)

    def conv_chunk(bank, ch, nseg, segw, T, post, postkey, cengine):
        i = g.prec % 2
        g.prec += 1
        pre = g.PRE[i][:, 0:nseg * (segw + 4)].rearrange("p (s w) -> p s w", s=nseg)
        ca = g.CA[i].rearrange("p (s w) -> p s w", s=nseg)
        s.op("act", lambda E: E.activation(out=pre[:, :, 2:2 + segw], in_=pst[bank][:, 0:T].rearrange("p (s w) -> p s w", s=nseg), func=AF.Copy),
             reads=[PK(bank)], writes=[("pre", i)])
        E_ = cengine
        s.op(E_, lambda E: E.tensor_scalar_mul(out=ca, in0=pre[:, :, 0:segw], scalar1=COLS[:, C_CW + ch:C_CW + ch + 1]),
             reads=[("pre", i), "cols"], writes=[("ca", i)])
        for t in range(1, 5):
            cw = COLS[:, C_CW + t * 48 + ch:C_CW + t * 48 + ch + 1]
            s.op("dve", lambda E, t=t, cw=cw: E.scalar_tensor_tensor(out=ca, in0=pre[:, :, t:t + segw], scalar=cw, in1=ca, op0=ALU.mult, op1=ALU.add),
                 reads=[("pre", i), ("ca", i), "cols"], writes=[("ca", i)])
        s.op("act", lambda E: E.activation(out=post, in_=g.CA[i][:, 0:T], func=AF.Silu, bias=COLS[:, C_CB + ch:C_CB + ch + 1]),
             reads=[("ca", i), "cols"], writes=[postkey])

    def phaseA_alloc(T, need_h2):
        alloc_common(T)
        ns = g.ns
        o_ = AR.off
        flat = AR.bf(max(FFC * T, ns * 5120, 8192))
        g.GTraw = arena_t[:, o_:o_ + 4096]
        g.JUNK = flat[:, 0:D]
        g.GTt = flat[:, 0:FFC * T].rearrange("p (f t) -> p f t", f=FFC)
        g.XHT = flat[:, 0:ns * 5120].rearrange("p (s c) -> p s c", s=ns)
        g.SG = [AR.bf(T) for _ in range(2)]
        g.PRE = [AR.bf(T + 64) for _ in range(2)]
        g.CA = [AR.f32(T) for _ in range(2)]
        g.POST = [AR.bf(T) for _ in range(4)]
        g.prec = 0
        g.postc = 0
        g.DT = AR.f32(ns * 128).rearrange("p (s c) -> p s c", s=ns)
        g.AA = AR.f32(ns * 128).rearrange("p (s c) -> p s c", s=ns)
        g.WGT = AR.f32(128)
        g.DEC = AR.f32(128)
        g.WGT2 = AR.f32(128)
        g.DEC2 = AR.f32(128)
        g.WX = [AR.bf(512) for _ in range(2)]
        g.wxc = 0
        g.SST = [AR.f32(512) for _ in range(2)]
        g.sstc = 0
        g.HS = [AR.bf(512) for _ in range(2)]
        g.hsc = 0
        g.H2 = AR.f32(4096) if need_h2 else None

    def zero_pads(nseg, segw):
        for i in range(2):
            s.op("pool", lambda E, i=i: E.memset(g.PRE[i], 0.0), writes=[("pre", i)])

    def phaseA_tile(src_ap, tok0, mode, gi, iaF, iaM, nseg, segw, own_tok0=None, sub_order=None):
        T, ns = g.T, g.ns
        dma("sp", g.X, src_ap[tok0:tok0 + T, :].rearrange("(s p) d -> p s d", p=128), writes=[("X", sub) for sub in range(ns)])
        stop_if("t_load")
        ffn(iaF, gi, "f1g", "f1u", "f1d")
        stop_if("t_ffn")
        if mode == "own":
            dma("sp", x1_d[own_tok0:own_tok0 + T, :].rearrange("(s p) d -> p s d", p=128), g.X, reads=[("X", sub) for sub in range(ns)], writes=["x1d"])
        norm_to_hT(iaM)
        if mode == "own":
            dma("sp", hT_d[:, :, own_tok0:own_tok0 + T], g.hT, reads=[("hT", c) for c in range(KC)], writes=["hTd"])
        stop_if("t_norm")
        dfr = Defer()
        for ch in range(40):
            dfr.tick()
            wblk, wk = load_wblk("win", BLK_XB + ch)
            b = newbank()
            for k in range(KC):
                mm(pst[b][:, 0:T], wblk[:, k, :], g.hT[:, k, :], k == 0, k == KC - 1, reads=[wk, ("hT", k)], writes=[PK(b)])
            pi = g.postc % 4
            g.postc += 1
            post = g.POST[pi]
            conv_chunk(b, ch, nseg, segw, T, post, ("post", pi), "pool" if ch % 2 else "dve")

            def tr(ch=ch, pi=pi, post=post):
                tb = newbank()
                for sub in range(ns):
                    s.op("pe", lambda E, tb=tb, sub=sub, post=post: E.transpose(psb[tb][:, sub * 128:(sub + 1) * 128], post[:, sub * 128:(sub + 1) * 128], IDB),
                         reads=[("post", pi), "idb"], writes=[PK(tb)])
                s.op("act", lambda E, tb=tb, ch=ch: E.activation(out=g.XHT[:, :, ch * 128:(ch + 1) * 128], in_=psb[tb][:, 0:T].rearrange("p (s c) -> p s c", s=ns), func=AF.Copy),
                     reads=[PK(tb)], writes=[("xht", ch)])
                if mode == "own" and ch >= 32:
                    dma("sp", BT_d[ch - 32, :, own_tok0:own_tok0 + T], post, reads=[("post", pi)], writes=["BTd"])
            dfr.push(tr, 3)
        dfr.flush()
        if mode == "own":
            dma("sp", xh_d[own_tok0:own_tok0 + T, :].rearrange("(s p) c -> p s c", p=128), g.XHT[:, :, 0:4096],
                reads=[("xht", ch) for ch in range(32)] + ["alias_gt"], writes=["xhd"])
        stop_if("t_xb")
        wblk, wk = load_wblk("wdt", 0)
        for sub in range(ns):
            b = newbank()
            for k in range(KC):
                mm(pst[b][:, 0:128], g.hT[:, k, sub * 128:(sub + 1) * 128], wblk[:, k, :], k == 0, k == KC - 1, reads=[wk, ("hT", k)], writes=[PK(b)])
            s.op("dve", lambda E, b=b, sub=sub: E.tensor_tensor(out=g.DT[:, sub, :], in0=pst[b][:, 0:128], in1=DTB, op=ALU.add), reads=[PK(b), "dtb"], writes=[("dt", sub)])
            s.op("act", lambda E, sub=sub: E.activation(out=g.DT[:, sub, :], in_=g.DT[:, sub, :], func=AF.Exp), reads=[("dt", sub)], writes=[("dt", sub)])
            s.op("act", lambda E, sub=sub: E.activation(out=g.DT[:, sub, :], in_=g.DT[:, sub, :], func=AF.Ln, bias=1.0), reads=[("dt", sub)], writes=[("dt", sub)])
            s.op("dve", lambda E, sub=sub: E.tensor_tensor(out=g.AA[:, sub, :], in0=g.DT[:, sub, :], in1=ANEG, op=ALU.mult), reads=[("dt", sub), "aneg"], writes=[("aa", sub)])
        if mode == "own":
            dv = dtA_d[own_tok0:own_tok0 + T, :].rearrange("(s p) c -> p s c", p=128)
            dma("sp", dv[:, :, 0:128], g.DT, reads=[("dt", sub) for sub in range(ns)], writes=["dtAd"])
            dma("sp", dv[:, :, 128:256], g.AA, reads=[("aa", sub) for sub in range(ns)], writes=["dtAd2"])
        stop_if("t_dt")
        passes = {"ctx": [(0, list(range(ns)), "chainF"), (1, list(range(ns))[::-1], "chainB")],
                  "other": [(1, list(range(ns))[::-1], "chainB")],
                  "own": [(0, list(range(ns)), "chainF"), (1, list(range(ns)), "store")]}[mode]
        items = [(dr, sub, act) for (dr, subs, act) in passes for sub in subs]
        WG = [g.WGT, g.WGT2]
        DC = [g.DEC, g.DEC2]

        def st1(n):
            dr, sub, act = items[n]
            wgt, dec, i2 = WG[n % 2], DC[n % 2], n % 2
            b = newbank()
            lhs = GT_ if dr == 0 else LT_
            mm(pst[b][:, 0:64], lhs, g.AA[:, sub, dr * 64:(dr + 1) * 64], True, True, reads=["const", ("aa", sub)], writes=[PK(b)])
            mm(pst[b][:, 64:128], ONES, g.AA[:, sub, dr * 64:(dr + 1) * 64], True, True, reads=["const", ("aa", sub)], writes=[PK(b)])
            s.op("act", lambda E: E.activation(out=wgt, in_=pst[b][:, 0:128], func=AF.Exp), reads=[PK(b)], writes=[("wgt", i2)])
            s.op("dve", lambda E: E.tensor_tensor(out=dec[:, 0:64], in0=wgt[:, 0:64], in1=g.DT[:, sub, dr * 64:(dr + 1) * 64], op=ALU.mult),
                 reads=[("wgt", i2), ("dt", sub)], writes=[("dec", i2)])
            if act == "store":
                c = own_tok0 // 128 + sub
                dma("sp", decb_d[c], wgt[:, 64:128], reads=[("wgt", i2)], writes=["decbd"])

        def st2(n):
            dr, sub, act = items[n]
            wgt, dec, i2 = WG[n % 2], DC[n % 2], n % 2
            Hbuf = HB_ if not (mode == "ctx" and dr == 0) else g.H2
            hkey = "HB" if Hbuf is HB_ else "H2"
            c = (own_tok0 // 128 + sub) if mode == "own" else None
            for gg in range(8):
                wi = g.wxc % 2
                g.wxc += 1
                wx = g.WX[wi]
                s.op("pool", lambda E, wx=wx, gg=gg: E.tensor_tensor(
                    out=wx.rearrange("p (h d) -> p h d", h=8), in0=g.XHT[:, sub, gg * 512:(gg + 1) * 512].rearrange("p (h d) -> p h d", h=8),
                    in1=dec[:, gg * 8:(gg + 1) * 8].unsqueeze(2).to_broadcast([128, 8, 64]), op=ALU.mult),
                    reads=[("xht", gg * 4 + q) for q in range(4)] + [("dec", i2), "alias_gt"], writes=[("wx", wi)])
                sb_ = newbank()
                mm(pst[sb_][:, 0:512], g.XHT[:, sub, 4096 + gg * 128:4096 + (gg + 1) * 128], wx, True, True,
                   reads=[("xht", 32 + gg), ("wx", wi), "alias_gt"], writes=[PK(sb_)])
                hs = Hbuf[:, gg * 512:(gg + 1) * 512]
                if act == "store":
                    si = g.sstc % 2
                    g.sstc += 1
                    s.op("act", lambda E, si=si, sb_=sb_: E.activation(out=g.SST[si], in_=pst[sb_][:, 0:512], func=AF.Copy), reads=[PK(sb_)], writes=[("sst", si)])
                    dma("sp", Sb_d[c, :, gg * 512:(gg + 1) * 512], g.SST[si], reads=[("sst", si)], writes=["Sbd"])
                else:
                    if act == "chainF" and mode == "own":
                        hi = g.hsc % 2
                        g.hsc += 1
                        s.op("act", lambda E, hi=hi, hs=hs: E.activation(out=g.HS[hi], in_=hs, func=AF.Copy), reads=[(hkey, gg)], writes=[("hs", hi)])
                        dma("sp", Hent_d[0, c, :, gg * 512:(gg + 1) * 512], g.HS[hi], reads=[("hs", hi)], writes=["Hentd"])
                    s.op("pool", lambda E, hs=hs, gg=gg: E.tensor_tensor(
                        out=hs.rearrange("p (h d) -> p h d", h=8), in0=hs.rearrange("p (h d) -> p h d", h=8),
                        in1=wgt[:, 64 + gg * 8:64 + (gg + 1) * 8].unsqueeze(2).to_broadcast([128, 8, 64]), op=ALU.mult),
                        reads=[(hkey, gg), ("wgt", i2)], writes=[(hkey, gg)])
                    s.op("dve", lambda E, hs=hs, sb_=sb_: E.tensor_tensor(out=hs, in0=hs, in1=pst[sb_][:, 0:512], op=ALU.add),
                         reads=[(hkey, gg), PK(sb_)], writes=[(hkey, gg)])

        st1(0)
        for n in range(len(items)):
            if n + 1 < len(items):
                st1(n + 1)
            st2(n)

    HKEYS = [("HB", gg) for gg in range(8)]
    AR.off = persist_mark
    phaseA_alloc(CTX, True)
    zero_pads(1, CTX)
    s.op("pool", lambda E: E.memset(HB_, 0.0), writes=HKEYS)
    s.op("pool", lambda E: E.memset(g.H2, 0.0), writes=[("H2", gg) for gg in range(8)])
    phaseA_tile(ctx_in, 0, "ctx", 3, 6, 8, 1, CTX)
    dma("sp", Hsave_d, g.H2, reads=[("H2", gg) for gg in range(8)], writes=["hsave"])
    s.barrier()
    stop_if("A0")
    AR.off = persist_mark
    phaseA_alloc(TA, False)
    zero_pads(TA // SEGW, SEGW)
    for t in reversed(range(TOKH // TA)):
        phaseA_tile(x_in, TOKH + t * TA, "other", 0, 0, 2, TA // SEGW, SEGW)
    s.barrier()
    dma("sp", g.GTraw, Hsave_d, reads=["hsave"], writes=["h2tmp"])
    dma("sp", Hsave_d, HB_, reads=HKEYS, writes=["hsave"])
    s.op("dve", lambda E: E.tensor_copy(out=HB_, in_=g.GTraw), reads=["h2tmp"], writes=HKEYS)
    s.barrier()
    stop_if("AO")
    for t in range(TOKH // TA):
        phaseA_tile(x_in, t * TA, "own", 0, 0, 2, TA // SEGW, SEGW, own_tok0=t * TA)
    s.barrier()
    stop_if("AW")
    AR.off = persist_mark
    dma("sp", HB_, Hsave_d, writes=HKEYS)
    SPC = [AR.f32(512) for _ in range(6)]
    DCB = [AR.f32(64) for _ in range(2)]
    HSP = [AR.bf(512) for _ in range(6)]
    pc = 0
    for n_, c in enumerate(reversed(range(NCH))):
        i = n_ % 2
        dma("sp", DCB[i], decb_d[c], reads=["decbd"], writes=[("dcb", i)])
        for gg in range(8):
            j = pc % 6
            pc += 1
            hs = HB_[:, gg * 512:(gg + 1) * 512]
            dma("sp", SPC[j], Sb_d[c, :, gg * 512:(gg + 1) * 512], reads=["Sbd"], writes=[("spc", j)])
            s.op("act", lambda E, j=j, hs=hs: E.activation(out=HSP[j], in_=hs, func=AF.Copy), reads=[("HB", gg)], writes=[("hsp", j)])
            dma("sp", Hent_d[1, c, :, gg * 512:(gg + 1) * 512], HSP[j], reads=[("hsp", j)], writes=["Hentd"])
            eng = "dve" if gg % 2 == 0 else "pool"
            s.op(eng, lambda E, hs=hs, i=i, gg=gg: E.tensor_tensor(out=hs.rearrange("p (h d) -> p h d", h=8), in0=hs.rearrange("p (h d) -> p h d", h=8),
                                                               in1=DCB[i][:, gg * 8:(gg + 1) * 8].unsqueeze(2).to_broadcast([128, 8, 64]), op=ALU.mult),
                 reads=[("dcb", i), ("HB", gg)], writes=[("HB", gg)])
            s.op(eng, lambda E, hs=hs, j=j: E.tensor_tensor(out=hs, in0=hs, in1=SPC[j], op=ALU.add),
                 reads=[("spc", j), ("HB", gg)], writes=[("HB", gg)])
    s.barrier()
    stop_if("SC")
    AR.off = persist_small
    alloc_common(TB, wdn=32 * 128)
    T, ns = g.T, g.ns
    g.PRE = [AR.bf(T + 64) for _ in range(2)]
    g.CA = [AR.f32(T) for _ in range(2)]
    g.prec = 0
    zero_pads(T // SEGW, SEGW)
    DTA = AR.f32(ns * 256).rearrange("p (s c) -> p s c", s=ns)
    EACS = AR.f32(ns * 128).rearrange("p (s c) -> p s c", s=ns)
    MRG = AR.bf(KC * T).rearrange("p (k t) -> p k t", k=KC)
    GBUF = [AR.bf(T) for _ in range(2)]
    WST = AR.bf(8 * 128).rearrange("p (g i) -> p g i", g=8)
    BSB = AR.f32(8 * 128).rearrange("p (g i) -> p g i", g=8)
    BS2 = AR.f32(16 * 128).rearrange("p (f i) -> p f i", f=16)
    ONB = AR.bf(128)
    BNS = AR.f32(4 * 6)
    BNA = AR.f32(2)
    LNS = AR.f32(4)
    TSB = [AR.f32(T) for _ in range(2)]
    regR = AR.off
    XHG = [AR.bf(ns * 512).rearrange("p (s c) -> p s c", s=ns) for _ in range(3)]
    BTG = [AR.bf(T) for _ in range(3)]
    HEG = [[[AR.bf(512) for _ in range(ns)] for _ in range(2)] for _ in range(3)]
    SSN = [AR.f32(512) for _ in range(3)]
    CTG = [AR.bf(T) for _ in range(3)]
    ZS = [AR.bf(ns * 512).rearrange("p (s c) -> p s c", s=ns) for _ in range(3)]
    CBM = [AR.bf(128) for _ in range(2)]
    LH8 = [AR.f32(1024) for _ in range(2)]
    E8 = [AR.bf(1024) for _ in range(2)]
    M8 = [AR.bf(1024) for _ in range(2)]
    XDT = [AR.bf(512) for _ in range(2)]
    TY = [AR.f32(512) for _ in range(3)]
    YT = AR.f32(512)
    YN = [AR.bf(512) for _ in range(3)]
    YNT = AR.bf(32 * T).rearrange("p (k t) -> p k t", k=32)
    endR1 = AR.off
    AR.off = regR
    UT = AR.bf(KC * T).rearrange("p (k t) -> p k t", k=KC)
    VF = AR.f32(ns * D).rearrange("p (s d) -> p s d", s=ns)
    VNB = AR.bf(ns * D).rearrange("p (s d) -> p s d", s=ns)
    AR.off = max(AR.off, endR1)
    dma("pool", WST, wsT_in, writes=["wst"])
    dma("sp", BSB.rearrange("p g i -> p (g i)"), bs_in.partition_broadcast(128), writes=["bsb"])
    s.op("dve", lambda E: E.tensor_copy(out=ONB, in_=ONES), reads=["const"], writes=["onb"])
    LNWC = COLS[:, C_LN:C_LN + 16]
    LNBC = COLS[:, C_LN + 16:C_LN + 32]
    for hh in range(2):
        b = newbank()
        mm(pst[b][:, 0:512], ONB, WST[:, hh * 4:(hh + 1) * 4, :].rearrange("p g i -> p (g i)"), True, True, reads=["onb", "wst"], writes=[PK(b)])
        for f4 in range(8):
            fc = hh * 8 + f4
            gq = f4 // 2
            s.op("dve", lambda E, b=b, fc=fc, gq=gq: E.scalar_tensor_tensor(out=BS2[:, fc, :], in0=pst[b][:, gq * 128:(gq + 1) * 128], scalar=LNBC[:, fc:fc + 1],
                                                                         in1=BSB[:, fc // 2, :], op0=ALU.mult, op1=ALU.add),
                 reads=[PK(b), "cols", "bsb"], writes=["bs2"])
    BGC = COLS[:, C_BG:C_BG + 32]

    for tix in range(TOKH // TB):
        tok0 = tix * TB
        dma("sp", g.hT, hT_d[:, :, tok0:tok0 + T], reads=["hTd"], writes=[("hT", c) for c in range(KC)])
        dma("sp", DTA, dtA_d[tok0:tok0 + T, :].rearrange("(s p) c -> p s c", p=128), reads=["dtAd", "dtAd2"], writes=["dta"])
        for sub in range(ns):
            b = newbank()
            mm(pst[b][:, 0:64], LE, DTA[:, sub, 128:192], True, True, reads=["const", "dta"], writes=[PK(b)])
            mm(pst[b][:, 64:128], GE, DTA[:, sub, 192:256], True, True, reads=["const", "dta"], writes=[PK(b)])
            s.op("act", lambda E, b=b, sub=sub: E.activation(out=EACS[:, sub, :], in_=pst[b][:, 0:128], func=AF.Exp), reads=[PK(b)], writes=[("eacs", sub)])
        def ssd_proj(gg):
            gi2 = gg % 3
            dma("sp", XHG[gi2], xh_d[tok0:tok0 + T, gg * 512:(gg + 1) * 512].rearrange("(s p) c -> p s c", p=128), reads=["xhd"], writes=[("xhg", gi2)])
            dma("sp", BTG[gi2], BT_d[gg, :, tok0:tok0 + T], reads=["BTd"], writes=[("btg", gi2)])
            for dr in range(2):
                for sub in range(ns):
                    dma("sp", HEG[gi2][dr][sub], Hent_d[dr, tok0 // 128 + sub, :, gg * 512:(gg + 1) * 512], reads=["Hentd"], writes=[("heg", gi2, dr, sub)])
            dma("sp", SSN[gi2], ssm_norm[gg * 512:(gg + 1) * 512].partition_broadcast(128), writes=[("ssn", gi2)])
            wblk, wk = load_wblk("win", BLK_C + gg)
            b = newbank()
            for k in range(KC):
                mm(pst[b][:, 0:T], wblk[:, k, :], g.hT[:, k, :], k == 0, k == KC - 1, reads=[wk, ("hT", k)], writes=[PK(b)])
            conv_chunk(b, 40 + gg, T // SEGW, SEGW, T, CTG[gi2], ("ctg", gi2), "pool")
            zb = [4 + sub for sub in range(ns)]
            for j in range(4):
                wblk, wk = load_wblk("win", BLK_Z + gg * 4 + j)
                for sub in range(ns):
                    for k in range(KC):
                        mm(pst[zb[sub]][:, j * 128:(j + 1) * 128], g.hT[:, k, sub * 128:(sub + 1) * 128], wblk[:, k, :], k == 0, k == KC - 1,
                           reads=[wk, ("hT", k)], writes=[PK(zb[sub])])
            for sub in range(ns):
                s.op("act", lambda E, sub=sub, gi2=gi2: E.activation(out=ZS[gi2][:, sub, :], in_=pst[zb[sub]][:, 0:512], func=AF.Silu),
                     reads=[PK(zb[sub])], writes=[("zs", gi2, sub)])

        def stageA(gg, sub, dr):
            gi2 = gg % 3
            if dr == 0:
                b = newbank()
                mm(pst[b][:, 0:128], BTG[gi2][:, sub * 128:(sub + 1) * 128], CTG[gi2][:, sub * 128:(sub + 1) * 128], True, True,
                   reads=[("btg", gi2), ("ctg", gi2)], writes=[PK(b)])
                s.op("dve", lambda E, b=b: E.tensor_tensor(out=CBM[0], in0=pst[b][:, 0:128], in1=LE, op=ALU.mult), reads=[PK(b), "const"], writes=[("cbm", 0)])
                s.op("dve", lambda E, b=b: E.tensor_tensor(out=CBM[1], in0=pst[b][:, 0:128], in1=GE, op=ALU.mult), reads=[PK(b), "const"], writes=[("cbm", 1)])
            UTm, TRI = (GT_, LE) if dr == 0 else (LT_, GE)
            acol = DTA[:, sub, 128 + dr * 64 + gg * 8:128 + dr * 64 + gg * 8 + 8]
            dcol = DTA[:, sub, dr * 64 + gg * 8:dr * 64 + gg * 8 + 8]
            s.op("pool", lambda E: E.tensor_tensor(
                out=LH8[dr].rearrange("p (h j) -> p h j", h=8), in0=acol.unsqueeze(2).to_broadcast([128, 8, 128]),
                in1=UTm.unsqueeze(1).to_broadcast([128, 8, 128]), op=ALU.mult), reads=["dta", "const"], writes=[("lh8", dr)])
            db = [newbank(), newbank()]
            for h in range(8):
                mm(pst[db[h // 4]][:, (h % 4) * 128:(h % 4 + 1) * 128], LH8[dr][:, h * 128:(h + 1) * 128], TRI, True, True,
                   reads=[("lh8", dr), "const"], writes=[PK(db[h // 4])])
            for hh in range(2):
                s.op("act", lambda E, hh=hh: E.activation(out=E8[dr][:, hh * 512:(hh + 1) * 512], in_=pst[db[hh]][:, 0:512], func=AF.Exp),
                     reads=[PK(db[hh])], writes=[("e8", dr, hh)])
            s.op("dve", lambda E: E.tensor_tensor(out=M8[dr].rearrange("p (h i) -> p h i", h=8), in0=E8[dr].rearrange("p (h i) -> p h i", h=8),
                                                  in1=CBM[dr].unsqueeze(1).to_broadcast([128, 8, 128]), op=ALU.mult),
                 reads=[("e8", dr, 0), ("e8", dr, 1), ("cbm", dr)], writes=[("m8", dr)])
            s.op("pool", lambda E: E.tensor_tensor(
                out=XDT[dr].rearrange("p (h d) -> p h d", h=8), in0=XHG[gi2][:, sub, :].rearrange("p (h d) -> p h d", h=8),
                in1=dcol.unsqueeze(2).to_broadcast([128, 8, 64]), op=ALU.mult), reads=[("xhg", gi2), "dta"], writes=[("xdt", dr)])

        def stageB(gg, sub, dr):
            gi2 = gg % 3
            yb = 6 + (sub % 2)
            ecol = EACS[:, sub, dr * 64 + gg * 8:dr * 64 + gg * 8 + 8]
            for h in range(8):
                mm(pst[yb][:, h * 64:(h + 1) * 64], M8[dr][:, h * 128:(h + 1) * 128], XDT[dr][:, h * 64:(h + 1) * 64], dr == 0 and h == 0, dr == 1 and h == 7,
                   reads=[("m8", dr), ("xdt", dr)], writes=[PK(yb)])
            ob = newbank()
            mm(pst[ob][:, 0:512], CTG[gi2][:, sub * 128:(sub + 1) * 128], HEG[gi2][dr][sub], True, True,
               reads=[("ctg", gi2), ("heg", gi2, dr, sub)], writes=[PK(ob)])
            s.op("dve", lambda E: E.tensor_tensor(
                out=TY[dr].rearrange("p (h d) -> p h d", h=8), in0=pst[ob][:, 0:512].rearrange("p (h d) -> p h d", h=8),
                in1=ecol.unsqueeze(2).to_broadcast([128, 8, 64]), op=ALU.mult), reads=[PK(ob), ("eacs", sub)], writes=[("ty", dr)])

        yn_ctr = [0]

        def epilogue(gg, sub):
            gi2 = gg % 3
            yb = 6 + (sub % 2)
            yi = yn_ctr[0] % 3
            yn_ctr[0] += 1
            yn = YN[yi]
            s.op("pool", lambda E: E.tensor_tensor(
                out=TY[2].rearrange("p (h d) -> p h d", h=8), in0=XHG[gi2][:, sub, :].rearrange("p (h d) -> p h d", h=8),
                in1=DSK[:, gg * 8:(gg + 1) * 8].unsqueeze(2).to_broadcast([128, 8, 64]), op=ALU.mult), reads=[("xhg", gi2), "dsk"], writes=[("ty", 2)])
            s.op("dve", lambda E: E.tensor_tensor(out=YT, in0=pst[yb][:, 0:512], in1=TY[0], op=ALU.add), reads=[PK(yb), ("ty", 0)], writes=["yt"])
            s.op("pool", lambda E: E.tensor_tensor(out=YT, in0=YT, in1=TY[1], op=ALU.add), reads=["yt", ("ty", 1)], writes=["yt"])
            s.op("pool", lambda E: E.tensor_tensor(out=YT, in0=YT, in1=TY[2], op=ALU.add), reads=["yt", ("ty", 2)], writes=["yt"])
            s.op("pool", lambda E: E.tensor_tensor(out=YT, in0=YT, in1=ZS[gi2][:, sub, :], op=ALU.mult), reads=["yt", ("zs", gi2, sub)], writes=["yt"])
            rstd_of(YT, 0, 512, "yt")
            s.op("dve", lambda E: E.scalar_tensor_tensor(out=yn, in0=YT, scalar=RSTD[:, 0:1], in1=SSN[gi2], op0=ALU.mult, op1=ALU.mult),
                 reads=["yt", ("rstd", 0), ("ssn", gi2)], writes=[("yn", yi)])

            def tr():
                tb = newbank()
                for q in range(4):
                    s.op("pe", lambda E, q=q: E.transpose(psb[tb][:, q * 128:(q + 1) * 128], yn[:, q * 128:(q + 1) * 128], IDB), reads=[("yn", yi), "idb"], writes=[PK(tb)])
                s.op("act", lambda E: E.activation(out=YNT[:, gg * 4:(gg + 1) * 4, sub * 128:(sub + 1) * 128],
                                                   in_=psb[tb][:, 0:512].rearrange("p (q c) -> p q c", q=4), func=AF.Copy),
                     reads=[PK(tb)], writes=[("ynt", gg)])
            return tr

        dfr = Defer()
        ssd_proj(0)
        prev = None
        for gg in range(8):
            for sub in range(ns):
                for dr in range(2):
                    if sub == 0 and dr == 0 and gg + 1 < 8:
                        ssd_proj(gg + 1)
                    stageA(gg, sub, dr)
                    if prev is not None:
                        stageB(*prev)
                        if prev[2] == 1:
                            dfr.push(epilogue(prev[0], prev[1]), 2)
                    prev = (gg, sub, dr)
                    dfr.tick()
        stageB(*prev)
        dfr.push(epilogue(prev[0], prev[1]), 1)
        dfr.flush()
        for dc in range(16):
            wbb, kb = load_wb32(dc)
            b = newbank()
            for k in range(32):
                mm(pst[b][:, 0:T], wbb[:, k, :], YNT[:, k, :], k == 0, k == 31, reads=[kb, ("ynt", k // 4)], writes=[PK(b)])
            wblk, wk = load_wblk("win", BLK_GB + dc)
            b2 = newbank()
            for k in range(KC):
                mm(pst[b2][:, 0:T], wblk[:, k, :], g.hT[:, k, :], k == 0, k == KC - 1, reads=[wk, ("hT", k)], writes=[PK(b2)])
            gi_ = dc % 2
            s.op("act", lambda E, b2=b2, gi_=gi_, dc=dc: E.activation(out=GBUF[gi_], in_=pst[b2][:, 0:T], func=AF.Sigmoid, bias=BGC[:, 16 + dc:17 + dc]),
                 reads=[PK(b2), "cols"], writes=[("gbuf", gi_)])
            s.op("dve", lambda E, b=b, gi_=gi_, dc=dc: E.tensor_tensor(out=MRG[:, dc, :], in0=pst[b][:, 0:T], in1=GBUF[gi_], op=ALU.mult),
                 reads=[PK(b), ("gbuf", gi_)], writes=[("mrg", dc)])
        s.barrier()
        for fc in range(16):
            wblk, wk = load_wblk("win", BLK_U + fc)
            b = newbank()
            for k in range(KC):
                mm(pst[b][:, 0:T], wblk[:, k, :], g.hT[:, k, :], k == 0, k == KC - 1, reads=[wk, ("hT", k)], writes=[PK(b)])
            s.op("act", lambda E, b=b, fc=fc: E.activation(out=UT[:, fc, :], in_=pst[b][:, 0:T], func=AF.Gelu), reads=[PK(b)], writes=[("ut", fc)])
        for jg in range(4):
            vb = [4 + sub for sub in range(ns)]
            for j in range(4):
                wblk, wk = load_wblk("win", BLK_V + jg * 4 + j)
                for sub in range(ns):
                    for k in range(KC):
                        mm(pst[vb[sub]][:, j * 128:(j + 1) * 128], g.hT[:, k, sub * 128:(sub + 1) * 128], wblk[:, k, :], k == 0, k == KC - 1,
                           reads=[wk, ("hT", k)], writes=[PK(vb[sub])])
            for sub in range(ns):
                s.op("act", lambda E, sub=sub, jg=jg: E.activation(out=VF[:, sub, jg * 512:(jg + 1) * 512], in_=pst[vb[sub]][:, 0:512], func=AF.Gelu),
                     reads=[PK(vb[sub])], writes=[("vf", sub)])
        for sub in range(ns):
            for q in range(4):
                s.op("dve", lambda E, sub=sub, q=q: E.bn_stats(out=BNS[:, q * 6:(q + 1) * 6], in_=VF[:, sub, q * 512:(q + 1) * 512]), reads=[("vf", sub)], writes=["bns"])
            s.op("dve", lambda E: E.bn_aggr(out=BNA, in_=BNS.rearrange("p (q s) -> p q s", q=4)), reads=["bns"], writes=["bna"])
            s.op("act", lambda E: E.activation(out=LNS[:, 0:1], in_=BNA[:, 1:2], func=AF.Sqrt, bias=EPSC), reads=["bna", "eps"], writes=["lns"])
            s.op("dve", lambda E: E.reciprocal(out=LNS[:, 1:2], in_=LNS[:, 0:1]), reads=["lns"], writes=["lns"])
            s.op("dve", lambda E: E.scalar_tensor_tensor(out=LNS[:, 2:3], in0=BNA[:, 0:1], scalar=-1.0, in1=LNS[:, 1:2], op0=ALU.mult, op1=ALU.mult),
                 reads=["lns", "bna"], writes=["lns"])
            s.op("act", lambda E, sub=sub: E.activation(out=VNB[:, sub, :], in_=VF[:, sub, :], func=AF.Identity, scale=LNS[:, 1:2], bias=LNS[:, 2:3]),
                 reads=["lns", ("vf", sub)], writes=[("vnb", sub)])
        for fc in range(16):
            b = newbank()
            for sub in range(ns):
                mm(pst[b][:, sub * 128:(sub + 1) * 128], VNB[:, sub, fc * 128:(fc + 1) * 128], WST[:, fc // 2, :], True, True,
                   reads=[("vnb", sub), "wst"], writes=[PK(b)])
            ti = fc % 2
            s.op("dve", lambda E, b=b, ti=ti, fc=fc: E.scalar_tensor_tensor(out=TSB[ti].rearrange("p (s i) -> p s i", s=ns), in0=pst[b][:, 0:T].rearrange("p (s i) -> p s i", s=ns),
                                                                          scalar=LNWC[:, fc:fc + 1], in1=BS2[:, fc, :].unsqueeze(1).to_broadcast([128, ns, 128]), op0=ALU.mult, op1=ALU.add),
                 reads=[PK(b), "bs2", "cols"], writes=[("tsb", ti)])
            s.op("pool", lambda E, ti=ti, fc=fc: E.tensor_tensor(out=UT[:, fc, :], in0=UT[:, fc, :], in1=TSB[ti], op=ALU.mult),
                 reads=[("tsb", ti), ("ut", fc)], writes=[("ut", fc)])
        for dc in range(16):
            wblk, wk = load_wblk("wa", dc)
            b = newbank()
            for k in range(KC):
                mm(pst[b][:, 0:T], wblk[:, k, :], UT[:, k, :], k == 0, k == KC - 1, reads=[wk, ("ut", k)], writes=[PK(b)])
            wblk2, wk2 = load_wblk("win", BLK_GA + dc)
            b2 = newbank()
            for k in range(KC):
                mm(pst[b2][:, 0:T], wblk2[:, k, :], g.hT[:, k, :], k == 0, k == KC - 1, reads=[wk2, ("hT", k)], writes=[PK(b2)])
            gi_ = dc % 2
            s.op("act", lambda E, b2=b2, gi_=gi_, dc=dc: E.activation(out=GBUF[gi_], in_=pst[b2][:, 0:T], func=AF.Sigmoid, bias=BGC[:, dc:dc + 1]),
                 reads=[PK(b2), "cols"], writes=[("gbuf", gi_)])
            ti = dc % 2
            s.op("dve", lambda E, b=b, gi_=gi_, ti=ti: E.tensor_tensor(out=TSB[ti], in0=pst[b][:, 0:T], in1=GBUF[gi_], op=ALU.mult),
                 reads=[PK(b), ("gbuf", gi_)], writes=[("tsb", ti)])
            s.op("pool", lambda E, ti=ti, dc=dc: E.tensor_tensor(out=MRG[:, dc, :], in0=MRG[:, dc, :], in1=TSB[ti], op=ALU.add),
                 reads=[("tsb", ti), ("mrg", dc)], writes=[("mrg", dc)])
        dma("sp", g.X, x1_d[tok0:tok0 + T, :].rearrange("(s p) d -> p s d", p=128), reads=["x1d"], writes=[("X", sub) for sub in range(ns)])
        for dblk in range(4):
            load_gsl(1, dblk)
            ob = [4 + sub for sub in range(ns)]
            for j in range(4):
                wblk, wk = load_wblk("wo", dblk * 4 + j)
                for sub in range(ns):
                    for k in range(KC):
                        mm(pst[ob[sub]][:, j * 128:(j + 1) * 128], MRG[:, k, sub * 128:(sub + 1) * 128], wblk[:, k, :], k == 0, k == KC - 1,
                           reads=[wk, ("mrg", k)], writes=[PK(ob[sub])])
            for sub in range(ns):
                resid_update(ob[sub], sub, dblk, 1)
        dma("sp", x1_d[tok0:tok0 + T, :].rearrange("(s p) d -> p s d", p=128), g.X, reads=[("X", sub) for sub in range(ns)], writes=["x1d"])
        s.barrier()

    stop_if("B")
    AR.off = persist_small
    alloc_common(TC)
    T, ns = g.T, g.ns
    flatC = AR.bf(max(FFC * T, D))
    g.GTt = flatC[:, 0:FFC * T].rearrange("p (f t) -> p f t", f=FFC)
    g.JUNK = flatC[:, 0:D]
    g.SG = [AR.bf(T) for _ in range(2)]
    NFB = AR.f32(D)
    OUTB = AR.f32(D)
    dma("sp", NFB, norm_final.partition_broadcast(128), writes=["nfb"])
    outs = []
    for tix in range(TOKH // TC):
        tok0 = tix * TC
        dma("sp", g.X, x1_d[tok0:tok0 + T, :].rearrange("(s p) d -> p s d", p=128), reads=["x1d"], writes=[("X", sub) for sub in range(ns)])
        ffn(4, 2, "f2g", "f2u", "f2d")
        for sub in range(ns):
            rstd_of(g.X[:, sub, :], sub, D, ("X", sub))
            s.op("dve", lambda E, sub=sub: E.scalar_tensor_tensor(out=OUTB, in0=g.X[:, sub, :], scalar=RSTD[:, sub:sub + 1], in1=NFB, op0=ALU.mult, op1=ALU.mult),
                 reads=[("X", sub), ("rstd", sub), "nfb"], writes=["outb"])
            outs.append(dma("sp", out_d[tok0 + sub * 128:tok0 + (sub + 1) * 128, :], OUTB, reads=["outb"], writes=["outd"]))
    s.final_wait("sp", outs)
    stats = s.emit(st)
    st.close()
    return nc, stats


FULL_CFG = dict(FF=5632, TOKH=4096, TA=512, TB=256, TC=512)


def make_in_maps(inp, cfg):
    TOKH = cfg["TOKH"]
    x = np.asarray(inp["x"], np.float32)
    ctx = np.asarray(inp["ctx"], np.float32)
    B = x.shape[0]
    f32 = lambda a: np.ascontiguousarray(np.asarray(a, np.float32))
    consts = make_consts()
    shared = {
        "w_mod": f32(inp["w_mod"][0]), "b_mod": f32(inp["b_mod"][0]),
        "norms": f32(np.concatenate([inp["norm_ffn1"][0], inp["norm_mix"][0], inp["norm_ffn2"][0]]).reshape(48, 128)),
        "norm_final": f32(inp["norm_final"]),
        "ffn1_gate": f32(inp["ffn1_gate"][0]), "ffn1_up": f32(inp["ffn1_up"][0]), "ffn1_down": f32(inp["ffn1_down"][0]),
        "ffn2_gate": f32(inp["ffn2_gate"][0]), "ffn2_up": f32(inp["ffn2_up"][0]), "ffn2_down": f32(inp["ffn2_down"][0]),
        "w_in": f32(inp["w_in"][0]), "w_a": f32(inp["w_a"][0]), "w_b": f32(inp["w_b"][0]), "w_out": f32(inp["w_out"][0]),
        "b_gate": f32(inp["b_gate"][0]).reshape(32, 128), "gmlp_ln_wb": f32(np.concatenate([inp["gmlp_ln_w"][0], inp["gmlp_ln_b"][0]]).reshape(32, 128)),
        "conv_b": f32(inp["conv_b"][0]).reshape(48, 128), "d_skip": f32(inp["d_skip"][0]), "ssm_norm": f32(inp["ssm_norm"][0]),
        "consts": consts,
    }
    win = shared["w_in"]
    ws = np.asarray(inp["gmlp_ws"][0], np.float32)
    bs = np.asarray(inp["gmlp_bs"][0], np.float32)
    cw = np.asarray(inp["conv_w"][0], np.float32)
    al = np.asarray(inp["a_log"][0], np.float32)
    db = np.asarray(inp["dt_bias"][0], np.float32)
    wdt = win[:, OFF_DT:OFF_DT + 128]
    per_s = []
    for s_ in range(2):
        if s_ == 0:
            d = {"w_dt": f32(wdt), "gmlp_wsT": f32(ws.transpose(2, 0, 1)), "gmlp_bs": f32(bs.reshape(-1)),
                 "conv_w": f32(cw.reshape(240, 128)), "a_log": f32(al.reshape(-1)), "dt_bias": f32(db.reshape(-1))}
        else:
            wsf = ws[:, ::-1, ::-1]
            d = {"w_dt": f32(np.concatenate([wdt[:, 64:128], wdt[:, 0:64]], axis=1)),
                 "gmlp_wsT": f32(wsf.transpose(2, 0, 1)), "gmlp_bs": f32(bs[:, ::-1].reshape(-1)),
                 "conv_w": f32(cw[::-1].reshape(240, 128)), "a_log": f32(al[::-1].reshape(-1)), "dt_bias": f32(db[::-1].reshape(-1))}
        per_s.append(d)
    in_maps = []
    cc = np.asarray(inp["c_ctx"], np.float32)
    for core in range(2 * B):
        b, s_ = core // 2, core % 2
        xb = x[b] if s_ == 0 else x[b, ::-1]
        cb = ctx[b] if s_ == 0 else ctx[b, ::-1]
        m = dict(shared)
        m.update(per_s[s_])
        m["x"] = f32(xb)
        m["ctx"] = f32(cb)
        m["cvec"] = f32(np.concatenate([np.asarray(inp["c"], np.float32)[b], cc]).reshape(32, 128))
        in_maps.append(m)
    return in_maps


_CACHE = {}


def run(inp, cfg):
    key = tuple(sorted(cfg.items()))
    if key not in _CACHE:
        _CACHE[key] = build(cfg)[0]
    nc = _CACHE[key]
    in_maps = make_in_maps(inp, cfg)
    res = run_bass_kernel_spmd(nc, in_maps, core_ids=list(range(len(in_maps))))
    TOKH = cfg["TOKH"]
    B = len(in_maps) // 2
    out = np.empty((B, 2 * TOKH, D), np.float32)
    for core in range(2 * B):
        b, s_ = core // 2, core % 2
        o = np.asarray(res.results[core]["out"], np.float32)
        if s_ == 0:
            out[b, 0:TOKH] = o
        else:
            out[b, TOKH:] = o[::-1]
    return out


def kernel(**inputs):
    return run(inputs, FULL_CFG)
```
